# Optimizing a Trainium2 kernel written in Bass

```python
import jax, jax.numpy as jnp
from jax import lax
import numpy as np

D_MODEL = 1024
BATCH = 4
SEQ = 8192
DEPTH = 2

CHUNK = 64
N_BRANCH = 3
CONV_WIDTH = 512
CONV_K = 31
ATT_HEADS = 8
ATT_HEAD_DIM = 64
ATT_WIDTH = ATT_HEADS * ATT_HEAD_DIM
Q_BLOCK = 128
POOL_WINDOWS = (2, 4, 8, 16)
POOL_GROUPS = len(POOL_WINDOWS)
POOL_GROUP_DIM = 128
POOL_WIDTH = POOL_GROUPS * POOL_GROUP_DIM
D_FF = 4 * D_MODEL
EPS = 1e-6
IN_SPLITS = (ATT_WIDTH, ATT_WIDTH, ATT_WIDTH, 2 * CONV_WIDTH, POOL_WIDTH, N_BRANCH * D_MODEL)
IN_WIDTH = sum(IN_SPLITS)

kernel_name = "hybrid_conv_stickbreak_pool_encoder"


def rms_norm(x, g):
    xf = x.astype(jnp.float32)
    y = xf * lax.rsqrt(jnp.mean(xf * xf, axis=-1, keepdims=True) + EPS)
    return (y * g.astype(jnp.float32)).astype(x.dtype)


def layer_norm(x, g, b):
    xf = x.astype(jnp.float32)
    mu = jnp.mean(xf, axis=-1, keepdims=True)
    xc = xf - mu
    y = xc * lax.rsqrt(jnp.mean(xc * xc, axis=-1, keepdims=True) + EPS)
    return (y * g.astype(jnp.float32) + b.astype(jnp.float32)).astype(x.dtype)


def conv_module(u, dw, dw_b, ln_g, ln_b, w_out):
    a, gate = jnp.split(u, 2, axis=-1)
    h = a * jax.nn.sigmoid(gate)
    h = lax.conv_general_dilated(
        h, dw[:, None, :].astype(h.dtype), window_strides=(1,), padding=((CONV_K - 1, 0),),
        dimension_numbers=("NWC", "WIO", "NWC"), feature_group_count=CONV_WIDTH) + dw_b
    h = jax.nn.silu(layer_norm(h, ln_g, ln_b))
    return h @ w_out


def stick_breaking_attention(q, k, v):
    S = q.shape[2]
    scale = ATT_HEAD_DIM ** -0.5
    outs = []
    for blk in range(S // Q_BLOCK):
        q0 = blk * Q_BLOCK
        kend = q0 + Q_BLOCK
        qb = q[:, :, q0:kend]
        kb = k[:, :, :kend]
        vb = v[:, :, :kend]
        z = jnp.einsum("bhqd,bhkd->bhqk", qb, kb).astype(jnp.float32) * scale
        qpos = q0 + jnp.arange(Q_BLOCK)[:, None]
        kpos = jnp.arange(kend)[None, :]
        mask = kpos < qpos
        log_beta = jax.nn.log_sigmoid(z)
        log_1m = jnp.where(mask, jax.nn.log_sigmoid(-z), 0.0)
        rev = lax.cumsum(log_1m, axis=3, reverse=True)
        a = jnp.where(mask, jnp.exp(log_beta + rev - log_1m), 0.0)
        outs.append(jnp.einsum("bhqk,bhkd->bhqd", a.astype(vb.dtype), vb))
    return jnp.concatenate(outs, axis=2)


def multiscale_pool(p, pool_w, pool_scale, w_out):
    S = p.shape[1]
    pf = p.astype(jnp.float32)
    groups = []
    for g, w in enumerate(POOL_WINDOWS):
        xg = pf[..., g * POOL_GROUP_DIM:(g + 1) * POOL_GROUP_DIM]
        c = jnp.cumsum(xg, axis=1)
        c_shift = jnp.pad(c, ((0, 0), (w, 0), (0, 0)))[:, :S]
        count = jnp.minimum(jnp.arange(S) + 1, w).astype(jnp.float32)[None, :, None]
        groups.append((c - c_shift) / count - xg)
    y = jnp.stack(groups, axis=2).astype(p.dtype)
    y = jnp.einsum("bsgc,gcd->bsgd", y, pool_w)
    y = y.reshape(y.shape[0], S, POOL_WIDTH) * pool_scale
    return y @ w_out


def setup_inputs(seed: int = 0) -> dict:
    key = jax.random.key(seed)
    ks = jax.random.split(key, 20)
    f32 = jnp.float32

    def nrm(k, shape, fan_in):
        return jax.random.normal(k, shape, f32) * (fan_in ** -0.5)

    def gain(k, shape):
        return 1.0 + 0.05 * jax.random.normal(k, shape, f32)

    return {
        "x": jax.random.normal(ks[0], (BATCH, SEQ, D_MODEL), f32),
        "mix_norm_g": gain(ks[1], (DEPTH, D_MODEL)),
        "w_in": nrm(ks[2], (DEPTH, D_MODEL, IN_WIDTH), D_MODEL),
        "gate_b": 0.02 * jax.random.normal(ks[3], (DEPTH, N_BRANCH, D_MODEL), f32),
        "conv_dw": nrm(ks[4], (DEPTH, CONV_K, CONV_WIDTH), CONV_K),
        "conv_dw_b": 0.02 * jax.random.normal(ks[5], (DEPTH, CONV_WIDTH), f32),
        "conv_ln_g": gain(ks[6], (DEPTH, CONV_WIDTH)),
        "conv_ln_b": 0.02 * jax.random.normal(ks[7], (DEPTH, CONV_WIDTH), f32),
        "w_conv_out": nrm(ks[8], (DEPTH, CONV_WIDTH, D_MODEL), CONV_WIDTH),
        "q_norm_g": gain(ks[9], (DEPTH, ATT_HEAD_DIM)),
        "k_norm_g": gain(ks[10], (DEPTH, ATT_HEAD_DIM)),
        "w_att_out": nrm(ks[11], (DEPTH, ATT_WIDTH, D_MODEL), ATT_WIDTH),
        "pool_w": nrm(ks[12], (DEPTH, POOL_GROUPS, POOL_GROUP_DIM, POOL_GROUP_DIM), POOL_GROUP_DIM),
        "pool_scale": gain(ks[13], (DEPTH, POOL_WIDTH)),
        "w_pool_out": nrm(ks[14], (DEPTH, POOL_WIDTH, D_MODEL), POOL_WIDTH),
        "w_o": nrm(ks[15], (DEPTH, D_MODEL, D_MODEL), D_MODEL),
        "mlp_norm_g": gain(ks[16], (DEPTH, D_MODEL)),
        "w_mlp_in": nrm(ks[17], (DEPTH, D_MODEL, D_FF), D_MODEL),
        "w_mlp_out": nrm(ks[18], (DEPTH, D_FF, D_MODEL), D_FF),
    }


def reference(x, mix_norm_g, w_in, gate_b, conv_dw, conv_dw_b, conv_ln_g, conv_ln_b, w_conv_out,
              q_norm_g, k_norm_g, w_att_out, pool_w, pool_scale, w_pool_out, w_o,
              mlp_norm_g, w_mlp_in, w_mlp_out):
    B, S, D = x.shape
    assert S % CHUNK == 0 and S % Q_BLOCK == 0
    split_pts = list(np.cumsum(IN_SPLITS)[:-1])
    h = x
    for l in range(DEPTH):
        xn = rms_norm(h, mix_norm_g[l])
        proj = xn @ w_in[l]
        q, k, v, u_conv, p_pool, g_pre = jnp.split(proj, split_pts, axis=-1)

        y_conv = conv_module(u_conv, conv_dw[l], conv_dw_b[l], conv_ln_g[l], conv_ln_b[l], w_conv_out[l])

        def heads(t):
            return t.reshape(B, S, ATT_HEADS, ATT_HEAD_DIM).transpose(0, 2, 1, 3)
        qh = rms_norm(heads(q), q_norm_g[l])
        kh = rms_norm(heads(k), k_norm_g[l])
        o = stick_breaking_attention(qh, kh, heads(v))
        y_att = o.transpose(0, 2, 1, 3).reshape(B, S, ATT_WIDTH) @ w_att_out[l]

        y_pool = multiscale_pool(p_pool, pool_w[l], pool_scale[l], w_pool_out[l])

        gates = jax.nn.sigmoid(g_pre.reshape(B, S, N_BRANCH, D).astype(jnp.float32)
                               + gate_b[l].astype(jnp.float32)).astype(h.dtype)
        merged = gates[:, :, 0] * y_conv + gates[:, :, 1] * y_att + gates[:, :, 2] * y_pool
        h = h + merged @ w_o[l]

        hn = rms_norm(h, mlp_norm_g[l])
        ff = jnp.square(jax.nn.relu(hn @ w_mlp_in[l]))
        h = h + ff @ w_mlp_out[l]
    return h
```

```python
import contextlib
import numpy as np
import ml_dtypes
import concourse.bass as bass
import concourse.mybir as mybir
from concourse.bass_utils import run_bass_kernel_spmd

F32 = mybir.dt.float32
BF16 = mybir.dt.bfloat16
AF = mybir.ActivationFunctionType
ALU = mybir.AluOpType
AX = mybir.AxisListType

SAME_ENG_SYNC = True
EPS = 1e-6
NVEC = 24 + 4 + 4 + 4 + 4 + 1 + 1 + 8 + 8 + 124
V_GB, V_DWB, V_LNG, V_LNB, V_PS, V_GQ, V_GK, V_GMIX, V_GMLP, V_DW = 0, 24, 28, 32, 36, 40, 41, 42, 50, 58


class Buf:
    __slots__ = ("w", "r")

    def __init__(self):
        self.w = None
        self.r = {}


class Sched:
    BLK = dict(pe="tensor", act="scalar", dve="vector", pool="gpsimd", sp="sync")

    def __init__(self, nc, es, ndsem=28):
        self.nc = nc
        self.names = ["sp", "pe", "act", "dve", "pool"]
        self.sem = {e: es.enter_context(nc.semaphore("s_" + e)) for e in self.names}
        self.semval = {e: 0 for e in self.names}
        self.dsem = [es.enter_context(nc.semaphore("d%d" % i)) for i in range(ndsem)]
        self.dval = [0] * ndsem
        self.dlast = [None] * ndsem
        self.dnext = 0
        self.q = {e: [] for e in self.names}
        self.seen = {e: {} for e in self.names}
        self.bufs = []

    def buf(self):
        b = Buf()
        self.bufs.append(b)
        return b

    def op(self, eng, fn, reads=(), writes=(), dma=False):
        deps = set()
        for b in reads:
            if b.w is not None:
                deps.add(b.w)
        for b in writes:
            if b.w is not None:
                deps.add(b.w)
            deps.update(b.r.values())
        if dma:
            k = self.dnext
            self.dnext = (self.dnext + 1) % len(self.dsem)
            if self.dlast[k] is not None:
                deps.add(self.dlast[k])
            self.dval[k] += 16
            ev = ("d", k, self.dval[k])
            self.dlast[k] = ev
        else:
            ev = ("c", eng, len(self.q[eng]))
        self.q[eng].append(dict(fn=fn, deps=deps, ev=ev, dma=dma))
        key = (ev[0], ev[1])
        for b in reads:
            b.r[key] = ev
        for b in writes:
            b.w = ev
            b.r = {}
        return ev

    def flush(self):
        nc = self.nc
        obs = set()
        for e in self.names:
            for o in self.q[e]:
                nd = set()
                for d in o["deps"]:
                    if d[0] == "c" and d[1] == e and (e == "pe" or not SAME_ENG_SYNC):
                        continue
                    nd.add(d)
                    if d[0] == "c":
                        obs.add(d)
                o["deps"] = nd
        for e in self.names:
            for o in reversed(self.q[e]):
                if not o["dma"]:
                    obs.add(o["ev"])
                    break
        val = {}
        endval = {}
        for e in self.names:
            c = self.semval[e]
            for o in self.q[e]:
                if o["ev"] in obs:
                    c += 1
                    val[o["ev"]] = c
            endval[e] = c
        with nc.Block() as block:
            for e in self.names:
                def body(engine, e=e):
                    seen = self.seen[e]
                    for o in self.q[e]:
                        for d in sorted(o["deps"]):
                            if d[0] == "c":
                                key, v, sem = ("c", d[1]), val[d], self.sem[d[1]]
                            else:
                                key, v, sem = ("d", d[1]), d[2], self.dsem[d[1]]
                            if seen.get(key, 0) < v:
                                engine.wait_ge(sem, v)
                                seen[key] = v
                        ins = o["fn"](engine)
                        if o["dma"]:
                            ins.then_inc(self.dsem[o["ev"][1]], 16)
                        elif o["ev"] in val:
                            ins.then_inc(self.sem[e], 1)
                    for e2 in self.names:
                        if e2 != e and endval[e2] > seen.get(("c", e2), 0):
                            engine.wait_ge(self.sem[e2], endval[e2])
                            seen[("c", e2)] = endval[e2]
                    for k in range(len(self.dsem)):
                        if self.dval[k] > seen.get(("d", k), 0):
                            engine.wait_ge(self.dsem[k], self.dval[k])
                            seen[("d", k)] = self.dval[k]
                getattr(block, self.BLK[e])(body)
        self.semval = endval
        self.q = {e: [] for e in self.names}
        for b in self.bufs:
            b.w = None
            b.r = {}
        self.bufs = []
        self.dlast = [None] * len(self.dsem)


class Ring:
    def __init__(self, S, aps):
        self.items = [(ap, S.buf()) for ap in aps]
        self.i = 0

    def next(self):
        it = self.items[self.i]
        self.i = (self.i + 1) % len(self.items)
        return it


def build(T, L):
    NG = T // 512
    nc = bass.Bass("TRN2", target_bir_lowering=False)

    def din(name, shape, dt=F32):
        return nc.dram_tensor(name, shape, dt, kind="ExternalInput").ap()

    x_d = din("x", [T, 1024])
    w_in_d = din("w_in", [L * 1024, 6144])
    w_co_d = din("w_conv_out", [L * 512, 1024])
    w_ao_d = din("w_att_out", [L * 512, 1024])
    w_po_d = din("w_pool_out", [L * 512, 1024])
    pw_d = din("pool_w", [L * 512, 128])
    w_o_d = din("w_o", [L * 1024, 1024])
    w_m1_d = din("w_mlp_in", [L * 1024, 4096])
    w_m2_d = din("w_mlp_out", [L * 4096, 1024])
    vecs_d = din("vecs", [L * 128, NVEC])
    cst_d = din("cst", [128, 128 * 5], BF16)
    identf_d = din("identf", [128, 128])
    masks_d = din("masks", [128, 4 * 512], BF16)
    icnt_d = din("icnt", [128, 4 * 16])
    y_d = nc.dram_tensor("y", [T, 1024], F32, kind="ExternalOutput").ap()
    hA_d = nc.dram_tensor("hA", [T, 1024], F32).ap()
    hB_d = nc.dram_tensor("hB", [T, 1024], F32).ap()
    xnT_d = nc.dram_tensor("xnT", [128, 8 * T], BF16).ap().rearrange("p (c t) -> p c t", c=8)
    hsT_d = nc.dram_tensor("hsT", [128, 4 * T], BF16).ap().rearrange("p (c t) -> p c t", c=4)
    zpT_d = nc.dram_tensor("zpT", [128, 4 * T], BF16).ap().rearrange("p (c t) -> p c t", c=4)
    qT_d = nc.dram_tensor("qT", [128, 4 * T], BF16).ap().rearrange("p (c t) -> p c t", c=4)
    oT_d = nc.dram_tensor("oT", [128, 4 * T], BF16).ap().rearrange("p (c t) -> p c t", c=4)

    with contextlib.ExitStack() as es:
        S = Sched(nc, es)

        uid = [0]

        def sb(stack, name, shape, dt):
            uid[0] += 1
            return stack.enter_context(nc.sbuf_tensor("sb%d_%s" % (uid[0], name), shape, dt))

        cst = sb(es, "cst", [128, 5 * 128], BF16)
        identf = sb(es, "identf", [128, 128], F32)
        vecs = sb(es, "vecs", [128, NVEC], F32)
        epsT = sb(es, "epsT", [128, 1], F32)
        oneT = sb(es, "oneT", [128, 1], F32)
        gq8 = sb(es, "gq8", [128, 1], F32)
        psum = [es.enter_context(nc.psum_tensor("ps%d" % i, [128, 1024], F32)) for i in range(4)]
        negtri = cst[:, 128:256]
        negones = cst[:, 256:384]
        ones = cst[:, 384:512]
        blk = cst[:, 512:640]

        cstB = S.buf()
        S.op("sp", lambda e: e.dma_start(out=cst[:, :], in_=cst_d[:, :]), writes=[cstB], dma=True)
        S.op("sp", lambda e: e.dma_start(out=identf[:, :], in_=identf_d[:, :]), writes=[cstB], dma=True)
        S.op("dve", lambda e: e.memset(epsT[:, :], EPS), writes=[cstB])
        S.op("dve", lambda e: e.memset(oneT[:, :], 1.0), writes=[cstB])
        S.flush()

        def bank_ring():
            aps = []
            for p in psum:
                aps.append(p[:, 0:512])
                aps.append(p[:, 512:1024])
            return Ring(S, aps)

        cvt_rr = [0]

        def load_w(stage_ring, dst, dstB, src, rows0, kcs, col0, ncols, scale_col=None):
            for kc in range(kcs):
                for n0 in range(0, ncols, 1024):
                    n1 = min(ncols, n0 + 1024)
                    st, stB = stage_ring.next()
                    sap = src[rows0 + kc * 128: rows0 + (kc + 1) * 128, col0 + n0: col0 + n1]
                    S.op("sp", lambda e, st=st, sap=sap, n=n1 - n0: e.dma_start(out=st[:, 0:n], in_=sap),
                         writes=[stB], dma=True)
                    dap = dst[:, kc, n0:n1]
                    sin = st[:, 0:n1 - n0]
                    eng = ["dve", "act"][cvt_rr[0] % 2]
                    cvt_rr[0] += 1
                    if scale_col is None:
                        if eng == "act":
                            S.op("act", lambda e, dap=dap, sin=sin: e.copy(out=dap, in_=sin), reads=[stB], writes=[dstB])
                        else:
                            S.op(eng, lambda e, dap=dap, sin=sin: e.tensor_copy(out=dap, in_=sin), reads=[stB], writes=[dstB])
                    else:
                        sc = vecs[:, scale_col + kc: scale_col + kc + 1]
                        if eng == "act":
                            S.op("act", lambda e, dap=dap, sin=sin, sc=sc: e.activation(out=dap, in_=sin, func=AF.Copy, scale=sc),
                                 reads=[stB], writes=[dstB])
                        else:
                            S.op(eng, lambda e, dap=dap, sin=sin, sc=sc: e.tensor_scalar(out=dap, in0=sin, scalar1=sc, scalar2=None, op0=ALU.mult),
                                 reads=[stB], writes=[dstB])

        def rms_tile(src_rows, xin, xinB, junk, junkB, st, stB, xn, xnB):
            if src_rows is not None:
                S.op("sp", lambda e: e.dma_start(out=xin, in_=src_rows), writes=[xinB], dma=True)
            S.op("act", lambda e: e.activation(out=junk, in_=xin, func=AF.Square), reads=[xinB], writes=[junkB])
            S.op("dve", lambda e: e.tensor_reduce(out=st[:, 0:1], in_=junk, axis=AX.X, op=ALU.add), reads=[junkB], writes=[stB])
            S.op("act", lambda e: e.activation(out=st[:, 1:2], in_=st[:, 0:1], func=AF.Sqrt, scale=1.0 / 1024, bias=epsT[:, 0:1]),
                 reads=[stB], writes=[stB])
            S.op("dve", lambda e: e.reciprocal(out=st[:, 2:3], in_=st[:, 1:2]), reads=[stB], writes=[stB])
            S.op("dve", lambda e: e.tensor_scalar(out=xn, in0=xin, scalar1=st[:, 2:3], scalar2=None, op0=ALU.mult),
                 reads=[xinB, stB], writes=[xnB])

        def transpose_tile(xn, xnB, pp, ppB, dstT, dstTB, t0, evac):
            for c in range(8):
                S.op("pe", lambda e, c=c: e.transpose(out=pp[:, c * 128:(c + 1) * 128], in_=xn[:, c * 128:(c + 1) * 128], identity=identf[:, :]),
                     reads=[xnB, cstB], writes=[ppB])
            src = pp[:, :].rearrange("p (c t) -> p c t", c=8)
            for hh in range(2):
                d = dstT[:, hh * 4:(hh + 1) * 4, t0:t0 + 128]
                s_ = src[:, hh * 4:(hh + 1) * 4, :]
                if evac[hh] == "act":
                    S.op("act", lambda e, d=d, s_=s_: e.copy(out=d, in_=s_), reads=[ppB], writes=[dstTB])
                else:
                    S.op("dve", lambda e, d=d, s_=s_: e.tensor_copy(out=d, in_=s_), reads=[ppB], writes=[dstTB])

        for l in range(L):
            src_d = x_d if l == 0 else hB_d
            out_d = hB_d if l == L - 1 and False else (y_d if l == L - 1 else hB_d)
            vB = S.buf()
            S.op("sp", lambda e, l=l: e.dma_start(out=vecs[:, :], in_=vecs_d[l * 128:(l + 1) * 128, :]), writes=[vB], dma=True)
            S.op("dve", lambda e: e.tensor_scalar(out=gq8[:, :], in0=vecs[:, V_GQ:V_GQ + 1], scalar1=0.125, scalar2=None, op0=ALU.mult),
                 reads=[vB], writes=[vB])
            S.flush()

            with contextlib.ExitStack() as ph:
                KT = sb(ph, "KT", [128, 4, T], BF16)
                VV = sb(ph, "VV", [128, T // 128, 512], BF16)
                with contextlib.ExitStack() as pa:
                    wqkv = sb(pa, "wqkv", [128, 8, 1536], BF16)
                    wB = S.buf()
                    with contextlib.ExitStack() as st_es:
                        stg = sb(st_es, "stg", [128, 6, 1024], F32)
                        sring = Ring(S, [stg[:, i, :] for i in range(6)])
                        load_w(sring, wqkv, wB, w_in_d, l * 1024, 8, 0, 1536, scale_col=V_GMIX)
                        S.flush()
                    xin_t = sb(pa, "xin", [128, 2, 1024], F32)
                    xn_t = sb(pa, "xn", [128, 2, 1024], F32)
                    stat_t = sb(pa, "stat", [128, 2, 4], F32)
                    xnT_t = sb(pa, "xnT", [128, 2, 8, 512], BF16)
                    QT_t = sb(pa, "QT", [128, 2, 4, 512], BF16)
                    sq_t = sb(pa, "sq", [128, 2, 512], BF16)
                    sd_t = sb(pa, "sd", [128, 2, 512], F32)
                    xin_r = Ring(S, [xin_t[:, i, :] for i in range(2)])
                    xn_r = Ring(S, [xn_t[:, i, :] for i in range(2)])
                    stat_r = Ring(S, [stat_t[:, i, :] for i in range(2)])
                    xnT_r = Ring(S, [xnT_t[:, i, :, :] for i in range(2)])
                    QT_r = Ring(S, [QT_t[:, i, :, :] for i in range(2)])
                    sq_r = Ring(S, [sq_t[:, i, :] for i in range(2)])
                    sd_r = Ring(S, [sd_t[:, i, :] for i in range(2)])
                    a_r = bank_ring()
                    tp_r = Ring(S, [psum[3][:, :], psum[2][:, :]])
                    KTB = S.buf()
                    VB = S.buf()
                    def ldx(ti):
                        xin, xinB = xin_r.next()
                        S.op("sp", lambda e: e.dma_start(out=xin, in_=src_d[ti * 128:(ti + 1) * 128, :]), writes=[xinB], dma=True)
                        return xin, xinB

                    nxt = ldx(0)
                    for G in range(NG):
                        xnT, xnTB = xnT_r.next()
                        for tt in range(4):
                            xin, xinB = nxt
                            if G * 4 + tt + 1 < NG * 4:
                                nxt = ldx(G * 4 + tt + 1)
                            xn, xnB = xn_r.next()
                            stt, sttB = stat_r.next()
                            rms_tile(None, xin, xinB, xn, xnB, stt, sttB, xn, xnB)
                            pidx = 3 if (G * 4 + tt) % 2 == 0 else 2
                            pB0, pB1 = a_r.items[2 * pidx][1], a_r.items[2 * pidx + 1][1]
                            pp = psum[pidx]
                            for c in range(8):
                                S.op("pe", lambda e, c=c, xn=xn, pp=pp: e.transpose(out=pp[:, c * 128:(c + 1) * 128], in_=xn[:, c * 128:(c + 1) * 128], identity=identf[:, :]),
                                     reads=[xnB, cstB], writes=[pB0 if c < 4 else pB1])
                            srcp = pp[:, :].rearrange("p (c t) -> p c t", c=8)
                            S.op("act", lambda e, tt=tt, srcp=srcp, xnT=xnT: e.copy(out=xnT[:, 0:4, tt * 128:(tt + 1) * 128], in_=srcp[:, 0:4, :]),
                                 reads=[pB0], writes=[xnTB])
                            S.op("dve", lambda e, tt=tt, srcp=srcp, xnT=xnT: e.tensor_copy(out=xnT[:, 4:8, tt * 128:(tt + 1) * 128], in_=srcp[:, 4:8, :]),
                                 reads=[pB1], writes=[xnTB])
                        S.op("sp", lambda e, G=G, xnT=xnT: e.dma_start(out=xnT_d[:, :, G * 512:(G + 1) * 512], in_=xnT), reads=[xnTB], dma=True)
                        QT, QTB = QT_r.next()
                        for qk in range(2):
                            for c in range(4):
                                i0 = (qk * 4 + c) % 2
                                ps, psB = a_r.items[i0 * 2]
                                ps2, ps2B = a_r.items[i0 * 2 + 1]
                                col = qk * 512 + c * 128
                                for kc in range(8):
                                    S.op("pe", lambda e, ps=ps, kc=kc, col=col, xnT=xnT: e.matmul(ps, lhsT=wqkv[:, kc, col:col + 128], rhs=xnT[:, kc, :], start=(kc == 0), stop=(kc == 7)),
                                         reads=[wB, xnTB], writes=[psB])
                                sq, sqB = sq_r.next()
                                S.op("act", lambda e, sq=sq, ps=ps: e.activation(out=sq, in_=ps, func=AF.Square), reads=[psB], writes=[sqB])
                                S.op("pe", lambda e, ps2=ps2, sq=sq: e.matmul(ps2, lhsT=blk, rhs=sq, start=True, stop=True), reads=[sqB, cstB], writes=[ps2B])
                                sd, sdB = sd_r.next()
                                S.op("act", lambda e, sd=sd, ps2=ps2: e.activation(out=sd, in_=ps2, func=AF.Ln, scale=1.0 / 64, bias=epsT[:, 0:1]),
                                     reads=[ps2B], writes=[sdB])
                                S.op("act", lambda e, sd=sd: e.activation(out=sd, in_=sd, func=AF.Exp, scale=-0.5), reads=[sdB], writes=[sdB])
                                if qk == 0:
                                    S.op("dve", lambda e, c=c, ps=ps, sd=sd, QT=QT: e.scalar_tensor_tensor(out=QT[:, c, :], in0=ps, scalar=gq8[:, 0:1], in1=sd, op0=ALU.mult, op1=ALU.mult),
                                         reads=[psB, sdB, vB], writes=[QTB])
                                else:
                                    S.op("dve", lambda e, c=c, ps=ps, sd=sd, G=G: e.scalar_tensor_tensor(out=KT[:, c, G * 512:(G + 1) * 512], in0=ps, scalar=vecs[:, V_GK:V_GK + 1], in1=sd, op0=ALU.mult, op1=ALU.mult),
                                         reads=[psB, sdB, vB], writes=[KTB])
                        S.op("sp", lambda e, G=G, QT=QT: e.dma_start(out=qT_d[:, :, G * 512:(G + 1) * 512], in_=QT), reads=[QTB], dma=True)
                        for tt in range(4):
                            ps, psB = a_r.items[tt % 4]
                            for kc in range(8):
                                S.op("pe", lambda e, ps=ps, kc=kc, tt=tt, xnT=xnT: e.matmul(ps, lhsT=xnT[:, kc, tt * 128:(tt + 1) * 128], rhs=wqkv[:, kc, 1024:1536], start=(kc == 0), stop=(kc == 7)),
                                     reads=[wB, xnTB], writes=[psB])
                            if tt % 2 == 0:
                                S.op("act", lambda e, ps=ps, tt=tt, G=G: e.copy(out=VV[:, G * 4 + tt, :], in_=ps), reads=[psB], writes=[VB])
                            else:
                                S.op("dve", lambda e, ps=ps, tt=tt, G=G: e.tensor_copy(out=VV[:, G * 4 + tt, :], in_=ps), reads=[psB], writes=[VB])
                    S.flush()

                with contextlib.ExitStack() as pb_:
                    masks = sb(pb_, "masks", [128, 4, 512], BF16)
                    QT_t = sb(pb_, "QTb", [128, 2, 4, 512], BF16)
                    e_t = sb(pb_, "ee", [128, 3, 2, 512], BF16)
                    lp_t = sb(pb_, "lp", [128, 3, 2, 512], BF16)
                    sr_t = sb(pb_, "sr", [128, 4, 2, 512], BF16)
                    p_t = sb(pb_, "pp", [128, 2, 2, 512], BF16)
                    at_t = sb(pb_, "at", [128, 3, 2, 512], BF16)
                    ot_t = sb(pb_, "ot", [128, 4, 512], BF16)
                    mB = S.buf()
                    S.op("sp", lambda e: e.dma_start(out=masks[:, :, :], in_=masks_d.rearrange("p (j q) -> p j q", j=4)), writes=[mB], dma=True)
                    QT_r = Ring(S, [QT_t[:, i, :, :] for i in range(2)])
                    e_r = Ring(S, [e_t[:, i, :, :] for i in range(3)])
                    lp_r = Ring(S, [lp_t[:, i, :, :] for i in range(3)])
                    sr_r = Ring(S, [sr_t[:, i, :, :] for i in range(4)])
                    p_r = Ring(S, [p_t[:, i, :, :] for i in range(2)])
                    at_r = Ring(S, [at_t[:, i, :, :] for i in range(3)])
                    ot_r = Ring(S, [ot_t[0:64, i, :] for i in range(4)])
                    zB = [[S.buf(), S.buf()], [S.buf(), S.buf()]]
                    zP = [psum[0], psum[3]]
                    wBk = [S.buf(), S.buf()]
                    wP = psum[1]
                    oBk = [S.buf(), S.buf()]
                    oP = [psum[2][0:64, 0:512], psum[2][0:64, 512:1024]]
                    zi = [0]

                    def loadq(G):
                        QT, QTB = QT_r.next()
                        S.op("sp", lambda e: e.dma_start(out=QT, in_=qT_d[:, :, G * 512:(G + 1) * 512]), writes=[QTB], dma=True)
                        return QT, QTB

                    qnext = loadq(0)
                    for G in range(NG):
                        QT, QTB = qnext
                        if G + 1 < NG:
                            qnext = loadq(G + 1)
                        jmax = 4 * G + 3
                        units = [(hp, j) for hp in range(4) for j in range(jmax, -1, -1)]
                        NU = len(units)
                        st_ = [dict() for _ in range(NU)]
                        cur_sr = {}

                        def stage1a(u, G=G, QT=QT, QTB=QTB, units=units, st_=st_):
                            hp, j = units[u]
                            d = st_[u]
                            k = zi[0] % 2
                            zi[0] += 1
                            z = zP[k]
                            for hh in range(2):
                                kt = KT[hh * 64:(hh + 1) * 64, hp, j * 128:(j + 1) * 128]
                                qs = QT[hh * 64:(hh + 1) * 64, hp, :]
                                zz = z[:, hh * 512:(hh + 1) * 512]
                                S.op("pe", lambda e, zz=zz, kt=kt, qs=qs: e.matmul(zz, lhsT=kt, rhs=qs, start=True, stop=True), reads=[QTB], writes=[zB[k][hh]])
                            ee, eeB = e_r.next()
                            z3 = z[:, :].rearrange("p (a b) -> p a b", a=2)
                            S.op("act", lambda e: e.activation(out=ee, in_=z3, func=AF.Exp), reads=[zB[k][0], zB[k][1]], writes=[eeB])
                            if j >= 4 * G:
                                mk = masks[:, j - 4 * G, :]
                                for hh in range(2):
                                    S.op("dve", lambda e, hh=hh: e.tensor_tensor(out=ee[:, hh, :], in0=ee[:, hh, :], in1=mk, op=ALU.mult), reads=[mB], writes=[eeB])
                            d["ee"], d["eeB"] = ee, eeB

                        def stage1b(u, units=units, st_=st_, cur_sr=cur_sr, jmax=jmax):
                            hp, j = units[u]
                            d = st_[u]
                            ee, eeB = d["ee"], d["eeB"]
                            lp, lpB = lp_r.next()
                            S.op("act", lambda e: e.activation(out=lp, in_=ee, func=AF.Ln, bias=oneT[:, 0:1]), reads=[eeB], writes=[lpB])
                            d["lp"], d["lpB"] = lp, lpB
                            d["sr_in"] = cur_sr.get(hp)
                            if j > 0:
                                sr, srB = sr_r.next()
                                if j == jmax:
                                    S.op("dve", lambda e: e.tensor_copy(out=sr, in_=lp), reads=[lpB], writes=[srB])
                                else:
                                    psr, psrB = cur_sr[hp]
                                    S.op("dve", lambda e: e.tensor_tensor(out=sr, in0=psr, in1=lp, op=ALU.add), reads=[lpB, psrB], writes=[srB])
                                cur_sr[hp] = (sr, srB)

                        def stage2(u, units=units, st_=st_, jmax=jmax):
                            hp, j = units[u]
                            d = st_[u]
                            lp, lpB = d["lp"], d["lpB"]
                            ee, eeB = d["ee"], d["eeB"]
                            last = (j == jmax)
                            for hh in range(2):
                                ww = wP[:, hh * 512:(hh + 1) * 512]
                                S.op("pe", lambda e, ww=ww, hh=hh: e.matmul(ww, lhsT=negtri, rhs=lp[:, hh, :], start=True, stop=last), reads=[lpB, cstB], writes=[wBk[hh]])
                                if not last:
                                    psr, psrB = d["sr_in"]
                                    S.op("pe", lambda e, ww=ww, hh=hh, psr=psr: e.matmul(ww, lhsT=negones, rhs=psr[:, hh, :], start=False, stop=True), reads=[psrB, cstB], writes=[wBk[hh]])
                            pp_, ppB_ = p_r.next()
                            w3 = wP[:, :].rearrange("p (a b) -> p a b", a=2)
                            S.op("act", lambda e: e.activation(out=pp_, in_=w3, func=AF.Exp), reads=[wBk[0], wBk[1]], writes=[ppB_])
                            at, atB = at_r.next()
                            S.op("dve", lambda e: e.tensor_tensor(out=at, in0=pp_, in1=ee, op=ALU.mult), reads=[ppB_, eeB], writes=[atB])
                            d["at"], d["atB"] = at, atB

                        def stage3(u, G=G, units=units, st_=st_, jmax=jmax):
                            hp, j = units[u]
                            d = st_[u]
                            at, atB = d["at"], d["atB"]
                            first = (j == jmax)
                            for hh in range(2):
                                h = 2 * hp + hh
                                o = oP[hh]
                                S.op("pe", lambda e, o=o, h=h, hh=hh: e.matmul(o, lhsT=VV[:, j, h * 64:(h + 1) * 64], rhs=at[:, hh, :], start=first, stop=(j == 0)),
                                     reads=[atB], writes=[oBk[hh]])
                            if j == 0:
                                for hh in range(2):
                                    ot, otB = ot_r.next()
                                    o = oP[hh]
                                    dst = oT_d[hh * 64:(hh + 1) * 64, hp, G * 512:(G + 1) * 512]
                                    if hh == 0:
                                        S.op("act", lambda e, ot=ot, o=o: e.copy(out=ot, in_=o), reads=[oBk[hh]], writes=[otB])
                                    else:
                                        S.op("dve", lambda e, ot=ot, o=o: e.tensor_copy(out=ot, in_=o), reads=[oBk[hh]], writes=[otB])
                                    S.op("sp", lambda e, dst=dst, ot=ot: e.dma_start(out=dst, in_=ot), reads=[otB], dma=True)
                            st_[u] = None

                        SK2, SK3 = 2, 4
                        for i in range(NU + SK3):
                            if i < NU:
                                stage1a(i)
                            if 0 <= i - SK2 < NU:
                                stage2(i - SK2)
                            if i < NU:
                                stage1b(i)
                            if 0 <= i - SK3 < NU:
                                stage3(i - SK3)
                    S.flush()

            with contextlib.ExitStack() as ph:
                w_u = sb(ph, "w_u", [128, 8, 1024], BF16)
                w_p = sb(ph, "w_p", [128, 8, 512], BF16)
                pw = sb(ph, "pw", [128, 4, 128], BF16)
                diag_t = sb(ph, "diag", [128, 4, 31, 128], BF16)
                wB = S.buf()
                with contextlib.ExitStack() as st_es:
                    stg = sb(st_es, "stg", [128, 6, 1024], F32)
                    sring = Ring(S, [stg[:, i, :] for i in range(6)])
                    load_w(sring, w_u, wB, w_in_d, l * 1024, 8, 1536, 1024, scale_col=V_GMIX)
                    load_w(sring, w_p, wB, w_in_d, l * 1024, 8, 2560, 512, scale_col=V_GMIX)
                    load_w(sring, pw, wB, pw_d, l * 512, 4, 0, 128)
                    for c in range(4):
                        for k in range(31):
                            eng = "dve" if (k % 2 == 0) else "pool"
                            S.op(eng, lambda e, c=c, k=k: e.tensor_scalar(out=diag_t[:, c, k, :], in0=cst[:, 0:128], scalar1=vecs[:, V_DW + c * 31 + k: V_DW + c * 31 + k + 1], scalar2=None, op0=ALU.mult),
                                 reads=[vB, cstB], writes=[wB])
                    S.flush()
                xnT_t = sb(ph, "xnT", [128, 2, 8, 512], BF16)
                cb_t = sb(ph, "cb", [128, 2, 4, 542], BF16)
                sig_t = sb(ph, "sig", [128, 2, 512], F32)
                xc_t = sb(ph, "xc", [128, 4, 512], F32)
                xbf_t = sb(ph, "xbf", [128, 4, 512], BF16)
                xsq_t = sb(ph, "xsq", [128, 4, 512], BF16)
                ln_t = sb(ph, "ln", [128, 4, 512], F32)
                tt_t = sb(ph, "tt", [128, 2, 512], F32)
                hs_t = sb(ph, "hs", [128, 2, 4, 512], BF16)
                pb_t = sb(ph, "pb", [128, 2, 4, 527], F32)
                pa_t = sb(ph, "pa", [128, 2, 527], F32)
                yp_t = sb(ph, "yp", [128, 4, 512], BF16)
                zp_t = sb(ph, "zp", [128, 2, 4, 512], BF16)
                icnt = sb(ph, "icnt", [128, 4, 16], F32)
                t16 = sb(ph, "t16", [128, 16], F32)
                iB = S.buf()
                S.op("sp", lambda e: e.dma_start(out=icnt[:, :, :], in_=icnt_d.rearrange("p (g t) -> p g t", g=4)), writes=[iB], dma=True)
                br = bank_ring()
                xnT_r = Ring(S, [xnT_t[:, i, :, :] for i in range(2)])
                cb_r = Ring(S, [cb_t[:, i, :, :] for i in range(2)])
                sig_r = Ring(S, [sig_t[:, i, :] for i in range(2)])
                tt_r = Ring(S, [tt_t[:, i, :] for i in range(2)])
                hs_r = Ring(S, [hs_t[:, i, :, :] for i in range(2)])
                pb_r = Ring(S, [pb_t[:, i, :, :] for i in range(2)])
                pa_r = Ring(S, [pa_t[:, i, :] for i in range(2)])
                zp_r = Ring(S, [zp_t[:, i, :, :] for i in range(2)])
                xcB, xbfB, xsqB, lnB, ypB, t16B = S.buf(), S.buf(), S.buf(), S.buf(), S.buf(), S.buf()
                cb_prev = None
                pb_prev = None

                def ldxT(G):
                    xT, xTB = xnT_r.next()
                    S.op("sp", lambda e: e.dma_start(out=xT, in_=xnT_d[:, :, G * 512:(G + 1) * 512]), writes=[xTB], dma=True)
                    return xT, xTB

                nxt = ldxT(0)
                for G in range(NG):
                    xT, xTB = nxt
                    if G + 1 < NG:
                        nxt = ldxT(G + 1)
                    cb, cbB = cb_r.next()
                    for c in range(4):
                        if G == 0:
                            S.op("pool", lambda e, cb=cb, c=c: e.memset(cb[:, c, 0:30], 0.0), writes=[cbB])
                        else:
                            pcb, pcbB = cb_prev
                            S.op("pool", lambda e, cb=cb, c=c, pcb=pcb: e.tensor_copy(out=cb[:, c, 0:30], in_=pcb[:, c, 512:542]), reads=[pcbB], writes=[cbB])
                        psa, psaB = br.next()
                        psg, psgB = br.next()
                        for kc in range(8):
                            S.op("pe", lambda e, psa=psa, kc=kc, c=c, xT=xT: e.matmul(psa, lhsT=w_u[:, kc, c * 128:(c + 1) * 128], rhs=xT[:, kc, :], start=(kc == 0), stop=(kc == 7)),
                                 reads=[wB, xTB], writes=[psaB])
                        for kc in range(8):
                            S.op("pe", lambda e, psg=psg, kc=kc, c=c, xT=xT: e.matmul(psg, lhsT=w_u[:, kc, 512 + c * 128:512 + (c + 1) * 128], rhs=xT[:, kc, :], start=(kc == 0), stop=(kc == 7)),
                                 reads=[wB, xTB], writes=[psgB])
                        sg, sgB = sig_r.next()
                        S.op("act", lambda e, sg=sg, psg=psg: e.activation(out=sg, in_=psg, func=AF.Sigmoid), reads=[psgB], writes=[sgB])
                        S.op("dve", lambda e, cb=cb, c=c, psa=psa, sg=sg: e.tensor_tensor(out=cb[:, c, 30:542], in0=psa, in1=sg, op=ALU.mult), reads=[psaB, sgB], writes=[cbB])
                        psc, pscB = br.next()
                        for k in range(31):
                            S.op("pe", lambda e, psc=psc, k=k, c=c, cb=cb: e.matmul(psc, lhsT=diag_t[:, c, k, :], rhs=cb[:, c, k:k + 512], start=(k == 0), stop=(k == 30)),
                                 reads=[wB, cbB], writes=[pscB])
                        bcol = vecs[:, V_DWB + c:V_DWB + c + 1]
                        S.op("act", lambda e, psc=psc, c=c, bcol=bcol: e.activation(out=xc_t[:, c, :], in_=psc, func=AF.Identity, bias=bcol), reads=[pscB, vB], writes=[xcB])
                        S.op("act", lambda e, psc=psc, c=c, bcol=bcol: e.activation(out=xsq_t[:, c, :], in_=psc, func=AF.Square, bias=bcol), reads=[pscB, vB], writes=[xsqB])
                        S.op("dve", lambda e, c=c: e.tensor_copy(out=xbf_t[:, c, :], in_=xc_t[:, c, :]), reads=[xcB], writes=[xbfB])
                    cb_prev = (cb, cbB)
                    s1, s1B = br.next()
                    s2, s2B = br.next()
                    for c in range(4):
                        S.op("pe", lambda e, s1=s1, c=c: e.matmul(s1, lhsT=ones, rhs=xbf_t[:, c, :], start=(c == 0), stop=(c == 3)), reads=[xbfB, cstB], writes=[s1B])
                    for c in range(4):
                        S.op("pe", lambda e, s2=s2, c=c: e.matmul(s2, lhsT=ones, rhs=xsq_t[:, c, :], start=(c == 0), stop=(c == 3)), reads=[xsqB, cstB], writes=[s2B])
                    mean, msq, var = ln_t[:, 0, :], ln_t[:, 1, :], ln_t[:, 2, :]
                    S.op("act", lambda e, s1=s1: e.activation(out=mean, in_=s1, func=AF.Copy, scale=1.0 / 512), reads=[s1B], writes=[lnB])
                    S.op("dve", lambda e: e.tensor_tensor(out=msq, in0=mean, in1=mean, op=ALU.mult), reads=[lnB], writes=[lnB])
                    S.op("dve", lambda e, s2=s2: e.scalar_tensor_tensor(out=var, in0=s2, scalar=1.0 / 512, in1=msq, op0=ALU.mult, op1=ALU.subtract), reads=[s2B, lnB], writes=[lnB])
                    S.op("act", lambda e: e.activation(out=var, in_=var, func=AF.Ln, bias=epsT[:, 0:1]), reads=[lnB], writes=[lnB])
                    S.op("act", lambda e: e.activation(out=var, in_=var, func=AF.Exp, scale=-0.5), reads=[lnB], writes=[lnB])
                    hs, hsB = hs_r.next()
                    for c in range(4):
                        t1, t1B = tt_r.next()
                        S.op("dve", lambda e, t1=t1, c=c: e.tensor_tensor(out=t1, in0=xc_t[:, c, :], in1=mean, op=ALU.subtract), reads=[xcB, lnB], writes=[t1B])
                        S.op("pool", lambda e, t1=t1: e.tensor_tensor(out=t1, in0=t1, in1=var, op=ALU.mult), reads=[lnB, t1B], writes=[t1B])
                        S.op("act", lambda e, t1=t1, c=c, hs=hs: e.activation(out=hs[:, c, :], in_=t1, func=AF.Silu, scale=vecs[:, V_LNG + c:V_LNG + c + 1], bias=vecs[:, V_LNB + c:V_LNB + c + 1]),
                             reads=[t1B, vB], writes=[hsB])
                    S.op("sp", lambda e, hs=hs, G=G: e.dma_start(out=hsT_d[:, :, G * 512:(G + 1) * 512], in_=hs), reads=[hsB], dma=True)
                    pb, pbB = pb_r.next()
                    zp, zpB = zp_r.next()
                    for g in range(4):
                        w = 2 << g
                        if G == 0:
                            S.op("pool", lambda e, pb=pb, g=g: e.memset(pb[:, g, 0:15], 0.0), writes=[pbB])
                        else:
                            ppb, ppbB = pb_prev
                            S.op("pool", lambda e, pb=pb, g=g, ppb=ppb: e.tensor_copy(out=pb[:, g, 0:15], in_=ppb[:, g, 512:527]), reads=[ppbB], writes=[pbB])
                        psp, pspB = br.next()
                        for kc in range(8):
                            S.op("pe", lambda e, psp=psp, kc=kc, g=g, xT=xT: e.matmul(psp, lhsT=w_p[:, kc, g * 128:(g + 1) * 128], rhs=xT[:, kc, :], start=(kc == 0), stop=(kc == 7)),
                                 reads=[wB, xTB], writes=[pspB])
                        S.op("act", lambda e, pb=pb, g=g, psp=psp: e.copy(out=pb[:, g, 15:527], in_=psp), reads=[pspB], writes=[pbB])
                        cur = pb[:, g, :]
                        curB = pbB
                        m = 1
                        while m < w:
                            nx, nxB = pa_r.next()
                            S.op("dve", lambda e, nx=nx, cur=cur, m=m: e.tensor_tensor(out=nx[:, m:527], in0=cur[:, m:527], in1=cur[:, 0:527 - m], op=ALU.add),
                                 reads=[curB], writes=[nxB])
                            cur, curB = nx, nxB
                            m *= 2
                        S.op("dve", lambda e, cur=cur, g=g, pb=pb, w=w: e.scalar_tensor_tensor(out=yp_t[:, g, :], in0=cur[:, 15:527], scalar=1.0 / w, in1=pb[:, g, 15:527], op0=ALU.mult, op1=ALU.subtract),
                             reads=[curB, pbB], writes=[ypB])
                        if G == 0:
                            S.op("dve", lambda e, cur=cur, g=g: e.tensor_tensor(out=t16[:, :], in0=cur[:, 15:31], in1=icnt[:, g, :], op=ALU.mult), reads=[curB, iB], writes=[t16B])
                            S.op("dve", lambda e, g=g, pb=pb: e.tensor_tensor(out=yp_t[:, g, 0:16], in0=t16[:, :], in1=pb[:, g, 15:31], op=ALU.subtract), reads=[t16B, pbB], writes=[ypB])
                        psq, psqB = br.next()
                        S.op("pe", lambda e, psq=psq, g=g: e.matmul(psq, lhsT=pw[:, g, :], rhs=yp_t[:, g, :], start=True, stop=True), reads=[wB, ypB], writes=[psqB])
                        S.op("act", lambda e, psq=psq, g=g, zp=zp: e.activation(out=zp[:, g, :], in_=psq, func=AF.Copy, scale=vecs[:, V_PS + g:V_PS + g + 1]), reads=[psqB, vB], writes=[zpB])
                    pb_prev = (pb, pbB)
                    S.op("sp", lambda e, zp=zp, G=G: e.dma_start(out=zpT_d[:, :, G * 512:(G + 1) * 512], in_=zp), reads=[zpB], dma=True)
                S.flush()

            with contextlib.ExitStack() as ph:
                w_g = sb(ph, "w_g", [128, 8, 3072], BF16)
                w_br = sb(ph, "w_br", [128, 3, 4, 1024], BF16)
                w_o = sb(ph, "w_o", [128, 8, 1024], BF16)
                wB = S.buf()
                with contextlib.ExitStack() as st_es:
                    stg = sb(st_es, "stg", [128, 6, 1024], F32)
                    sring = Ring(S, [stg[:, i, :] for i in range(6)])
                    load_w(sring, w_g, wB, w_in_d, l * 1024, 8, 3072, 3072, scale_col=V_GMIX)
                    load_w(sring, w_br[:, 0, :, :], wB, w_co_d, l * 512, 4, 0, 1024)
                    load_w(sring, w_br[:, 1, :, :], wB, w_ao_d, l * 512, 4, 0, 1024)
                    load_w(sring, w_br[:, 2, :, :], wB, w_po_d, l * 512, 4, 0, 1024)
                    load_w(sring, w_o, wB, w_o_d, l * 1024, 8, 0, 1024)
                    S.flush()
                xnT_t = sb(ph, "xnT", [128, 2, 8, 512], BF16)
                src_t = sb(ph, "srcs", [128, 2, 3, 4, 512], BF16)
                hin_t = sb(ph, "hin", [128, 2, 4, 1024], F32)
                mg_t = sb(ph, "mg", [128, 8, 512], F32)
                mgb_t = sb(ph, "mgb", [128, 8, 512], BF16)
                gate_t = sb(ph, "gate", [128, 2, 512], F32)
                tm_t = sb(ph, "tm", [128, 2, 512], F32)
                ho_t = sb(ph, "ho", [128, 2, 1024], F32)
                br = bank_ring()
                xnT_r = Ring(S, [xnT_t[:, i, :, :] for i in range(2)])
                src_r = Ring(S, [src_t[:, i, :, :, :] for i in range(2)])
                gate_r = Ring(S, [gate_t[:, i, :] for i in range(2)])
                tm_r = Ring(S, [tm_t[:, i, :] for i in range(2)])
                ho_r = Ring(S, [ho_t[:, i, :] for i in range(2)])
                hin_r = Ring(S, [hin_t[:, i, :, :] for i in range(2)])
                mgB = [S.buf() for _ in range(8)]
                mgbB = S.buf()
                srcs_d = [hsT_d, oT_d, zpT_d]
                def ldall(G):
                    xT, xTB = xnT_r.next()
                    S.op("sp", lambda e: e.dma_start(out=xT, in_=xnT_d[:, :, G * 512:(G + 1) * 512]), writes=[xTB], dma=True)
                    sr, srB = src_r.next()
                    for b3 in range(3):
                        S.op("sp", lambda e, b3=b3: e.dma_start(out=sr[:, b3, :, :], in_=srcs_d[b3][:, :, G * 512:(G + 1) * 512]), writes=[srB], dma=True)
                    hin, hinB = hin_r.next()
                    for tt in range(4):
                        r0 = G * 512 + tt * 128
                        S.op("sp", lambda e, tt=tt, r0=r0: e.dma_start(out=hin[:, tt, :], in_=src_d[r0:r0 + 128, :]), writes=[hinB], dma=True)
                    return xT, xTB, sr, srB, hin, hinB

                nxt = ldall(0)
                for G in range(NG):
                    xT, xTB, sr, srB, hin, hinB = nxt
                    if G + 1 < NG:
                        nxt = ldall(G + 1)
                    for dc in range(8):
                        for b3 in range(3):
                            psg, psgB = br.next()
                            col = b3 * 1024 + dc * 128
                            for kc in range(8):
                                S.op("pe", lambda e, psg=psg, kc=kc, col=col, xT=xT: e.matmul(psg, lhsT=w_g[:, kc, col:col + 128], rhs=xT[:, kc, :], start=(kc == 0), stop=(kc == 7)),
                                     reads=[wB, xTB], writes=[psgB])
                            gt, gtB = gate_r.next()
                            gcol = vecs[:, V_GB + b3 * 8 + dc: V_GB + b3 * 8 + dc + 1]
                            S.op("act", lambda e, gt=gt, psg=psg, gcol=gcol: e.activation(out=gt, in_=psg, func=AF.Sigmoid, bias=gcol), reads=[psgB, vB], writes=[gtB])
                            psy, psyB = br.next()
                            for kc in range(4):
                                S.op("pe", lambda e, psy=psy, kc=kc, b3=b3, dc=dc, sr=sr: e.matmul(psy, lhsT=w_br[:, b3, kc, dc * 128:(dc + 1) * 128], rhs=sr[:, b3, kc, :], start=(kc == 0), stop=(kc == 3)),
                                     reads=[wB, srB], writes=[psyB])
                            if b3 == 0:
                                S.op("dve", lambda e, dc=dc, psy=psy, gt=gt: e.tensor_tensor(out=mg_t[:, dc, :], in0=psy, in1=gt, op=ALU.mult), reads=[psyB, gtB], writes=[mgB[dc]])
                            else:
                                tm, tmB = tm_r.next()
                                S.op("dve", lambda e, tm=tm, psy=psy, gt=gt: e.tensor_tensor(out=tm, in0=psy, in1=gt, op=ALU.mult), reads=[psyB, gtB], writes=[tmB])
                                if b3 == 1:
                                    S.op("pool", lambda e, dc=dc, tm=tm: e.tensor_tensor(out=mg_t[:, dc, :], in0=mg_t[:, dc, :], in1=tm, op=ALU.add), reads=[tmB], writes=[mgB[dc]])
                                else:
                                    S.op("pool", lambda e, dc=dc, tm=tm: e.tensor_tensor(out=mgb_t[:, dc, :], in0=mg_t[:, dc, :], in1=tm, op=ALU.add), reads=[tmB, mgB[dc]], writes=[mgbB])
                    for tt in range(4):
                        ho, hoB = ho_r.next()
                        for dh in range(2):
                            pso, psoB = br.next()
                            for fc in range(8):
                                S.op("pe", lambda e, pso=pso, fc=fc, tt=tt, dh=dh: e.matmul(pso, lhsT=mgb_t[:, fc, tt * 128:(tt + 1) * 128], rhs=w_o[:, fc, dh * 512:(dh + 1) * 512], start=(fc == 0), stop=(fc == 7)),
                                     reads=[wB, mgbB], writes=[psoB])
                            S.op("dve", lambda e, ho=ho, pso=pso, tt=tt, dh=dh, hin=hin: e.tensor_tensor(out=ho[:, dh * 512:(dh + 1) * 512], in0=pso, in1=hin[:, tt, dh * 512:(dh + 1) * 512], op=ALU.add),
                                 reads=[psoB, hinB], writes=[hoB])
                        r0 = G * 512 + tt * 128
                        S.op("sp", lambda e, ho=ho, r0=r0: e.dma_start(out=hA_d[r0:r0 + 128, :], in_=ho), reads=[hoB], dma=True)
                S.flush()

            with contextlib.ExitStack() as ph:
                w1 = sb(ph, "w1", [128, 8, 4096], BF16)
                w2 = sb(ph, "w2", [128, 32, 1024], BF16)
                wB = S.buf()
                with contextlib.ExitStack() as st_es:
                    stg = sb(st_es, "stg", [128, 6, 1024], F32)
                    sring = Ring(S, [stg[:, i, :] for i in range(6)])
                    load_w(sring, w1, wB, w_m1_d, l * 1024, 8, 0, 4096, scale_col=V_GMLP)
                    load_w(sring, w2, wB, w_m2_d, l * 4096, 32, 0, 1024)
                    S.flush()
                HT = 256
                xin_t = sb(ph, "xin", [128, 4, 1024], F32)
                junk_t = sb(ph, "junk", [128, 1024], BF16)
                xn_t = sb(ph, "xn", [128, 2, 1024], F32)
                stat_t = sb(ph, "stat", [128, 2, 4], F32)
                xnT_t = sb(ph, "xnT", [128, 2, 8, HT], BF16)
                ff_t = sb(ph, "ff", [128, 32, HT], BF16)
                rl_t = sb(ph, "rl", [128, 2, HT], F32)
                ho_t = sb(ph, "ho", [128, 2, 1024], F32)
                xin_r = Ring(S, [xin_t[:, i, :] for i in range(4)])
                xn_r = Ring(S, [xn_t[:, i, :] for i in range(2)])
                stat_r = Ring(S, [stat_t[:, i, :] for i in range(2)])
                xnT_r = Ring(S, [xnT_t[:, i, :, :] for i in range(2)])
                rl_r = Ring(S, [rl_t[:, i, :] for i in range(2)])
                ho_r = Ring(S, [ho_t[:, i, :] for i in range(2)])
                junkB = S.buf()
                ffB = [S.buf() for _ in range(32)]
                aps = []
                for p in psum[0:3]:
                    aps.append(p[:, 0:512])
                    aps.append(p[:, 512:1024])
                br = Ring(S, aps)
                ppB = S.buf()
                def ldh(H):
                    res = []
                    for tt in range(HT // 128):
                        r0 = H * HT + tt * 128
                        xin, xinB = xin_r.next()
                        S.op("sp", lambda e, xin=xin, r0=r0: e.dma_start(out=xin, in_=hA_d[r0:r0 + 128, :]), writes=[xinB], dma=True)
                        res.append((xin, xinB))
                    return res

                nxt = ldh(0)
                for H in range(T // HT):
                    xT, xTB = xnT_r.next()
                    xins = []
                    curx = nxt
                    if H + 1 < T // HT:
                        nxt = ldh(H + 1)
                    for tt in range(HT // 128):
                        r0 = H * HT + tt * 128
                        xin, xinB = curx[tt]
                        xn, xnB = xn_r.next()
                        stt, sttB = stat_r.next()
                        rms_tile(None, xin, xinB, junk_t[:, :], junkB, stt, sttB, xn, xnB)
                        transpose_tile(xn, xnB, psum[3], ppB, xT, xTB, tt * 128, ("act", "dve"))
                        xins.append((xin, xinB))
                    for fc in range(32):
                        ps, psB = br.next()
                        for kc in range(8):
                            S.op("pe", lambda e, ps=ps, kc=kc, fc=fc, xT=xT: e.matmul(ps[:, 0:HT], lhsT=w1[:, kc, fc * 128:(fc + 1) * 128], rhs=xT[:, kc, :], start=(kc == 0), stop=(kc == 7)),
                                 reads=[wB, xTB], writes=[psB])
                        rl, rlB = rl_r.next()
                        S.op("act", lambda e, rl=rl, ps=ps: e.activation(out=rl, in_=ps[:, 0:HT], func=AF.Relu), reads=[psB], writes=[rlB])
                        eng = "pool" if fc % 2 == 0 else "dve"
                        S.op(eng, lambda e, rl=rl, fc=fc: e.tensor_tensor(out=ff_t[:, fc, :], in0=rl, in1=rl, op=ALU.mult), reads=[rlB], writes=[ffB[fc]])
                    for tt in range(HT // 128):
                        ho, hoB = ho_r.next()
                        xin, xinB = xins[tt]
                        for dh in range(2):
                            pso, psoB = br.next()
                            for fc in range(32):
                                S.op("pe", lambda e, pso=pso, fc=fc, tt=tt, dh=dh: e.matmul(pso, lhsT=ff_t[:, fc, tt * 128:(tt + 1) * 128], rhs=w2[:, fc, dh * 512:(dh + 1) * 512], start=(fc == 0), stop=(fc == 31)),
                                     reads=[wB, ffB[fc]], writes=[psoB])
                            S.op("dve", lambda e, ho=ho, pso=pso, xin=xin, dh=dh: e.tensor_tensor(out=ho[:, dh * 512:(dh + 1) * 512], in0=pso, in1=xin[:, dh * 512:(dh + 1) * 512], op=ALU.add),
                                 reads=[psoB, xinB], writes=[hoB])
                        r0 = H * HT + tt * 128
                        S.op("sp", lambda e, ho=ho, r0=r0: e.dma_start(out=out_d[r0:r0 + 128, :], in_=ho), reads=[hoB], dma=True)
                S.flush()
    return nc


def host_prep(inp, L):
    f = lambda a: np.ascontiguousarray(np.asarray(a, dtype=np.float32))
    vecs = np.zeros((L, 128, NVEC), np.float32)
    for l in range(L):
        vecs[l, :, V_GB:V_GB + 24] = f(inp["gate_b"])[l].reshape(24, 128).T
        vecs[l, :, V_DWB:V_DWB + 4] = f(inp["conv_dw_b"])[l].reshape(4, 128).T
        vecs[l, :, V_LNG:V_LNG + 4] = f(inp["conv_ln_g"])[l].reshape(4, 128).T
        vecs[l, :, V_LNB:V_LNB + 4] = f(inp["conv_ln_b"])[l].reshape(4, 128).T
        vecs[l, :, V_PS:V_PS + 4] = f(inp["pool_scale"])[l].reshape(4, 128).T
        vecs[l, :, V_GQ] = np.tile(f(inp["q_norm_g"])[l], 2)
        vecs[l, :, V_GK] = np.tile(f(inp["k_norm_g"])[l], 2)
        vecs[l, :, V_GMIX:V_GMIX + 8] = f(inp["mix_norm_g"])[l].reshape(8, 128).T
        vecs[l, :, V_GMLP:V_GMLP + 8] = f(inp["mlp_norm_g"])[l].reshape(8, 128).T
        dw = f(inp["conv_dw"])[l]
        vecs[l, :, V_DW:V_DW + 124] = dw.reshape(31, 4, 128).transpose(2, 1, 0).reshape(128, 124)
    ident = np.eye(128, dtype=np.float32)
    jj, kk = np.meshgrid(np.arange(128), np.arange(128), indexing="ij")
    negtri = -(jj >= kk).astype(np.float32)
    negones = -np.ones((128, 128), np.float32)
    ones = np.ones((128, 128), np.float32)
    blk = (jj // 64 == kk // 64).astype(np.float32)
    cst = np.concatenate([ident, negtri, negones, ones, blk], axis=1).astype(ml_dtypes.bfloat16)
    p = np.arange(128)[:, None]
    q = np.arange(512)[None, :]
    masks = np.concatenate([(q > p + j * 128).astype(np.float32) for j in range(4)], axis=1).astype(ml_dtypes.bfloat16)
    icnt = np.zeros((128, 4, 16), np.float32)
    for g in range(4):
        w = 2 << g
        icnt[:, g, :] = 1.0 / np.minimum(np.arange(16) + 1, w).astype(np.float32)
    com = {
        "w_in": f(inp["w_in"]).reshape(L * 1024, 6144),
        "w_conv_out": f(inp["w_conv_out"]).reshape(L * 512, 1024),
        "w_att_out": f(inp["w_att_out"]).reshape(L * 512, 1024),
        "w_pool_out": f(inp["w_pool_out"]).reshape(L * 512, 1024),
        "pool_w": f(inp["pool_w"]).reshape(L * 512, 128),
        "w_o": f(inp["w_o"]).reshape(L * 1024, 1024),
        "w_mlp_in": f(inp["w_mlp_in"]).reshape(L * 1024, 4096),
        "w_mlp_out": f(inp["w_mlp_out"]).reshape(L * 4096, 1024),
        "vecs": vecs.reshape(L * 128, NVEC),
        "cst": cst,
        "identf": ident,
        "masks": masks,
        "icnt": icnt.reshape(128, 64),
    }
    return com


_NC_CACHE = {}


def run(inp, seqs, T, L):
    key = (T, L)
    if key not in _NC_CACHE:
        _NC_CACHE[key] = build(T, L)
    nc = _NC_CACHE[key]
    com = host_prep(inp, L)
    in_maps = []
    for c in range(8):
        m = dict(com)
        m["x"] = np.ascontiguousarray(seqs[c], dtype=np.float32)
        in_maps.append(m)
    res = run_bass_kernel_spmd(nc, in_maps, core_ids=list(range(8)))
    return [res.results[c]["y"] for c in range(8)]


def kernel(**inputs):
    x = np.asarray(inputs["x"], dtype=np.float32)
    B, T, D = x.shape
    L = np.asarray(inputs["w_in"]).shape[0]
    seqs = [x[c // 2] for c in range(8)]
    outs = run(inputs, seqs, T, L)
    return np.stack([outs[2 * b] for b in range(B)], axis=0).astype(np.float32)
```

```python
import contextlib
import numpy as np
import ml_dtypes
import concourse.bass as bass
import concourse.mybir as mybir
from concourse.bass_utils import run_bass_kernel_spmd

F32 = mybir.dt.float32
BF16 = mybir.dt.bfloat16
AF = mybir.ActivationFunctionType
ALU = mybir.AluOpType
AX = mybir.AxisListType

SAME_ENG_SYNC = True
EPS = 1e-6
NVEC = 24 + 4 + 4 + 4 + 4 + 1 + 1 + 8 + 8 + 124
V_GB, V_DWB, V_LNG, V_LNB, V_PS, V_GQ, V_GK, V_GMIX, V_GMLP, V_DW = 0, 24, 28, 32, 36, 40, 41, 42, 50, 58


class Buf:
    __slots__ = ("w", "r")

    def __init__(self):
        self.w = None
        self.r = {}


class Sched:
    BLK = dict(pe="tensor", act="scalar", dve="vector", pool="gpsimd", sp="sync")

    def __init__(self, nc, es, ndsem=28):
        self.nc = nc
        self.names = ["sp", "pe", "act", "dve", "pool"]
        self.sem = {e: es.enter_context(nc.semaphore("s_" + e)) for e in self.names}
        self.semval = {e: 0 for e in self.names}
        self.dsem = [es.enter_context(nc.semaphore("d%d" % i)) for i in range(ndsem)]
        self.dval = [0] * ndsem
        self.dlast = [None] * ndsem
        self.dnext = 0
        self.q = {e: [] for e in self.names}
        self.seen = {e: {} for e in self.names}
        self.bufs = []

    def buf(self):
        b = Buf()
        self.bufs.append(b)
        return b

    def op(self, eng, fn, reads=(), writes=(), dma=False):
        deps = set()
        for b in reads:
            if b.w is not None:
                deps.add(b.w)
        for b in writes:
            if b.w is not None:
                deps.add(b.w)
            deps.update(b.r.values())
        if dma:
            k = self.dnext
            self.dnext = (self.dnext + 1) % len(self.dsem)
            if self.dlast[k] is not None:
                deps.add(self.dlast[k])
            self.dval[k] += 16
            ev = ("d", k, self.dval[k])
            self.dlast[k] = ev
        else:
            ev = ("c", eng, len(self.q[eng]))
        self.q[eng].append(dict(fn=fn, deps=deps, ev=ev, dma=dma))
        key = (ev[0], ev[1])
        for b in reads:
            b.r[key] = ev
        for b in writes:
            b.w = ev
            b.r = {}
        return ev

    def flush(self):
        nc = self.nc
        obs = set()
        for e in self.names:
            for o in self.q[e]:
                nd = set()
                for d in o["deps"]:
                    if d[0] == "c" and d[1] == e and (e == "pe" or not SAME_ENG_SYNC):
                        continue
                    nd.add(d)
                    if d[0] == "c":
                        obs.add(d)
                o["deps"] = nd
        for e in self.names:
            for o in reversed(self.q[e]):
                if not o["dma"]:
                    obs.add(o["ev"])
                    break
        val = {}
        endval = {}
        for e in self.names:
            c = self.semval[e]
            for o in self.q[e]:
                if o["ev"] in obs:
                    c += 1
                    val[o["ev"]] = c
            endval[e] = c
        with nc.Block() as block:
            for e in self.names:
                def body(engine, e=e):
                    seen = self.seen[e]
                    for o in self.q[e]:
                        for d in sorted(o["deps"]):
                            if d[0] == "c":
                                key, v, sem = ("c", d[1]), val[d], self.sem[d[1]]
                            else:
                                key, v, sem = ("d", d[1]), d[2], self.dsem[d[1]]
                            if seen.get(key, 0) < v:
                                engine.wait_ge(sem, v)
                                seen[key] = v
                        ins = o["fn"](engine)
                        if o["dma"]:
                            ins.then_inc(self.dsem[o["ev"][1]], 16)
                        elif o["ev"] in val:
                            ins.then_inc(self.sem[e], 1)
                    for e2 in self.names:
                        if e2 != e and endval[e2] > seen.get(("c", e2), 0):
                            engine.wait_ge(self.sem[e2], endval[e2])
                            seen[("c", e2)] = endval[e2]
                    for k in range(len(self.dsem)):
                        if self.dval[k] > seen.get(("d", k), 0):
                            engine.wait_ge(self.dsem[k], self.dval[k])
                            seen[("d", k)] = self.dval[k]
                getattr(block, self.BLK[e])(body)
        self.semval = endval
        self.q = {e: [] for e in self.names}
        for b in self.bufs:
            b.w = None
            b.r = {}
        self.bufs = []
        self.dlast = [None] * len(self.dsem)


class Ring:
    def __init__(self, S, aps):
        self.items = [(ap, S.buf()) for ap in aps]
        self.i = 0

    def next(self):
        it = self.items[self.i]
        self.i = (self.i + 1) % len(self.items)
        return it


def build(T, L):
    NG = T // 512
    nc = bass.Bass("TRN2", target_bir_lowering=False)

    def din(name, shape, dt=F32):
        return nc.dram_tensor(name, shape, dt, kind="ExternalInput").ap()

    x_d = din("x", [T, 1024])
    w_in_d = din("w_in", [L * 1024, 6144])
    w_co_d = din("w_conv_out", [L * 512, 1024])
    w_ao_d = din("w_att_out", [L * 512, 1024])
    w_po_d = din("w_pool_out", [L * 512, 1024])
    pw_d = din("pool_w", [L * 512, 128])
    w_o_d = din("w_o", [L * 1024, 1024])
    w_m1_d = din("w_mlp_in", [L * 1024, 4096])
    w_m2_d = din("w_mlp_out", [L * 4096, 1024])
    vecs_d = din("vecs", [L * 128, NVEC])
    cst_d = din("cst", [128, 128 * 5], BF16)
    identf_d = din("identf", [128, 128])
    masks_d = din("masks", [128, 4 * 512], BF16)
    icnt_d = din("icnt", [128, 4 * 16])
    y_d = nc.dram_tensor("y", [T, 1024], F32, kind="ExternalOutput").ap()
    hA_d = nc.dram_tensor("hA", [T, 1024], F32).ap()
    hB_d = nc.dram_tensor("hB", [T, 1024], F32).ap()
    xnT_d = nc.dram_tensor("xnT", [128, 8 * T], BF16).ap().rearrange("p (c t) -> p c t", c=8)
    hsT_d = nc.dram_tensor("hsT", [128, 4 * T], BF16).ap().rearrange("p (c t) -> p c t", c=4)
    zpT_d = nc.dram_tensor("zpT", [128, 4 * T], BF16).ap().rearrange("p (c t) -> p c t", c=4)
    qT_d = nc.dram_tensor("qT", [128, 4 * T], BF16).ap().rearrange("p (c t) -> p c t", c=4)
    oT_d = nc.dram_tensor("oT", [128, 4 * T], BF16).ap().rearrange("p (c t) -> p c t", c=4)

    with contextlib.ExitStack() as es:
        S = Sched(nc, es)

        uid = [0]

        def sb(stack, name, shape, dt):
            uid[0] += 1
            return stack.enter_context(nc.sbuf_tensor("sb%d_%s" % (uid[0], name), shape, dt))

        cst = sb(es, "cst", [128, 5 * 128], BF16)
        identf = sb(es, "identf", [128, 128], F32)
        vecs = sb(es, "vecs", [128, NVEC], F32)
        epsT = sb(es, "epsT", [128, 1], F32)
        oneT = sb(es, "oneT", [128, 1], F32)
        gq8 = sb(es, "gq8", [128, 1], F32)
        psum = [es.enter_context(nc.psum_tensor("ps%d" % i, [128, 1024], F32)) for i in range(4)]
        negtri = cst[:, 128:256]
        negones = cst[:, 256:384]
        ones = cst[:, 384:512]
        blk = cst[:, 512:640]

        cstB = S.buf()
        S.op("sp", lambda e: e.dma_start(out=cst[:, :], in_=cst_d[:, :]), writes=[cstB], dma=True)
        S.op("sp", lambda e: e.dma_start(out=identf[:, :], in_=identf_d[:, :]), writes=[cstB], dma=True)
        S.op("dve", lambda e: e.memset(epsT[:, :], EPS), writes=[cstB])
        S.op("dve", lambda e: e.memset(oneT[:, :], 1.0), writes=[cstB])
        S.flush()

        def bank_ring():
            aps = []
            for p in psum:
                aps.append(p[:, 0:512])
                aps.append(p[:, 512:1024])
            return Ring(S, aps)

        cvt_rr = [0]

        def load_w(stage_ring, dst, dstB, src, rows0, kcs, col0, ncols, scale_col=None):
            for kc in range(kcs):
                for n0 in range(0, ncols, 1024):
                    n1 = min(ncols, n0 + 1024)
                    st, stB = stage_ring.next()
                    sap = src[rows0 + kc * 128: rows0 + (kc + 1) * 128, col0 + n0: col0 + n1]
                    S.op("sp", lambda e, st=st, sap=sap, n=n1 - n0: e.dma_start(out=st[:, 0:n], in_=sap),
                         writes=[stB], dma=True)
                    dap = dst[:, kc, n0:n1]
                    sin = st[:, 0:n1 - n0]
                    eng = ["dve", "act"][cvt_rr[0] % 2]
                    cvt_rr[0] += 1
                    if scale_col is None:
                        if eng == "act":
                            S.op("act", lambda e, dap=dap, sin=sin: e.copy(out=dap, in_=sin), reads=[stB], writes=[dstB])
                        else:
                            S.op(eng, lambda e, dap=dap, sin=sin: e.tensor_copy(out=dap, in_=sin), reads=[stB], writes=[dstB])
                    else:
                        sc = vecs[:, scale_col + kc: scale_col + kc + 1]
                        if eng == "act":
                            S.op("act", lambda e, dap=dap, sin=sin, sc=sc: e.activation(out=dap, in_=sin, func=AF.Copy, scale=sc),
                                 reads=[stB], writes=[dstB])
                        else:
                            S.op(eng, lambda e, dap=dap, sin=sin, sc=sc: e.tensor_scalar(out=dap, in0=sin, scalar1=sc, scalar2=None, op0=ALU.mult),
                                 reads=[stB], writes=[dstB])

        def rms_tile(src_rows, xin, xinB, junk, junkB, st, stB, xn, xnB):
            if src_rows is not None:
                S.op("sp", lambda e: e.dma_start(out=xin, in_=src_rows), writes=[xinB], dma=True)
            S.op("act", lambda e: e.activation(out=junk, in_=xin, func=AF.Square), reads=[xinB], writes=[junkB])
            S.op("dve", lambda e: e.tensor_reduce(out=st[:, 0:1], in_=junk, axis=AX.X, op=ALU.add), reads=[junkB], writes=[stB])
            S.op("act", lambda e: e.activation(out=st[:, 1:2], in_=st[:, 0:1], func=AF.Sqrt, scale=1.0 / 1024, bias=epsT[:, 0:1]),
                 reads=[stB], writes=[stB])
            S.op("dve", lambda e: e.reciprocal(out=st[:, 2:3], in_=st[:, 1:2]), reads=[stB], writes=[stB])
            S.op("dve", lambda e: e.tensor_scalar(out=xn, in0=xin, scalar1=st[:, 2:3], scalar2=None, op0=ALU.mult),
                 reads=[xinB, stB], writes=[xnB])

        def transpose_tile(xn, xnB, pp, ppB, dstT, dstTB, t0, evac):
            for c in range(8):
                S.op("pe", lambda e, c=c: e.transpose(out=pp[:, c * 128:(c + 1) * 128], in_=xn[:, c * 128:(c + 1) * 128], identity=identf[:, :]),
                     reads=[xnB, cstB], writes=[ppB])
            src = pp[:, :].rearrange("p (c t) -> p c t", c=8)
            for hh in range(2):
                d = dstT[:, hh * 4:(hh + 1) * 4, t0:t0 + 128]
                s_ = src[:, hh * 4:(hh + 1) * 4, :]
                if evac[hh] == "act":
                    S.op("act", lambda e, d=d, s_=s_: e.copy(out=d, in_=s_), reads=[ppB], writes=[dstTB])
                else:
                    S.op("dve", lambda e, d=d, s_=s_: e.tensor_copy(out=d, in_=s_), reads=[ppB], writes=[dstTB])

        for l in range(L):
            src_d = x_d if l == 0 else hB_d
            out_d = hB_d if l == L - 1 and False else (y_d if l == L - 1 else hB_d)
            vB = S.buf()
            S.op("sp", lambda e, l=l: e.dma_start(out=vecs[:, :], in_=vecs_d[l * 128:(l + 1) * 128, :]), writes=[vB], dma=True)
            S.op("dve", lambda e: e.tensor_scalar(out=gq8[:, :], in0=vecs[:, V_GQ:V_GQ + 1], scalar1=0.125, scalar2=None, op0=ALU.mult),
                 reads=[vB], writes=[vB])
            S.flush()

            with contextlib.ExitStack() as ph:
                KT = sb(ph, "KT", [128, 4, T], BF16)
                VV = sb(ph, "VV", [128, T // 128, 512], BF16)
                with contextlib.ExitStack() as pa:
                    wqkv = sb(pa, "wqkv", [128, 8, 1536], BF16)
                    wB = S.buf()
                    with contextlib.ExitStack() as st_es:
                        stg = sb(st_es, "stg", [128, 6, 1024], F32)
                        sring = Ring(S, [stg[:, i, :] for i in range(6)])
                        load_w(sring, wqkv, wB, w_in_d, l * 1024, 8, 0, 1536, scale_col=V_GMIX)
                        S.flush()
                    xin_t = sb(pa, "xin", [128, 2, 1024], F32)
                    xn_t = sb(pa, "xn", [128, 2, 1024], F32)
                    stat_t = sb(pa, "stat", [128, 2, 4], F32)
                    xnT_t = sb(pa, "xnT", [128, 2, 8, 512], BF16)
                    QT_t = sb(pa, "QT", [128, 2, 4, 512], BF16)
                    sq_t = sb(pa, "sq", [128, 2, 512], BF16)
                    sd_t = sb(pa, "sd", [128, 2, 512], F32)
                    xin_r = Ring(S, [xin_t[:, i, :] for i in range(2)])
                    xn_r = Ring(S, [xn_t[:, i, :] for i in range(2)])
                    stat_r = Ring(S, [stat_t[:, i, :] for i in range(2)])
                    xnT_r = Ring(S, [xnT_t[:, i, :, :] for i in range(2)])
                    QT_r = Ring(S, [QT_t[:, i, :, :] for i in range(2)])
                    sq_r = Ring(S, [sq_t[:, i, :] for i in range(2)])
                    sd_r = Ring(S, [sd_t[:, i, :] for i in range(2)])
                    a_r = bank_ring()
                    tp_r = Ring(S, [psum[3][:, :], psum[2][:, :]])
                    KTB = S.buf()
                    VB = S.buf()
                    def ldx(ti):
                        xin, xinB = xin_r.next()
                        S.op("sp", lambda e: e.dma_start(out=xin, in_=src_d[ti * 128:(ti + 1) * 128, :]), writes=[xinB], dma=True)
                        return xin, xinB

                    nxt = ldx(0)
                    for G in range(NG):
                        xnT, xnTB = xnT_r.next()
                        for tt in range(4):
                            xin, xinB = nxt
                            if G * 4 + tt + 1 < NG * 4:
                                nxt = ldx(G * 4 + tt + 1)
                            xn, xnB = xn_r.next()
                            stt, sttB = stat_r.next()
                            rms_tile(None, xin, xinB, xn, xnB, stt, sttB, xn, xnB)
                            pidx = 3 if (G * 4 + tt) % 2 == 0 else 2
                            pB0, pB1 = a_r.items[2 * pidx][1], a_r.items[2 * pidx + 1][1]
                            pp = psum[pidx]
                            for c in range(8):
                                S.op("pe", lambda e, c=c, xn=xn, pp=pp: e.transpose(out=pp[:, c * 128:(c + 1) * 128], in_=xn[:, c * 128:(c + 1) * 128], identity=identf[:, :]),
                                     reads=[xnB, cstB], writes=[pB0 if c < 4 else pB1])
                            srcp = pp[:, :].rearrange("p (c t) -> p c t", c=8)
                            S.op("act", lambda e, tt=tt, srcp=srcp, xnT=xnT: e.copy(out=xnT[:, 0:4, tt * 128:(tt + 1) * 128], in_=srcp[:, 0:4, :]),
                                 reads=[pB0], writes=[xnTB])
                            S.op("dve", lambda e, tt=tt, srcp=srcp, xnT=xnT: e.tensor_copy(out=xnT[:, 4:8, tt * 128:(tt + 1) * 128], in_=srcp[:, 4:8, :]),
                                 reads=[pB1], writes=[xnTB])
                        S.op("sp", lambda e, G=G, xnT=xnT: e.dma_start(out=xnT_d[:, :, G * 512:(G + 1) * 512], in_=xnT), reads=[xnTB], dma=True)
                        QT, QTB = QT_r.next()
                        for qk in range(2):
                            for c in range(4):
                                i0 = (qk * 4 + c) % 2
                                ps, psB = a_r.items[i0 * 2]
                                ps2, ps2B = a_r.items[i0 * 2 + 1]
                                col = qk * 512 + c * 128
                                for kc in range(8):
                                    S.op("pe", lambda e, ps=ps, kc=kc, col=col, xnT=xnT: e.matmul(ps, lhsT=wqkv[:, kc, col:col + 128], rhs=xnT[:, kc, :], start=(kc == 0), stop=(kc == 7)),
                                         reads=[wB, xnTB], writes=[psB])
                                sq, sqB = sq_r.next()
                                S.op("act", lambda e, sq=sq, ps=ps: e.activation(out=sq, in_=ps, func=AF.Square), reads=[psB], writes=[sqB])
                                S.op("pe", lambda e, ps2=ps2, sq=sq: e.matmul(ps2, lhsT=blk, rhs=sq, start=True, stop=True), reads=[sqB, cstB], writes=[ps2B])
                                sd, sdB = sd_r.next()
                                S.op("act", lambda e, sd=sd, ps2=ps2: e.activation(out=sd, in_=ps2, func=AF.Ln, scale=1.0 / 64, bias=epsT[:, 0:1]),
                                     reads=[ps2B], writes=[sdB])
                                S.op("act", lambda e, sd=sd: e.activation(out=sd, in_=sd, func=AF.Exp, scale=-0.5), reads=[sdB], writes=[sdB])
                                if qk == 0:
                                    S.op("dve", lambda e, c=c, ps=ps, sd=sd, QT=QT: e.scalar_tensor_tensor(out=QT[:, c, :], in0=ps, scalar=gq8[:, 0:1], in1=sd, op0=ALU.mult, op1=ALU.mult),
                                         reads=[psB, sdB, vB], writes=[QTB])
                                else:
                                    S.op("dve", lambda e, c=c, ps=ps, sd=sd, G=G: e.scalar_tensor_tensor(out=KT[:, c, G * 512:(G + 1) * 512], in0=ps, scalar=vecs[:, V_GK:V_GK + 1], in1=sd, op0=ALU.mult, op1=ALU.mult),
                                         reads=[psB, sdB, vB], writes=[KTB])
                        S.op("sp", lambda e, G=G, QT=QT: e.dma_start(out=qT_d[:, :, G * 512:(G + 1) * 512], in_=QT), reads=[QTB], dma=True)
                        for tt in range(4):
                            ps, psB = a_r.items[tt % 4]
                            for kc in range(8):
                                S.op("pe", lambda e, ps=ps, kc=kc, tt=tt, xnT=xnT: e.matmul(ps, lhsT=xnT[:, kc, tt * 128:(tt + 1) * 128], rhs=wqkv[:, kc, 1024:1536], start=(kc == 0), stop=(kc == 7)),
                                     reads=[wB, xnTB], writes=[psB])
                            if tt % 2 == 0:
                                S.op("act", lambda e, ps=ps, tt=tt, G=G: e.copy(out=VV[:, G * 4 + tt, :], in_=ps), reads=[psB], writes=[VB])
                            else:
                                S.op("dve", lambda e, ps=ps, tt=tt, G=G: e.tensor_copy(out=VV[:, G * 4 + tt, :], in_=ps), reads=[psB], writes=[VB])
                    S.flush()

                with contextlib.ExitStack() as pb_:
                    masks = sb(pb_, "masks", [128, 4, 512], BF16)
                    QT_t = sb(pb_, "QTb", [128, 2, 4, 512], BF16)
                    e_t = sb(pb_, "ee", [128, 3, 2, 512], BF16)
                    lp_t = sb(pb_, "lp", [128, 3, 2, 512], BF16)
                    sr_t = sb(pb_, "sr", [128, 4, 2, 512], BF16)
                    p_t = sb(pb_, "pp", [128, 2, 2, 512], BF16)
                    at_t = sb(pb_, "at", [128, 3, 2, 512], BF16)
                    ot_t = sb(pb_, "ot", [128, 4, 512], BF16)
                    mB = S.buf()
                    S.op("sp", lambda e: e.dma_start(out=masks[:, :, :], in_=masks_d.rearrange("p (j q) -> p j q", j=4)), writes=[mB], dma=True)
                    QT_r = Ring(S, [QT_t[:, i, :, :] for i in range(2)])
                    e_r = Ring(S, [e_t[:, i, :, :] for i in range(3)])
                    lp_r = Ring(S, [lp_t[:, i, :, :] for i in range(3)])
                    sr_r = Ring(S, [sr_t[:, i, :, :] for i in range(4)])
                    p_r = Ring(S, [p_t[:, i, :, :] for i in range(2)])
                    at_r = Ring(S, [at_t[:, i, :, :] for i in range(3)])
                    ot_r = Ring(S, [ot_t[0:64, i, :] for i in range(4)])
                    zB = [[S.buf(), S.buf()], [S.buf(), S.buf()]]
                    zP = [psum[0], psum[3]]
                    wBk = [S.buf(), S.buf()]
                    wP = psum[1]
                    oBk = [S.buf(), S.buf()]
                    oP = [psum[2][0:64, 0:512], psum[2][0:64, 512:1024]]
                    zi = [0]

                    def loadq(G):
                        QT, QTB = QT_r.next()
                        S.op("sp", lambda e: e.dma_start(out=QT, in_=qT_d[:, :, G * 512:(G + 1) * 512]), writes=[QTB], dma=True)
                        return QT, QTB

                    qnext = loadq(0)
                    for G in range(NG):
                        QT, QTB = qnext
                        if G + 1 < NG:
                            qnext = loadq(G + 1)
                        jmax = 4 * G + 3
                        units = [(hp, j) for hp in range(4) for j in range(jmax, -1, -1)]
                        NU = len(units)
                        st_ = [dict() for _ in range(NU)]
                        cur_sr = {}

                        def stage1a(u, G=G, QT=QT, QTB=QTB, units=units, st_=st_):
                            hp, j = units[u]
                            d = st_[u]
                            k = zi[0] % 2
                            zi[0] += 1
                            z = zP[k]
                            for hh in range(2):
                                kt = KT[hh * 64:(hh + 1) * 64, hp, j * 128:(j + 1) * 128]
                                qs = QT[hh * 64:(hh + 1) * 64, hp, :]
                                zz = z[:, hh * 512:(hh + 1) * 512]
                                S.op("pe", lambda e, zz=zz, kt=kt, qs=qs: e.matmul(zz, lhsT=kt, rhs=qs, start=True, stop=True), reads=[QTB], writes=[zB[k][hh]])
                            ee, eeB = e_r.next()
                            z3 = z[:, :].rearrange("p (a b) -> p a b", a=2)
                            S.op("act", lambda e: e.activation(out=ee, in_=z3, func=AF.Exp), reads=[zB[k][0], zB[k][1]], writes=[eeB])
                            if j >= 4 * G:
                                mk = masks[:, j - 4 * G, :]
                                for hh in range(2):
                                    S.op("dve", lambda e, hh=hh: e.tensor_tensor(out=ee[:, hh, :], in0=ee[:, hh, :], in1=mk, op=ALU.mult), reads=[mB], writes=[eeB])
                            d["ee"], d["eeB"] = ee, eeB

                        def stage1b(u, units=units, st_=st_, cur_sr=cur_sr, jmax=jmax):
                            hp, j = units[u]
                            d = st_[u]
                            ee, eeB = d["ee"], d["eeB"]
                            lp, lpB = lp_r.next()
                            S.op("act", lambda e: e.activation(out=lp, in_=ee, func=AF.Ln, bias=oneT[:, 0:1]), reads=[eeB], writes=[lpB])
                            d["lp"], d["lpB"] = lp, lpB
                            d["sr_in"] = cur_sr.get(hp)
                            if j > 0:
                                sr, srB = sr_r.next()
                                if j == jmax:
                                    S.op("dve", lambda e: e.tensor_copy(out=sr, in_=lp), reads=[lpB], writes=[srB])
                                else:
                                    psr, psrB = cur_sr[hp]
                                    S.op("dve", lambda e: e.tensor_tensor(out=sr, in0=psr, in1=lp, op=ALU.add), reads=[lpB, psrB], writes=[srB])
                                cur_sr[hp] = (sr, srB)

                        def stage2(u, units=units, st_=st_, jmax=jmax):
                            hp, j = units[u]
                            d = st_[u]
                            lp, lpB = d["lp"], d["lpB"]
                            ee, eeB = d["ee"], d["eeB"]
                            last = (j == jmax)
                            for hh in range(2):
                                ww = wP[:, hh * 512:(hh + 1) * 512]
                                S.op("pe", lambda e, ww=ww, hh=hh: e.matmul(ww, lhsT=negtri, rhs=lp[:, hh, :], start=True, stop=last), reads=[lpB, cstB], writes=[wBk[hh]])
                                if not last:
                                    psr, psrB = d["sr_in"]
                                    S.op("pe", lambda e, ww=ww, hh=hh, psr=psr: e.matmul(ww, lhsT=negones, rhs=psr[:, hh, :], start=False, stop=True), reads=[psrB, cstB], writes=[wBk[hh]])
                            pp_, ppB_ = p_r.next()
                            w3 = wP[:, :].rearrange("p (a b) -> p a b", a=2)
                            S.op("act", lambda e: e.activation(out=pp_, in_=w3, func=AF.Exp), reads=[wBk[0], wBk[1]], writes=[ppB_])
                            at, atB = at_r.next()
                            S.op("dve", lambda e: e.tensor_tensor(out=at, in0=pp_, in1=ee, op=ALU.mult), reads=[ppB_, eeB], writes=[atB])
                            d["at"], d["atB"] = at, atB

                        def stage3(u, G=G, units=units, st_=st_, jmax=jmax):
                            hp, j = units[u]
                            d = st_[u]
                            at, atB = d["at"], d["atB"]
                            first = (j == jmax)
                            for hh in range(2):
                                h = 2 * hp + hh
                                o = oP[hh]
                                S.op("pe", lambda e, o=o, h=h, hh=hh: e.matmul(o, lhsT=VV[:, j, h * 64:(h + 1) * 64], rhs=at[:, hh, :], start=first, stop=(j == 0)),
                                     reads=[atB], writes=[oBk[hh]])
                            if j == 0:
                                for hh in range(2):
                                    ot, otB = ot_r.next()
                                    o = oP[hh]
                                    dst = oT_d[hh * 64:(hh + 1) * 64, hp, G * 512:(G + 1) * 512]
                                    if hh == 0:
                                        S.op("act", lambda e, ot=ot, o=o: e.copy(out=ot, in_=o), reads=[oBk[hh]], writes=[otB])
                                    else:
                                        S.op("dve", lambda e, ot=ot, o=o: e.tensor_copy(out=ot, in_=o), reads=[oBk[hh]], writes=[otB])
                                    S.op("sp", lambda e, dst=dst, ot=ot: e.dma_start(out=dst, in_=ot), reads=[otB], dma=True)
                            st_[u] = None

                        SK2, SK3 = 2, 4
                        for i in range(NU + SK3):
                            if i < NU:
                                stage1a(i)
                            if 0 <= i - SK2 < NU:
                                stage2(i - SK2)
                            if i < NU:
                                stage1b(i)
                            if 0 <= i - SK3 < NU:
                                stage3(i - SK3)
                    S.flush()

            with contextlib.ExitStack() as ph:
                w_u = sb(ph, "w_u", [128, 8, 1024], BF16)
                w_p = sb(ph, "w_p", [128, 8, 512], BF16)
                pw = sb(ph, "pw", [128, 4, 128], BF16)
                diag_t = sb(ph, "diag", [128, 4, 31, 128], BF16)
                wB = S.buf()
                with contextlib.ExitStack() as st_es:
                    stg = sb(st_es, "stg", [128, 6, 1024], F32)
                    sring = Ring(S, [stg[:, i, :] for i in range(6)])
                    load_w(sring, w_u, wB, w_in_d, l * 1024, 8, 1536, 1024, scale_col=V_GMIX)
                    load_w(sring, w_p, wB, w_in_d, l * 1024, 8, 2560, 512, scale_col=V_GMIX)
                    load_w(sring, pw, wB, pw_d, l * 512, 4, 0, 128)
                    for c in range(4):
                        for k in range(31):
                            eng = "dve" if (k % 2 == 0) else "pool"
                            S.op(eng, lambda e, c=c, k=k: e.tensor_scalar(out=diag_t[:, c, k, :], in0=cst[:, 0:128], scalar1=vecs[:, V_DW + c * 31 + k: V_DW + c * 31 + k + 1], scalar2=None, op0=ALU.mult),
                                 reads=[vB, cstB], writes=[wB])
                    S.flush()
                xnT_t = sb(ph, "xnT", [128, 2, 8, 512], BF16)
                cb_t = sb(ph, "cb", [128, 2, 4, 542], BF16)
                sig_t = sb(ph, "sig", [128, 2, 512], F32)
                xc_t = sb(ph, "xc", [128, 4, 512], F32)
                xbf_t = sb(ph, "xbf", [128, 4, 512], BF16)
                xsq_t = sb(ph, "xsq", [128, 4, 512], BF16)
                ln_t = sb(ph, "ln", [128, 4, 512], F32)
                tt_t = sb(ph, "tt", [128, 2, 512], F32)
                hs_t = sb(ph, "hs", [128, 2, 4, 512], BF16)
                pb_t = sb(ph, "pb", [128, 2, 4, 527], F32)
                pa_t = sb(ph, "pa", [128, 2, 527], F32)
                yp_t = sb(ph, "yp", [128, 4, 512], BF16)
                zp_t = sb(ph, "zp", [128, 2, 4, 512], BF16)
                icnt = sb(ph, "icnt", [128, 4, 16], F32)
                t16 = sb(ph, "t16", [128, 16], F32)
                iB = S.buf()
                S.op("sp", lambda e: e.dma_start(out=icnt[:, :, :], in_=icnt_d.rearrange("p (g t) -> p g t", g=4)), writes=[iB], dma=True)
                br = bank_ring()
                xnT_r = Ring(S, [xnT_t[:, i, :, :] for i in range(2)])
                cb_r = Ring(S, [cb_t[:, i, :, :] for i in range(2)])
                sig_r = Ring(S, [sig_t[:, i, :] for i in range(2)])
                tt_r = Ring(S, [tt_t[:, i, :] for i in range(2)])
                hs_r = Ring(S, [hs_t[:, i, :, :] for i in range(2)])
                pb_r = Ring(S, [pb_t[:, i, :, :] for i in range(2)])
                pa_r = Ring(S, [pa_t[:, i, :] for i in range(2)])
                zp_r = Ring(S, [zp_t[:, i, :, :] for i in range(2)])
                lnB, t16B = S.buf(), S.buf()
                xcB = [S.buf() for _ in range(4)]
                xbfB = [S.buf() for _ in range(4)]
                xsqB = [S.buf() for _ in range(4)]
                ypB = [S.buf() for _ in range(4)]
                cb_prev = None
                pb_prev = None

                def ldxT(G):
                    xT, xTB = xnT_r.next()
                    S.op("sp", lambda e: e.dma_start(out=xT, in_=xnT_d[:, :, G * 512:(G + 1) * 512]), writes=[xTB], dma=True)
                    return xT, xTB

                nxt = ldxT(0)
                for G in range(NG):
                    xT, xTB = nxt
                    if G + 1 < NG:
                        nxt = ldxT(G + 1)
                    cb, _ = cb_r.next()
                    cbB = [S.buf() for _ in range(4)]
                    pb, _ = pb_r.next()
                    pbB = [S.buf() for _ in range(4)]
                    zp, zpB = zp_r.next()

                    def Pj(c, G=G, cb=cb, cbB=cbB, xT=xT, xTB=xTB, cb_prev=cb_prev):
                        if G == 0:
                            S.op("pool", lambda e: e.memset(cb[:, c, 0:30], 0.0), writes=[cbB[c]])
                        else:
                            pcb, pcbB = cb_prev
                            S.op("pool", lambda e: e.tensor_copy(out=cb[:, c, 0:30], in_=pcb[:, c, 512:542]), reads=[pcbB[c]], writes=[cbB[c]])
                        psa, psaB = br.next()
                        psg, psgB = br.next()
                        for kc in range(8):
                            S.op("pe", lambda e, kc=kc: e.matmul(psa, lhsT=w_u[:, kc, c * 128:(c + 1) * 128], rhs=xT[:, kc, :], start=(kc == 0), stop=(kc == 7)),
                                 reads=[wB, xTB], writes=[psaB])
                        for kc in range(8):
                            S.op("pe", lambda e, kc=kc: e.matmul(psg, lhsT=w_u[:, kc, 512 + c * 128:512 + (c + 1) * 128], rhs=xT[:, kc, :], start=(kc == 0), stop=(kc == 7)),
                                 reads=[wB, xTB], writes=[psgB])
                        sg, sgB = sig_r.next()
                        S.op("act", lambda e: e.activation(out=sg, in_=psg, func=AF.Sigmoid), reads=[psgB], writes=[sgB])
                        S.op("dve", lambda e: e.tensor_tensor(out=cb[:, c, 30:542], in0=psa, in1=sg, op=ALU.mult), reads=[psaB, sgB], writes=[cbB[c]])

                    def Cv(c, cb=cb, cbB=cbB):
                        psc, pscB = br.next()
                        for k in range(31):
                            S.op("pe", lambda e, k=k: e.matmul(psc, lhsT=diag_t[:, c, k, :], rhs=cb[:, c, k:k + 512], start=(k == 0), stop=(k == 30)),
                                 reads=[wB, cbB[c]], writes=[pscB])
                        bcol = vecs[:, V_DWB + c:V_DWB + c + 1]
                        S.op("act", lambda e: e.activation(out=xc_t[:, c, :], in_=psc, func=AF.Identity, bias=bcol), reads=[pscB, vB], writes=[xcB[c]])
                        S.op("act", lambda e: e.activation(out=xsq_t[:, c, :], in_=psc, func=AF.Square, bias=bcol), reads=[pscB, vB], writes=[xsqB[c]])
                        S.op("dve", lambda e: e.tensor_copy(out=xbf_t[:, c, :], in_=xc_t[:, c, :]), reads=[xcB[c]], writes=[xbfB[c]])

                    def Pp(g, G=G, pb=pb, pbB=pbB, xT=xT, xTB=xTB, pb_prev=pb_prev):
                        w = 2 << g
                        if G == 0:
                            S.op("pool", lambda e: e.memset(pb[:, g, 0:15], 0.0), writes=[pbB[g]])
                        else:
                            ppb, ppbB = pb_prev
                            S.op("pool", lambda e: e.tensor_copy(out=pb[:, g, 0:15], in_=ppb[:, g, 512:527]), reads=[ppbB[g]], writes=[pbB[g]])
                        psp, pspB = br.next()
                        for kc in range(8):
                            S.op("pe", lambda e, kc=kc: e.matmul(psp, lhsT=w_p[:, kc, g * 128:(g + 1) * 128], rhs=xT[:, kc, :], start=(kc == 0), stop=(kc == 7)),
                                 reads=[wB, xTB], writes=[pspB])
                        S.op("act", lambda e: e.copy(out=pb[:, g, 15:527], in_=psp), reads=[pspB], writes=[pbB[g]])
                        cur = pb[:, g, :]
                        curB = pbB[g]
                        m = 1
                        while m < w:
                            nx, nxB = pa_r.next()
                            S.op("dve", lambda e, nx=nx, cur=cur, m=m: e.tensor_tensor(out=nx[:, m:527], in0=cur[:, m:527], in1=cur[:, 0:527 - m], op=ALU.add),
                                 reads=[curB], writes=[nxB])
                            cur, curB = nx, nxB
                            m *= 2
                        S.op("dve", lambda e, cur=cur: e.scalar_tensor_tensor(out=yp_t[:, g, :], in0=cur[:, 15:527], scalar=1.0 / w, in1=pb[:, g, 15:527], op0=ALU.mult, op1=ALU.subtract),
                             reads=[curB, pbB[g]], writes=[ypB[g]])
                        if G == 0:
                            S.op("dve", lambda e, cur=cur: e.tensor_tensor(out=t16[:, :], in0=cur[:, 15:31], in1=icnt[:, g, :], op=ALU.mult), reads=[curB, iB], writes=[t16B])
                            S.op("dve", lambda e: e.tensor_tensor(out=yp_t[:, g, 0:16], in0=t16[:, :], in1=pb[:, g, 15:31], op=ALU.subtract), reads=[t16B, pbB[g]], writes=[ypB[g]])

                    def Pw(g, zp=zp, zpB=zpB):
                        psq, psqB = br.next()
                        S.op("pe", lambda e: e.matmul(psq, lhsT=pw[:, g, :], rhs=yp_t[:, g, :], start=True, stop=True), reads=[wB, ypB[g]], writes=[psqB])
                        S.op("act", lambda e: e.activation(out=zp[:, g, :], in_=psq, func=AF.Copy, scale=vecs[:, V_PS + g:V_PS + g + 1]), reads=[psqB, vB], writes=[zpB])

                    Pj(0)
                    Pj(1)
                    Cv(0)
                    Pp(0)
                    Pj(2)
                    Cv(1)
                    Pp(1)
                    Pj(3)
                    Cv(2)
                    Pp(2)
                    Cv(3)
                    Pp(3)
                    cb_prev = (cb, cbB)
                    s1, s1B = br.next()
                    s2, s2B = br.next()
                    for c in range(4):
                        S.op("pe", lambda e, s1=s1, c=c: e.matmul(s1, lhsT=ones, rhs=xbf_t[:, c, :], start=(c == 0), stop=(c == 3)), reads=[xbfB[c], cstB], writes=[s1B])
                    for c in range(4):
                        S.op("pe", lambda e, s2=s2, c=c: e.matmul(s2, lhsT=ones, rhs=xsq_t[:, c, :], start=(c == 0), stop=(c == 3)), reads=[xsqB[c], cstB], writes=[s2B])
                    for g in range(4):
                        Pw(g)
                    S.op("sp", lambda e, zp=zp, G=G: e.dma_start(out=zpT_d[:, :, G * 512:(G + 1) * 512], in_=zp), reads=[zpB], dma=True)
                    mean, msq, var = ln_t[:, 0, :], ln_t[:, 1, :], ln_t[:, 2, :]
                    S.op("act", lambda e, s1=s1: e.activation(out=mean, in_=s1, func=AF.Copy, scale=1.0 / 512), reads=[s1B], writes=[lnB])
                    S.op("dve", lambda e: e.tensor_tensor(out=msq, in0=mean, in1=mean, op=ALU.mult), reads=[lnB], writes=[lnB])
                    S.op("dve", lambda e, s2=s2: e.scalar_tensor_tensor(out=var, in0=s2, scalar=1.0 / 512, in1=msq, op0=ALU.mult, op1=ALU.subtract), reads=[s2B, lnB], writes=[lnB])
                    S.op("act", lambda e: e.activation(out=var, in_=var, func=AF.Ln, bias=epsT[:, 0:1]), reads=[lnB], writes=[lnB])
                    S.op("act", lambda e: e.activation(out=var, in_=var, func=AF.Exp, scale=-0.5), reads=[lnB], writes=[lnB])
                    hs, hsB = hs_r.next()
                    for c in range(4):
                        t1, t1B = tt_r.next()
                        S.op("dve", lambda e, t1=t1, c=c: e.tensor_tensor(out=t1, in0=xc_t[:, c, :], in1=mean, op=ALU.subtract), reads=[xcB[c], lnB], writes=[t1B])
                        S.op("pool", lambda e, t1=t1: e.tensor_tensor(out=t1, in0=t1, in1=var, op=ALU.mult), reads=[lnB, t1B], writes=[t1B])
                        S.op("act", lambda e, t1=t1, c=c, hs=hs: e.activation(out=hs[:, c, :], in_=t1, func=AF.Silu, scale=vecs[:, V_LNG + c:V_LNG + c + 1], bias=vecs[:, V_LNB + c:V_LNB + c + 1]),
                             reads=[t1B, vB], writes=[hsB])
                    S.op("sp", lambda e, hs=hs, G=G: e.dma_start(out=hsT_d[:, :, G * 512:(G + 1) * 512], in_=hs), reads=[hsB], dma=True)
                    pb_prev = (pb, pbB)
                S.flush()

            with contextlib.ExitStack() as ph:
                w_g = sb(ph, "w_g", [128, 8, 3072], BF16)
                w_br = sb(ph, "w_br", [128, 3, 4, 1024], BF16)
                w_o = sb(ph, "w_o", [128, 8, 1024], BF16)
                wB = S.buf()
                with contextlib.ExitStack() as st_es:
                    stg = sb(st_es, "stg", [128, 6, 1024], F32)
                    sring = Ring(S, [stg[:, i, :] for i in range(6)])
                    load_w(sring, w_g, wB, w_in_d, l * 1024, 8, 3072, 3072, scale_col=V_GMIX)
                    load_w(sring, w_br[:, 0, :, :], wB, w_co_d, l * 512, 4, 0, 1024)
                    load_w(sring, w_br[:, 1, :, :], wB, w_ao_d, l * 512, 4, 0, 1024)
                    load_w(sring, w_br[:, 2, :, :], wB, w_po_d, l * 512, 4, 0, 1024)
                    load_w(sring, w_o, wB, w_o_d, l * 1024, 8, 0, 1024)
                    S.flush()
                xnT_t = sb(ph, "xnT", [128, 2, 8, 512], BF16)
                src_t = sb(ph, "srcs", [128, 2, 3, 4, 512], BF16)
                hin_t = sb(ph, "hin", [128, 2, 4, 1024], F32)
                mg_t = sb(ph, "mg", [128, 8, 512], F32)
                mgb_t = sb(ph, "mgb", [128, 8, 512], BF16)
                gate_t = sb(ph, "gate", [128, 2, 512], F32)
                tm_t = sb(ph, "tm", [128, 2, 512], F32)
                ho_t = sb(ph, "ho", [128, 2, 1024], F32)
                br = bank_ring()
                xnT_r = Ring(S, [xnT_t[:, i, :, :] for i in range(2)])
                src_r = Ring(S, [src_t[:, i, :, :, :] for i in range(2)])
                gate_r = Ring(S, [gate_t[:, i, :] for i in range(2)])
                tm_r = Ring(S, [tm_t[:, i, :] for i in range(2)])
                ho_r = Ring(S, [ho_t[:, i, :] for i in range(2)])
                hin_r = Ring(S, [hin_t[:, i, :, :] for i in range(2)])
                mgB = [S.buf() for _ in range(8)]
                mgbB = S.buf()
                srcs_d = [hsT_d, oT_d, zpT_d]
                def ldall(G):
                    xT, xTB = xnT_r.next()
                    S.op("sp", lambda e: e.dma_start(out=xT, in_=xnT_d[:, :, G * 512:(G + 1) * 512]), writes=[xTB], dma=True)
                    sr, srB = src_r.next()
                    for b3 in range(3):
                        S.op("sp", lambda e, b3=b3: e.dma_start(out=sr[:, b3, :, :], in_=srcs_d[b3][:, :, G * 512:(G + 1) * 512]), writes=[srB], dma=True)
                    hin, hinB = hin_r.next()
                    for tt in range(4):
                        r0 = G * 512 + tt * 128
                        S.op("sp", lambda e, tt=tt, r0=r0: e.dma_start(out=hin[:, tt, :], in_=src_d[r0:r0 + 128, :]), writes=[hinB], dma=True)
                    return xT, xTB, sr, srB, hin, hinB

                nxt = ldall(0)
                for G in range(NG):
                    xT, xTB, sr, srB, hin, hinB = nxt
                    if G + 1 < NG:
                        nxt = ldall(G + 1)
                    for dc in range(8):
                        for b3 in range(3):
                            psg, psgB = br.next()
                            col = b3 * 1024 + dc * 128
                            for kc in range(8):
                                S.op("pe", lambda e, psg=psg, kc=kc, col=col, xT=xT: e.matmul(psg, lhsT=w_g[:, kc, col:col + 128], rhs=xT[:, kc, :], start=(kc == 0), stop=(kc == 7)),
                                     reads=[wB, xTB], writes=[psgB])
                            gt, gtB = gate_r.next()
                            gcol = vecs[:, V_GB + b3 * 8 + dc: V_GB + b3 * 8 + dc + 1]
                            S.op("act", lambda e, gt=gt, psg=psg, gcol=gcol: e.activation(out=gt, in_=psg, func=AF.Sigmoid, bias=gcol), reads=[psgB, vB], writes=[gtB])
                            psy, psyB = br.next()
                            for kc in range(4):
                                S.op("pe", lambda e, psy=psy, kc=kc, b3=b3, dc=dc, sr=sr: e.matmul(psy, lhsT=w_br[:, b3, kc, dc * 128:(dc + 1) * 128], rhs=sr[:, b3, kc, :], start=(kc == 0), stop=(kc == 3)),
                                     reads=[wB, srB], writes=[psyB])
                            if b3 == 0:
                                S.op("dve", lambda e, dc=dc, psy=psy, gt=gt: e.tensor_tensor(out=mg_t[:, dc, :], in0=psy, in1=gt, op=ALU.mult), reads=[psyB, gtB], writes=[mgB[dc]])
                            else:
                                tm, tmB = tm_r.next()
                                S.op("dve", lambda e, tm=tm, psy=psy, gt=gt: e.tensor_tensor(out=tm, in0=psy, in1=gt, op=ALU.mult), reads=[psyB, gtB], writes=[tmB])
                                if b3 == 1:
                                    S.op("pool", lambda e, dc=dc, tm=tm: e.tensor_tensor(out=mg_t[:, dc, :], in0=mg_t[:, dc, :], in1=tm, op=ALU.add), reads=[tmB], writes=[mgB[dc]])
                                else:
                                    S.op("pool", lambda e, dc=dc, tm=tm: e.tensor_tensor(out=mgb_t[:, dc, :], in0=mg_t[:, dc, :], in1=tm, op=ALU.add), reads=[tmB, mgB[dc]], writes=[mgbB])
                    for tt in range(4):
                        ho, hoB = ho_r.next()
                        for dh in range(2):
                            pso, psoB = br.next()
                            for fc in range(8):
                                S.op("pe", lambda e, pso=pso, fc=fc, tt=tt, dh=dh: e.matmul(pso, lhsT=mgb_t[:, fc, tt * 128:(tt + 1) * 128], rhs=w_o[:, fc, dh * 512:(dh + 1) * 512], start=(fc == 0), stop=(fc == 7)),
                                     reads=[wB, mgbB], writes=[psoB])
                            S.op("dve", lambda e, ho=ho, pso=pso, tt=tt, dh=dh, hin=hin: e.tensor_tensor(out=ho[:, dh * 512:(dh + 1) * 512], in0=pso, in1=hin[:, tt, dh * 512:(dh + 1) * 512], op=ALU.add),
                                 reads=[psoB, hinB], writes=[hoB])
                        r0 = G * 512 + tt * 128
                        S.op("sp", lambda e, ho=ho, r0=r0: e.dma_start(out=hA_d[r0:r0 + 128, :], in_=ho), reads=[hoB], dma=True)
                S.flush()

            with contextlib.ExitStack() as ph:
                w1 = sb(ph, "w1", [128, 8, 4096], BF16)
                w2 = sb(ph, "w2", [128, 32, 1024], BF16)
                wB = S.buf()
                with contextlib.ExitStack() as st_es:
                    stg = sb(st_es, "stg", [128, 6, 1024], F32)
                    sring = Ring(S, [stg[:, i, :] for i in range(6)])
                    load_w(sring, w1, wB, w_m1_d, l * 1024, 8, 0, 4096, scale_col=V_GMLP)
                    load_w(sring, w2, wB, w_m2_d, l * 4096, 32, 0, 1024)
                    S.flush()
                HT = 256
                xin_t = sb(ph, "xin", [128, 4, 1024], F32)
                junk_t = sb(ph, "junk", [128, 1024], BF16)
                xn_t = sb(ph, "xn", [128, 2, 1024], F32)
                stat_t = sb(ph, "stat", [128, 2, 4], F32)
                xnT_t = sb(ph, "xnT", [128, 2, 8, HT], BF16)
                ff_t = sb(ph, "ff", [128, 32, HT], BF16)
                rl_t = sb(ph, "rl", [128, 2, HT], F32)
                ho_t = sb(ph, "ho", [128, 2, 1024], F32)
                xin_r = Ring(S, [xin_t[:, i, :] for i in range(4)])
                xn_r = Ring(S, [xn_t[:, i, :] for i in range(2)])
                stat_r = Ring(S, [stat_t[:, i, :] for i in range(2)])
                xnT_r = Ring(S, [xnT_t[:, i, :, :] for i in range(2)])
                rl_r = Ring(S, [rl_t[:, i, :] for i in range(2)])
                ho_r = Ring(S, [ho_t[:, i, :] for i in range(2)])
                junkB = S.buf()
                ffB = [S.buf() for _ in range(32)]
                aps = []
                for p in psum[0:3]:
                    aps.append(p[:, 0:512])
                    aps.append(p[:, 512:1024])
                br = Ring(S, aps)
                ppB = S.buf()
                def ldh(H):
                    res = []
                    for tt in range(HT // 128):
                        r0 = H * HT + tt * 128
                        xin, xinB = xin_r.next()
                        S.op("sp", lambda e, xin=xin, r0=r0: e.dma_start(out=xin, in_=hA_d[r0:r0 + 128, :]), writes=[xinB], dma=True)
                        res.append((xin, xinB))
                    return res

                nxt = ldh(0)
                for H in range(T // HT):
                    xT, xTB = xnT_r.next()
                    xins = []
                    curx = nxt
                    if H + 1 < T // HT:
                        nxt = ldh(H + 1)
                    for tt in range(HT // 128):
                        r0 = H * HT + tt * 128
                        xin, xinB = curx[tt]
                        xn, xnB = xn_r.next()
                        stt, sttB = stat_r.next()
                        rms_tile(None, xin, xinB, junk_t[:, :], junkB, stt, sttB, xn, xnB)
                        transpose_tile(xn, xnB, psum[3], ppB, xT, xTB, tt * 128, ("act", "dve"))
                        xins.append((xin, xinB))
                    for fc in range(32):
                        ps, psB = br.next()
                        for kc in range(8):
                            S.op("pe", lambda e, ps=ps, kc=kc, fc=fc, xT=xT: e.matmul(ps[:, 0:HT], lhsT=w1[:, kc, fc * 128:(fc + 1) * 128], rhs=xT[:, kc, :], start=(kc == 0), stop=(kc == 7)),
                                 reads=[wB, xTB], writes=[psB])
                        rl, rlB = rl_r.next()
                        S.op("act", lambda e, rl=rl, ps=ps: e.activation(out=rl, in_=ps[:, 0:HT], func=AF.Relu), reads=[psB], writes=[rlB])
                        eng = "pool" if fc % 2 == 0 else "dve"
                        S.op(eng, lambda e, rl=rl, fc=fc: e.tensor_tensor(out=ff_t[:, fc, :], in0=rl, in1=rl, op=ALU.mult), reads=[rlB], writes=[ffB[fc]])
                    for tt in range(HT // 128):
                        ho, hoB = ho_r.next()
                        xin, xinB = xins[tt]
                        for dh in range(2):
                            pso, psoB = br.next()
                            for fc in range(32):
                                S.op("pe", lambda e, pso=pso, fc=fc, tt=tt, dh=dh: e.matmul(pso, lhsT=ff_t[:, fc, tt * 128:(tt + 1) * 128], rhs=w2[:, fc, dh * 512:(dh + 1) * 512], start=(fc == 0), stop=(fc == 31)),
                                     reads=[wB, ffB[fc]], writes=[psoB])
                            S.op("dve", lambda e, ho=ho, pso=pso, xin=xin, dh=dh: e.tensor_tensor(out=ho[:, dh * 512:(dh + 1) * 512], in0=pso, in1=xin[:, dh * 512:(dh + 1) * 512], op=ALU.add),
                                 reads=[psoB, xinB], writes=[hoB])
                        r0 = H * HT + tt * 128
                        S.op("sp", lambda e, ho=ho, r0=r0: e.dma_start(out=out_d[r0:r0 + 128, :], in_=ho), reads=[hoB], dma=True)
                S.flush()
    return nc


def host_prep(inp, L):
    f = lambda a: np.ascontiguousarray(np.asarray(a, dtype=np.float32))
    vecs = np.zeros((L, 128, NVEC), np.float32)
    for l in range(L):
        vecs[l, :, V_GB:V_GB + 24] = f(inp["gate_b"])[l].reshape(24, 128).T
        vecs[l, :, V_DWB:V_DWB + 4] = f(inp["conv_dw_b"])[l].reshape(4, 128).T
        vecs[l, :, V_LNG:V_LNG + 4] = f(inp["conv_ln_g"])[l].reshape(4, 128).T
        vecs[l, :, V_LNB:V_LNB + 4] = f(inp["conv_ln_b"])[l].reshape(4, 128).T
        vecs[l, :, V_PS:V_PS + 4] = f(inp["pool_scale"])[l].reshape(4, 128).T
        vecs[l, :, V_GQ] = np.tile(f(inp["q_norm_g"])[l], 2)
        vecs[l, :, V_GK] = np.tile(f(inp["k_norm_g"])[l], 2)
        vecs[l, :, V_GMIX:V_GMIX + 8] = f(inp["mix_norm_g"])[l].reshape(8, 128).T
        vecs[l, :, V_GMLP:V_GMLP + 8] = f(inp["mlp_norm_g"])[l].reshape(8, 128).T
        dw = f(inp["conv_dw"])[l]
        vecs[l, :, V_DW:V_DW + 124] = dw.reshape(31, 4, 128).transpose(2, 1, 0).reshape(128, 124)
    ident = np.eye(128, dtype=np.float32)
    jj, kk = np.meshgrid(np.arange(128), np.arange(128), indexing="ij")
    negtri = -(jj >= kk).astype(np.float32)
    negones = -np.ones((128, 128), np.float32)
    ones = np.ones((128, 128), np.float32)
    blk = (jj // 64 == kk // 64).astype(np.float32)
    cst = np.concatenate([ident, negtri, negones, ones, blk], axis=1).astype(ml_dtypes.bfloat16)
    p = np.arange(128)[:, None]
    q = np.arange(512)[None, :]
    masks = np.concatenate([(q > p + j * 128).astype(np.float32) for j in range(4)], axis=1).astype(ml_dtypes.bfloat16)
    icnt = np.zeros((128, 4, 16), np.float32)
    for g in range(4):
        w = 2 << g
        icnt[:, g, :] = 1.0 / np.minimum(np.arange(16) + 1, w).astype(np.float32)
    com = {
        "w_in": f(inp["w_in"]).reshape(L * 1024, 6144),
        "w_conv_out": f(inp["w_conv_out"]).reshape(L * 512, 1024),
        "w_att_out": f(inp["w_att_out"]).reshape(L * 512, 1024),
        "w_pool_out": f(inp["w_pool_out"]).reshape(L * 512, 1024),
        "pool_w": f(inp["pool_w"]).reshape(L * 512, 128),
        "w_o": f(inp["w_o"]).reshape(L * 1024, 1024),
        "w_mlp_in": f(inp["w_mlp_in"]).reshape(L * 1024, 4096),
        "w_mlp_out": f(inp["w_mlp_out"]).reshape(L * 4096, 1024),
        "vecs": vecs.reshape(L * 128, NVEC),
        "cst": cst,
        "identf": ident,
        "masks": masks,
        "icnt": icnt.reshape(128, 64),
    }
    return com


_NC_CACHE = {}


def run(inp, seqs, T, L):
    key = (T, L)
    if key not in _NC_CACHE:
        _NC_CACHE[key] = build(T, L)
    nc = _NC_CACHE[key]
    com = host_prep(inp, L)
    in_maps = []
    for c in range(8):
        m = dict(com)
        m["x"] = np.ascontiguousarray(seqs[c], dtype=np.float32)
        in_maps.append(m)
    res = run_bass_kernel_spmd(nc, in_maps, core_ids=list(range(8)))
    return [res.results[c]["y"] for c in range(8)]


def kernel(**inputs):
    x = np.asarray(inputs["x"], dtype=np.float32)
    B, T, D = x.shape
    L = np.asarray(inputs["w_in"]).shape[0]
    seqs = [x[c // 2] for c in range(8)]
    outs = run(inputs, seqs, T, L)
    return np.stack([outs[2 * b] for b in range(B)], axis=0).astype(np.float32)
```

```python
import contextlib
import numpy as np
import ml_dtypes
import concourse.bass as bass
import concourse.mybir as mybir
from concourse.bass_utils import run_bass_kernel_spmd

F32 = mybir.dt.float32
BF16 = mybir.dt.bfloat16
AF = mybir.ActivationFunctionType
ALU = mybir.AluOpType
AX = mybir.AxisListType

SAME_ENG_SYNC = True
EPS = 1e-6
NVEC = 24 + 4 + 4 + 4 + 4 + 1 + 1 + 8 + 8 + 124
V_GB, V_DWB, V_LNG, V_LNB, V_PS, V_GQ, V_GK, V_GMIX, V_GMLP, V_DW = 0, 24, 28, 32, 36, 40, 41, 42, 50, 58


class Buf:
    __slots__ = ("w", "r")

    def __init__(self):
        self.w = None
        self.r = {}


class Sched:
    BLK = dict(pe="tensor", act="scalar", dve="vector", pool="gpsimd", sp="sync")

    def __init__(self, nc, es, ndsem=28):
        self.nc = nc
        self.names = ["sp", "pe", "act", "dve", "pool"]
        self.sem = {e: es.enter_context(nc.semaphore("s_" + e)) for e in self.names}
        self.semval = {e: 0 for e in self.names}
        self.dsem = [es.enter_context(nc.semaphore("d%d" % i)) for i in range(ndsem)]
        self.dval = [0] * ndsem
        self.dlast = [None] * ndsem
        self.dnext = 0
        self.q = {e: [] for e in self.names}
        self.seen = {e: {} for e in self.names}
        self.bufs = []

    def buf(self):
        b = Buf()
        self.bufs.append(b)
        return b

    def op(self, eng, fn, reads=(), writes=(), dma=False):
        deps = set()
        for b in reads:
            if b.w is not None:
                deps.add(b.w)
        for b in writes:
            if b.w is not None:
                deps.add(b.w)
            deps.update(b.r.values())
        if dma:
            k = self.dnext
            self.dnext = (self.dnext + 1) % len(self.dsem)
            if self.dlast[k] is not None:
                deps.add(self.dlast[k])
            self.dval[k] += 16
            ev = ("d", k, self.dval[k])
            self.dlast[k] = ev
        else:
            ev = ("c", eng, len(self.q[eng]))
        self.q[eng].append(dict(fn=fn, deps=deps, ev=ev, dma=dma))
        key = (ev[0], ev[1])
        for b in reads:
            b.r[key] = ev
        for b in writes:
            b.w = ev
            b.r = {}
        return ev

    def flush(self):
        nc = self.nc
        obs = set()
        for e in self.names:
            for o in self.q[e]:
                nd = set()
                for d in o["deps"]:
                    if d[0] == "c" and d[1] == e and (e == "pe" or not SAME_ENG_SYNC):
                        continue
                    nd.add(d)
                    if d[0] == "c":
                        obs.add(d)
                o["deps"] = nd
        for e in self.names:
            for o in reversed(self.q[e]):
                if not o["dma"]:
                    obs.add(o["ev"])
                    break
        val = {}
        endval = {}
        for e in self.names:
            c = self.semval[e]
            for o in self.q[e]:
                if o["ev"] in obs:
                    c += 1
                    val[o["ev"]] = c
            endval[e] = c
        with nc.Block() as block:
            for e in self.names:
                def body(engine, e=e):
                    seen = self.seen[e]
                    for o in self.q[e]:
                        for d in sorted(o["deps"]):
                            if d[0] == "c":
                                key, v, sem = ("c", d[1]), val[d], self.sem[d[1]]
                            else:
                                key, v, sem = ("d", d[1]), d[2], self.dsem[d[1]]
                            if seen.get(key, 0) < v:
                                engine.wait_ge(sem, v)
                                seen[key] = v
                        ins = o["fn"](engine)
                        if o["dma"]:
                            ins.then_inc(self.dsem[o["ev"][1]], 16)
                        elif o["ev"] in val:
                            ins.then_inc(self.sem[e], 1)
                    for e2 in self.names:
                        if e2 != e and endval[e2] > seen.get(("c", e2), 0):
                            engine.wait_ge(self.sem[e2], endval[e2])
                            seen[("c", e2)] = endval[e2]
                    for k in range(len(self.dsem)):
                        if self.dval[k] > seen.get(("d", k), 0):
                            engine.wait_ge(self.dsem[k], self.dval[k])
                            seen[("d", k)] = self.dval[k]
                getattr(block, self.BLK[e])(body)
        self.semval = endval
        self.q = {e: [] for e in self.names}
        for b in self.bufs:
            b.w = None
            b.r = {}
        self.bufs = []
        self.dlast = [None] * len(self.dsem)


class Ring:
    def __init__(self, S, aps):
        self.items = [(ap, S.buf()) for ap in aps]
        self.i = 0

    def next(self):
        it = self.items[self.i]
        self.i = (self.i + 1) % len(self.items)
        return it


def build(T, L):
    NG = T // 512
    nc = bass.Bass("TRN2", target_bir_lowering=False)

    def din(name, shape, dt=F32):
        return nc.dram_tensor(name, shape, dt, kind="ExternalInput").ap()

    x_d = din("x", [T, 1024])
    w_in_d = din("w_in", [L * 1024, 6144])
    w_co_d = din("w_conv_out", [L * 512, 1024])
    w_ao_d = din("w_att_out", [L * 512, 1024])
    w_po_d = din("w_pool_out", [L * 512, 1024])
    pw_d = din("pool_w", [L * 512, 128])
    w_o_d = din("w_o", [L * 1024, 1024])
    w_m1_d = din("w_mlp_in", [L * 1024, 4096])
    w_m2_d = din("w_mlp_out", [L * 4096, 1024])
    vecs_d = din("vecs", [L * 128, NVEC])
    cst_d = din("cst", [128, 128 * 5], BF16)
    identf_d = din("identf", [128, 128])
    masks_d = din("masks", [128, 4 * 512], BF16)
    icnt_d = din("icnt", [128, 4 * 16])
    y_d = nc.dram_tensor("y", [T, 1024], F32, kind="ExternalOutput").ap()
    hA_d = nc.dram_tensor("hA", [T, 1024], F32).ap()
    hB_d = nc.dram_tensor("hB", [T, 1024], F32).ap()
    xnT_d = nc.dram_tensor("xnT", [128, 8 * T], BF16).ap().rearrange("p (c t) -> p c t", c=8)
    hsT_d = nc.dram_tensor("hsT", [128, 4 * T], BF16).ap().rearrange("p (c t) -> p c t", c=4)
    zpT_d = nc.dram_tensor("zpT", [128, 4 * T], BF16).ap().rearrange("p (c t) -> p c t", c=4)
    qT_d = nc.dram_tensor("qT", [128, 4 * T], BF16).ap().rearrange("p (c t) -> p c t", c=4)
    oT_d = nc.dram_tensor("oT", [128, 4 * T], BF16).ap().rearrange("p (c t) -> p c t", c=4)

    with contextlib.ExitStack() as es:
        S = Sched(nc, es)

        uid = [0]

        def sb(stack, name, shape, dt):
            uid[0] += 1
            return stack.enter_context(nc.sbuf_tensor("sb%d_%s" % (uid[0], name), shape, dt))

        cst = sb(es, "cst", [128, 5 * 128], BF16)
        identf = sb(es, "identf", [128, 128], F32)
        vecs = sb(es, "vecs", [128, NVEC], F32)
        epsT = sb(es, "epsT", [128, 1], F32)
        oneT = sb(es, "oneT", [128, 1], F32)
        gq8 = sb(es, "gq8", [128, 1], F32)
        psum = [es.enter_context(nc.psum_tensor("ps%d" % i, [128, 1024], F32)) for i in range(4)]
        negtri = cst[:, 128:256]
        negones = cst[:, 256:384]
        ones = cst[:, 384:512]
        blk = cst[:, 512:640]

        cstB = S.buf()
        S.op("sp", lambda e: e.dma_start(out=cst[:, :], in_=cst_d[:, :]), writes=[cstB], dma=True)
        S.op("sp", lambda e: e.dma_start(out=identf[:, :], in_=identf_d[:, :]), writes=[cstB], dma=True)
        S.op("dve", lambda e: e.memset(epsT[:, :], EPS), writes=[cstB])
        S.op("dve", lambda e: e.memset(oneT[:, :], 1.0), writes=[cstB])
        S.flush()

        def bank_ring():
            aps = []
            for p in psum:
                aps.append(p[:, 0:512])
                aps.append(p[:, 512:1024])
            return Ring(S, aps)

        cvt_rr = [0]

        def load_w(stage_ring, dst, dstB, src, rows0, kcs, col0, ncols, scale_col=None):
            for kc in range(kcs):
                for n0 in range(0, ncols, 1024):
                    n1 = min(ncols, n0 + 1024)
                    st, stB = stage_ring.next()
                    sap = src[rows0 + kc * 128: rows0 + (kc + 1) * 128, col0 + n0: col0 + n1]
                    S.op("sp", lambda e, st=st, sap=sap, n=n1 - n0: e.dma_start(out=st[:, 0:n], in_=sap),
                         writes=[stB], dma=True)
                    dap = dst[:, kc, n0:n1]
                    sin = st[:, 0:n1 - n0]
                    eng = ["dve", "act"][cvt_rr[0] % 2]
                    cvt_rr[0] += 1
                    if scale_col is None:
                        if eng == "act":
                            S.op("act", lambda e, dap=dap, sin=sin: e.copy(out=dap, in_=sin), reads=[stB], writes=[dstB])
                        else:
                            S.op(eng, lambda e, dap=dap, sin=sin: e.tensor_copy(out=dap, in_=sin), reads=[stB], writes=[dstB])
                    else:
                        sc = vecs[:, scale_col + kc: scale_col + kc + 1]
                        if eng == "act":
                            S.op("act", lambda e, dap=dap, sin=sin, sc=sc: e.activation(out=dap, in_=sin, func=AF.Copy, scale=sc),
                                 reads=[stB], writes=[dstB])
                        else:
                            S.op(eng, lambda e, dap=dap, sin=sin, sc=sc: e.tensor_scalar(out=dap, in0=sin, scalar1=sc, scalar2=None, op0=ALU.mult),
                                 reads=[stB], writes=[dstB])

        def rms_tile(src_rows, xin, xinB, junk, junkB, st, stB, xn, xnB):
            if src_rows is not None:
                S.op("sp", lambda e: e.dma_start(out=xin, in_=src_rows), writes=[xinB], dma=True)
            S.op("act", lambda e: e.activation(out=junk, in_=xin, func=AF.Square), reads=[xinB], writes=[junkB])
            S.op("dve", lambda e: e.tensor_reduce(out=st[:, 0:1], in_=junk, axis=AX.X, op=ALU.add), reads=[junkB], writes=[stB])
            S.op("act", lambda e: e.activation(out=st[:, 1:2], in_=st[:, 0:1], func=AF.Sqrt, scale=1.0 / 1024, bias=epsT[:, 0:1]),
                 reads=[stB], writes=[stB])
            S.op("dve", lambda e: e.reciprocal(out=st[:, 2:3], in_=st[:, 1:2]), reads=[stB], writes=[stB])
            S.op("dve", lambda e: e.tensor_scalar(out=xn, in0=xin, scalar1=st[:, 2:3], scalar2=None, op0=ALU.mult),
                 reads=[xinB, stB], writes=[xnB])

        def transpose_tile(xn, xnB, pp, ppB, dstT, dstTB, t0, evac):
            for c in range(8):
                S.op("pe", lambda e, c=c: e.transpose(out=pp[:, c * 128:(c + 1) * 128], in_=xn[:, c * 128:(c + 1) * 128], identity=identf[:, :]),
                     reads=[xnB, cstB], writes=[ppB])
            src = pp[:, :].rearrange("p (c t) -> p c t", c=8)
            for hh in range(2):
                d = dstT[:, hh * 4:(hh + 1) * 4, t0:t0 + 128]
                s_ = src[:, hh * 4:(hh + 1) * 4, :]
                if evac[hh] == "act":
                    S.op("act", lambda e, d=d, s_=s_: e.copy(out=d, in_=s_), reads=[ppB], writes=[dstTB])
                else:
                    S.op("dve", lambda e, d=d, s_=s_: e.tensor_copy(out=d, in_=s_), reads=[ppB], writes=[dstTB])

        for l in range(L):
            src_d = x_d if l == 0 else hB_d
            out_d = hB_d if l == L - 1 and False else (y_d if l == L - 1 else hB_d)
            vB = S.buf()
            S.op("sp", lambda e, l=l: e.dma_start(out=vecs[:, :], in_=vecs_d[l * 128:(l + 1) * 128, :]), writes=[vB], dma=True)
            S.op("dve", lambda e: e.tensor_scalar(out=gq8[:, :], in0=vecs[:, V_GQ:V_GQ + 1], scalar1=0.125, scalar2=None, op0=ALU.mult),
                 reads=[vB], writes=[vB])
            S.flush()

            with contextlib.ExitStack() as ph:
                KT = sb(ph, "KT", [128, 4, T], BF16)
                VV = sb(ph, "VV", [128, T // 128, 512], BF16)
                with contextlib.ExitStack() as pa:
                    wqkv = sb(pa, "wqkv", [128, 8, 1536], BF16)
                    wB = S.buf()
                    with contextlib.ExitStack() as st_es:
                        stg = sb(st_es, "stg", [128, 6, 1024], F32)
                        sring = Ring(S, [stg[:, i, :] for i in range(6)])
                        load_w(sring, wqkv, wB, w_in_d, l * 1024, 8, 0, 1536, scale_col=V_GMIX)
                        S.flush()
                    xin_t = sb(pa, "xin", [128, 2, 1024], F32)
                    xn_t = sb(pa, "xn", [128, 2, 1024], F32)
                    stat_t = sb(pa, "stat", [128, 2, 4], F32)
                    xnT_t = sb(pa, "xnT", [128, 2, 8, 512], BF16)
                    QT_t = sb(pa, "QT", [128, 2, 4, 512], BF16)
                    sq_t = sb(pa, "sq", [128, 2, 512], BF16)
                    sd_t = sb(pa, "sd", [128, 2, 512], F32)
                    xin_r = Ring(S, [xin_t[:, i, :] for i in range(2)])
                    xn_r = Ring(S, [xn_t[:, i, :] for i in range(2)])
                    stat_r = Ring(S, [stat_t[:, i, :] for i in range(2)])
                    xnT_r = Ring(S, [xnT_t[:, i, :, :] for i in range(2)])
                    QT_r = Ring(S, [QT_t[:, i, :, :] for i in range(2)])
                    sq_r = Ring(S, [sq_t[:, i, :] for i in range(2)])
                    sd_r = Ring(S, [sd_t[:, i, :] for i in range(2)])
                    a_r = bank_ring()
                    tp_r = Ring(S, [psum[3][:, :], psum[2][:, :]])
                    KTB = S.buf()
                    VB = S.buf()
                    def ldx(ti):
                        xin, xinB = xin_r.next()
                        S.op("sp", lambda e: e.dma_start(out=xin, in_=src_d[ti * 128:(ti + 1) * 128, :]), writes=[xinB], dma=True)
                        return xin, xinB

                    nxt = ldx(0)
                    for G in range(NG):
                        xnT, xnTB = xnT_r.next()
                        for tt in range(4):
                            xin, xinB = nxt
                            if G * 4 + tt + 1 < NG * 4:
                                nxt = ldx(G * 4 + tt + 1)
                            xn, xnB = xn_r.next()
                            stt, sttB = stat_r.next()
                            rms_tile(None, xin, xinB, xn, xnB, stt, sttB, xn, xnB)
                            pidx = 3 if (G * 4 + tt) % 2 == 0 else 2
                            pB0, pB1 = a_r.items[2 * pidx][1], a_r.items[2 * pidx + 1][1]
                            pp = psum[pidx]
                            for c in range(8):
                                S.op("pe", lambda e, c=c, xn=xn, pp=pp: e.transpose(out=pp[:, c * 128:(c + 1) * 128], in_=xn[:, c * 128:(c + 1) * 128], identity=identf[:, :]),
                                     reads=[xnB, cstB], writes=[pB0 if c < 4 else pB1])
                            srcp = pp[:, :].rearrange("p (c t) -> p c t", c=8)
                            S.op("act", lambda e, tt=tt, srcp=srcp, xnT=xnT: e.copy(out=xnT[:, 0:4, tt * 128:(tt + 1) * 128], in_=srcp[:, 0:4, :]),
                                 reads=[pB0], writes=[xnTB])
                            S.op("dve", lambda e, tt=tt, srcp=srcp, xnT=xnT: e.tensor_copy(out=xnT[:, 4:8, tt * 128:(tt + 1) * 128], in_=srcp[:, 4:8, :]),
                                 reads=[pB1], writes=[xnTB])
                        S.op("sp", lambda e, G=G, xnT=xnT: e.dma_start(out=xnT_d[:, :, G * 512:(G + 1) * 512], in_=xnT), reads=[xnTB], dma=True)
                        QT, QTB = QT_r.next()
                        for qk in range(2):
                            for c in range(4):
                                i0 = (qk * 4 + c) % 2
                                ps, psB = a_r.items[i0 * 2]
                                ps2, ps2B = a_r.items[i0 * 2 + 1]
                                col = qk * 512 + c * 128
                                for kc in range(8):
                                    S.op("pe", lambda e, ps=ps, kc=kc, col=col, xnT=xnT: e.matmul(ps, lhsT=wqkv[:, kc, col:col + 128], rhs=xnT[:, kc, :], start=(kc == 0), stop=(kc == 7)),
                                         reads=[wB, xnTB], writes=[psB])
                                sq, sqB = sq_r.next()
                                S.op("act", lambda e, sq=sq, ps=ps: e.activation(out=sq, in_=ps, func=AF.Square), reads=[psB], writes=[sqB])
                                S.op("pe", lambda e, ps2=ps2, sq=sq: e.matmul(ps2, lhsT=blk, rhs=sq, start=True, stop=True), reads=[sqB, cstB], writes=[ps2B])
                                sd, sdB = sd_r.next()
                                S.op("act", lambda e, sd=sd, ps2=ps2: e.activation(out=sd, in_=ps2, func=AF.Ln, scale=1.0 / 64, bias=epsT[:, 0:1]),
                                     reads=[ps2B], writes=[sdB])
                                S.op("act", lambda e, sd=sd: e.activation(out=sd, in_=sd, func=AF.Exp, scale=-0.5), reads=[sdB], writes=[sdB])
                                if qk == 0:
                                    S.op("dve", lambda e, c=c, ps=ps, sd=sd, QT=QT: e.scalar_tensor_tensor(out=QT[:, c, :], in0=ps, scalar=gq8[:, 0:1], in1=sd, op0=ALU.mult, op1=ALU.mult),
                                         reads=[psB, sdB, vB], writes=[QTB])
                                else:
                                    S.op("dve", lambda e, c=c, ps=ps, sd=sd, G=G: e.scalar_tensor_tensor(out=KT[:, c, G * 512:(G + 1) * 512], in0=ps, scalar=vecs[:, V_GK:V_GK + 1], in1=sd, op0=ALU.mult, op1=ALU.mult),
                                         reads=[psB, sdB, vB], writes=[KTB])
                        S.op("sp", lambda e, G=G, QT=QT: e.dma_start(out=qT_d[:, :, G * 512:(G + 1) * 512], in_=QT), reads=[QTB], dma=True)
                        for tt in range(4):
                            ps, psB = a_r.items[tt % 4]
                            for kc in range(8):
                                S.op("pe", lambda e, ps=ps, kc=kc, tt=tt, xnT=xnT: e.matmul(ps, lhsT=xnT[:, kc, tt * 128:(tt + 1) * 128], rhs=wqkv[:, kc, 1024:1536], start=(kc == 0), stop=(kc == 7)),
                                     reads=[wB, xnTB], writes=[psB])
                            if tt % 2 == 0:
                                S.op("act", lambda e, ps=ps, tt=tt, G=G: e.copy(out=VV[:, G * 4 + tt, :], in_=ps), reads=[psB], writes=[VB])
                            else:
                                S.op("dve", lambda e, ps=ps, tt=tt, G=G: e.tensor_copy(out=VV[:, G * 4 + tt, :], in_=ps), reads=[psB], writes=[VB])
                    S.flush()

                with contextlib.ExitStack() as pb_:
                    masks = sb(pb_, "masks", [128, 4, 512], BF16)
                    QT_t = sb(pb_, "QTb", [128, 2, 4, 512], BF16)
                    e_t = sb(pb_, "ee", [128, 3, 2, 512], BF16)
                    lp_t = sb(pb_, "lp", [128, 3, 2, 512], BF16)
                    sr_t = sb(pb_, "sr", [128, 4, 2, 512], BF16)
                    p_t = sb(pb_, "pp", [128, 2, 2, 512], BF16)
                    at_t = sb(pb_, "at", [128, 3, 2, 512], BF16)
                    ot_t = sb(pb_, "ot", [128, 4, 512], BF16)
                    mB = S.buf()
                    S.op("sp", lambda e: e.dma_start(out=masks[:, :, :], in_=masks_d.rearrange("p (j q) -> p j q", j=4)), writes=[mB], dma=True)
                    QT_r = Ring(S, [QT_t[:, i, :, :] for i in range(2)])
                    e_r = Ring(S, [e_t[:, i, :, :] for i in range(3)])
                    lp_r = Ring(S, [lp_t[:, i, :, :] for i in range(3)])
                    sr_r = Ring(S, [sr_t[:, i, :, :] for i in range(4)])
                    p_r = Ring(S, [p_t[:, i, :, :] for i in range(2)])
                    at_r = Ring(S, [at_t[:, i, :, :] for i in range(3)])
                    ot_r = Ring(S, [ot_t[0:64, i, :] for i in range(4)])
                    zB = [[S.buf(), S.buf()], [S.buf(), S.buf()]]
                    zP = [psum[0], psum[3]]
                    wBk = [S.buf(), S.buf()]
                    wP = psum[1]
                    oBk = [S.buf(), S.buf()]
                    oP = [psum[2][0:64, 0:512], psum[2][0:64, 512:1024]]
                    zi = [0]

                    def loadq(G):
                        QT, QTB = QT_r.next()
                        S.op("sp", lambda e: e.dma_start(out=QT, in_=qT_d[:, :, G * 512:(G + 1) * 512]), writes=[QTB], dma=True)
                        return QT, QTB

                    qnext = loadq(0)
                    for G in range(NG):
                        QT, QTB = qnext
                        if G + 1 < NG:
                            qnext = loadq(G + 1)
                        jmax = 4 * G + 3
                        units = [(hp, j) for hp in range(4) for j in range(jmax, -1, -1)]
                        NU = len(units)
                        st_ = [dict() for _ in range(NU)]
                        cur_sr = {}

                        def stage1a(u, G=G, QT=QT, QTB=QTB, units=units, st_=st_):
                            hp, j = units[u]
                            d = st_[u]
                            k = zi[0] % 2
                            zi[0] += 1
                            z = zP[k]
                            for hh in range(2):
                                kt = KT[hh * 64:(hh + 1) * 64, hp, j * 128:(j + 1) * 128]
                                qs = QT[hh * 64:(hh + 1) * 64, hp, :]
                                zz = z[:, hh * 512:(hh + 1) * 512]
                                S.op("pe", lambda e, zz=zz, kt=kt, qs=qs: e.matmul(zz, lhsT=kt, rhs=qs, start=True, stop=True), reads=[QTB], writes=[zB[k][hh]])
                            ee, eeB = e_r.next()
                            z3 = z[:, :].rearrange("p (a b) -> p a b", a=2)
                            S.op("act", lambda e: e.activation(out=ee, in_=z3, func=AF.Exp), reads=[zB[k][0], zB[k][1]], writes=[eeB])
                            if j >= 4 * G:
                                mk = masks[:, j - 4 * G, :]
                                for hh in range(2):
                                    S.op("dve", lambda e, hh=hh: e.tensor_tensor(out=ee[:, hh, :], in0=ee[:, hh, :], in1=mk, op=ALU.mult), reads=[mB], writes=[eeB])
                            d["ee"], d["eeB"] = ee, eeB

                        def stage1b(u, units=units, st_=st_, cur_sr=cur_sr, jmax=jmax):
                            hp, j = units[u]
                            d = st_[u]
                            ee, eeB = d["ee"], d["eeB"]
                            lp, lpB = lp_r.next()
                            S.op("act", lambda e: e.activation(out=lp, in_=ee, func=AF.Ln, bias=oneT[:, 0:1]), reads=[eeB], writes=[lpB])
                            d["lp"], d["lpB"] = lp, lpB
                            d["sr_in"] = cur_sr.get(hp)
                            if j > 0:
                                sr, srB = sr_r.next()
                                if j == jmax:
                                    S.op("dve", lambda e: e.tensor_copy(out=sr, in_=lp), reads=[lpB], writes=[srB])
                                else:
                                    psr, psrB = cur_sr[hp]
                                    S.op("dve", lambda e: e.tensor_tensor(out=sr, in0=psr, in1=lp, op=ALU.add), reads=[lpB, psrB], writes=[srB])
                                cur_sr[hp] = (sr, srB)

                        def stage2(u, units=units, st_=st_, jmax=jmax):
                            hp, j = units[u]
                            d = st_[u]
                            lp, lpB = d["lp"], d["lpB"]
                            ee, eeB = d["ee"], d["eeB"]
                            last = (j == jmax)
                            for hh in range(2):
                                ww = wP[:, hh * 512:(hh + 1) * 512]
                                S.op("pe", lambda e, ww=ww, hh=hh: e.matmul(ww, lhsT=negtri, rhs=lp[:, hh, :], start=True, stop=last), reads=[lpB, cstB], writes=[wBk[hh]])
                                if not last:
                                    psr, psrB = d["sr_in"]
                                    S.op("pe", lambda e, ww=ww, hh=hh, psr=psr: e.matmul(ww, lhsT=negones, rhs=psr[:, hh, :], start=False, stop=True), reads=[psrB, cstB], writes=[wBk[hh]])
                            pp_, ppB_ = p_r.next()
                            w3 = wP[:, :].rearrange("p (a b) -> p a b", a=2)
                            S.op("act", lambda e: e.activation(out=pp_, in_=w3, func=AF.Exp), reads=[wBk[0], wBk[1]], writes=[ppB_])
                            at, atB = at_r.next()
                            S.op("dve", lambda e: e.tensor_tensor(out=at, in0=pp_, in1=ee, op=ALU.mult), reads=[ppB_, eeB], writes=[atB])
                            d["at"], d["atB"] = at, atB

                        def stage3(u, G=G, units=units, st_=st_, jmax=jmax):
                            hp, j = units[u]
                            d = st_[u]
                            at, atB = d["at"], d["atB"]
                            first = (j == jmax)
                            for hh in range(2):
                                h = 2 * hp + hh
                                o = oP[hh]
                                S.op("pe", lambda e, o=o, h=h, hh=hh: e.matmul(o, lhsT=VV[:, j, h * 64:(h + 1) * 64], rhs=at[:, hh, :], start=first, stop=(j == 0)),
                                     reads=[atB], writes=[oBk[hh]])
                            if j == 0:
                                for hh in range(2):
                                    ot, otB = ot_r.next()
                                    o = oP[hh]
                                    dst = oT_d[hh * 64:(hh + 1) * 64, hp, G * 512:(G + 1) * 512]
                                    if hh == 0:
                                        S.op("act", lambda e, ot=ot, o=o: e.copy(out=ot, in_=o), reads=[oBk[hh]], writes=[otB])
                                    else:
                                        S.op("dve", lambda e, ot=ot, o=o: e.tensor_copy(out=ot, in_=o), reads=[oBk[hh]], writes=[otB])
                                    S.op("sp", lambda e, dst=dst, ot=ot: e.dma_start(out=dst, in_=ot), reads=[otB], dma=True)
                            st_[u] = None

                        SK2, SK3 = 2, 4
                        for i in range(NU + SK3):
                            if i < NU:
                                stage1a(i)
                            if 0 <= i - SK2 < NU:
                                stage2(i - SK2)
                            if i < NU:
                                stage1b(i)
                            if 0 <= i - SK3 < NU:
                                stage3(i - SK3)
                    S.flush()

            with contextlib.ExitStack() as ph:
                w_u = sb(ph, "w_u", [128, 8, 1024], BF16)
                w_p = sb(ph, "w_p", [128, 8, 512], BF16)
                pw = sb(ph, "pw", [128, 4, 128], BF16)
                diag_t = sb(ph, "diag", [128, 4, 31, 128], BF16)
                wB = S.buf()
                with contextlib.ExitStack() as st_es:
                    stg = sb(st_es, "stg", [128, 6, 1024], F32)
                    sring = Ring(S, [stg[:, i, :] for i in range(6)])
                    load_w(sring, w_u, wB, w_in_d, l * 1024, 8, 1536, 1024, scale_col=V_GMIX)
                    load_w(sring, w_p, wB, w_in_d, l * 1024, 8, 2560, 512, scale_col=V_GMIX)
                    load_w(sring, pw, wB, pw_d, l * 512, 4, 0, 128)
                    for c in range(4):
                        for k in range(31):
                            eng = "dve" if (k % 2 == 0) else "pool"
                            S.op(eng, lambda e, c=c, k=k: e.tensor_scalar(out=diag_t[:, c, k, :], in0=cst[:, 0:128], scalar1=vecs[:, V_DW + c * 31 + k: V_DW + c * 31 + k + 1], scalar2=None, op0=ALU.mult),
                                 reads=[vB, cstB], writes=[wB])
                    S.flush()
                xnT_t = sb(ph, "xnT", [128, 2, 8, 512], BF16)
                cb_t = sb(ph, "cb", [128, 2, 4, 542], BF16)
                sig_t = sb(ph, "sig", [128, 2, 512], F32)
                xc_t = sb(ph, "xc", [128, 4, 512], F32)
                xbf_t = sb(ph, "xbf", [128, 4, 512], BF16)
                xsq_t = sb(ph, "xsq", [128, 4, 512], BF16)
                ln_t = sb(ph, "ln", [128, 4, 512], F32)
                tt_t = sb(ph, "tt", [128, 2, 512], F32)
                hs_t = sb(ph, "hs", [128, 2, 4, 512], BF16)
                pb_t = sb(ph, "pb", [128, 2, 4, 527], F32)
                pa_t = sb(ph, "pa", [128, 2, 527], F32)
                yp_t = sb(ph, "yp", [128, 4, 512], BF16)
                zp_t = sb(ph, "zp", [128, 2, 4, 512], BF16)
                icnt = sb(ph, "icnt", [128, 4, 16], F32)
                t16 = sb(ph, "t16", [128, 16], F32)
                iB = S.buf()
                S.op("sp", lambda e: e.dma_start(out=icnt[:, :, :], in_=icnt_d.rearrange("p (g t) -> p g t", g=4)), writes=[iB], dma=True)
                br = bank_ring()
                xnT_r = Ring(S, [xnT_t[:, i, :, :] for i in range(2)])
                cb_r = Ring(S, [cb_t[:, i, :, :] for i in range(2)])
                sig_r = Ring(S, [sig_t[:, i, :] for i in range(2)])
                tt_r = Ring(S, [tt_t[:, i, :] for i in range(2)])
                hs_r = Ring(S, [hs_t[:, i, :, :] for i in range(2)])
                pb_r = Ring(S, [pb_t[:, i, :, :] for i in range(2)])
                pa_r = Ring(S, [pa_t[:, i, :] for i in range(2)])
                zp_r = Ring(S, [zp_t[:, i, :, :] for i in range(2)])
                lnB, t16B = S.buf(), S.buf()
                xcB = [S.buf() for _ in range(4)]
                xbfB = [S.buf() for _ in range(4)]
                xsqB = [S.buf() for _ in range(4)]
                ypB = [S.buf() for _ in range(4)]
                cb_prev = None
                pb_prev = None

                def ldxT(G):
                    xT, xTB = xnT_r.next()
                    S.op("sp", lambda e: e.dma_start(out=xT, in_=xnT_d[:, :, G * 512:(G + 1) * 512]), writes=[xTB], dma=True)
                    return xT, xTB

                nxt = ldxT(0)
                for G in range(NG):
                    xT, xTB = nxt
                    if G + 1 < NG:
                        nxt = ldxT(G + 1)
                    cb, _ = cb_r.next()
                    cbB = [S.buf() for _ in range(4)]
                    pb, _ = pb_r.next()
                    pbB = [S.buf() for _ in range(4)]
                    zp, zpB = zp_r.next()

                    def Pj(c, G=G, cb=cb, cbB=cbB, xT=xT, xTB=xTB, cb_prev=cb_prev):
                        if G == 0:
                            S.op("pool", lambda e: e.memset(cb[:, c, 0:30], 0.0), writes=[cbB[c]])
                        else:
                            pcb, pcbB = cb_prev
                            S.op("pool", lambda e: e.tensor_copy(out=cb[:, c, 0:30], in_=pcb[:, c, 512:542]), reads=[pcbB[c]], writes=[cbB[c]])
                        psa, psaB = br.next()
                        psg, psgB = br.next()
                        for kc in range(8):
                            S.op("pe", lambda e, kc=kc: e.matmul(psa, lhsT=w_u[:, kc, c * 128:(c + 1) * 128], rhs=xT[:, kc, :], start=(kc == 0), stop=(kc == 7)),
                                 reads=[wB, xTB], writes=[psaB])
                        for kc in range(8):
                            S.op("pe", lambda e, kc=kc: e.matmul(psg, lhsT=w_u[:, kc, 512 + c * 128:512 + (c + 1) * 128], rhs=xT[:, kc, :], start=(kc == 0), stop=(kc == 7)),
                                 reads=[wB, xTB], writes=[psgB])
                        sg, sgB = sig_r.next()
                        S.op("act", lambda e: e.activation(out=sg, in_=psg, func=AF.Sigmoid), reads=[psgB], writes=[sgB])
                        S.op("dve", lambda e: e.tensor_tensor(out=cb[:, c, 30:542], in0=psa, in1=sg, op=ALU.mult), reads=[psaB, sgB], writes=[cbB[c]])

                    def Cv(c, cb=cb, cbB=cbB):
                        psc, pscB = br.next()
                        for k in range(31):
                            S.op("pe", lambda e, k=k: e.matmul(psc, lhsT=diag_t[:, c, k, :], rhs=cb[:, c, k:k + 512], start=(k == 0), stop=(k == 30)),
                                 reads=[wB, cbB[c]], writes=[pscB])
                        bcol = vecs[:, V_DWB + c:V_DWB + c + 1]
                        S.op("act", lambda e: e.activation(out=xc_t[:, c, :], in_=psc, func=AF.Identity, bias=bcol), reads=[pscB, vB], writes=[xcB[c]])
                        S.op("act", lambda e: e.activation(out=xsq_t[:, c, :], in_=psc, func=AF.Square, bias=bcol), reads=[pscB, vB], writes=[xsqB[c]])
                        S.op("dve", lambda e: e.tensor_copy(out=xbf_t[:, c, :], in_=xc_t[:, c, :]), reads=[xcB[c]], writes=[xbfB[c]])

                    def Pp(g, G=G, pb=pb, pbB=pbB, xT=xT, xTB=xTB, pb_prev=pb_prev):
                        w = 2 << g
                        if G == 0:
                            S.op("pool", lambda e: e.memset(pb[:, g, 0:15], 0.0), writes=[pbB[g]])
                        else:
                            ppb, ppbB = pb_prev
                            S.op("pool", lambda e: e.tensor_copy(out=pb[:, g, 0:15], in_=ppb[:, g, 512:527]), reads=[ppbB[g]], writes=[pbB[g]])
                        psp, pspB = br.next()
                        for kc in range(8):
                            S.op("pe", lambda e, kc=kc: e.matmul(psp, lhsT=w_p[:, kc, g * 128:(g + 1) * 128], rhs=xT[:, kc, :], start=(kc == 0), stop=(kc == 7)),
                                 reads=[wB, xTB], writes=[pspB])
                        S.op("act", lambda e: e.copy(out=pb[:, g, 15:527], in_=psp), reads=[pspB], writes=[pbB[g]])
                        cur = pb[:, g, :]
                        curB = pbB[g]
                        m = 1
                        while m < w:
                            nx, nxB = pa_r.next()
                            S.op("dve", lambda e, nx=nx, cur=cur, m=m: e.tensor_tensor(out=nx[:, m:527], in0=cur[:, m:527], in1=cur[:, 0:527 - m], op=ALU.add),
                                 reads=[curB], writes=[nxB])
                            cur, curB = nx, nxB
                            m *= 2
                        S.op("dve", lambda e, cur=cur: e.scalar_tensor_tensor(out=yp_t[:, g, :], in0=cur[:, 15:527], scalar=1.0 / w, in1=pb[:, g, 15:527], op0=ALU.mult, op1=ALU.subtract),
                             reads=[curB, pbB[g]], writes=[ypB[g]])
                        if G == 0:
                            S.op("dve", lambda e, cur=cur: e.tensor_tensor(out=t16[:, :], in0=cur[:, 15:31], in1=icnt[:, g, :], op=ALU.mult), reads=[curB, iB], writes=[t16B])
                            S.op("dve", lambda e: e.tensor_tensor(out=yp_t[:, g, 0:16], in0=t16[:, :], in1=pb[:, g, 15:31], op=ALU.subtract), reads=[t16B, pbB[g]], writes=[ypB[g]])

                    def Pw(g, zp=zp, zpB=zpB):
                        psq, psqB = br.next()
                        S.op("pe", lambda e: e.matmul(psq, lhsT=pw[:, g, :], rhs=yp_t[:, g, :], start=True, stop=True), reads=[wB, ypB[g]], writes=[psqB])
                        S.op("act", lambda e: e.activation(out=zp[:, g, :], in_=psq, func=AF.Copy, scale=vecs[:, V_PS + g:V_PS + g + 1]), reads=[psqB, vB], writes=[zpB])

                    Pj(0)
                    Pj(1)
                    Cv(0)
                    Pp(0)
                    Pj(2)
                    Cv(1)
                    Pp(1)
                    Pj(3)
                    Cv(2)
                    Pp(2)
                    Cv(3)
                    Pp(3)
                    cb_prev = (cb, cbB)
                    s1, s1B = br.next()
                    s2, s2B = br.next()
                    for c in range(4):
                        S.op("pe", lambda e, s1=s1, c=c: e.matmul(s1, lhsT=ones, rhs=xbf_t[:, c, :], start=(c == 0), stop=(c == 3)), reads=[xbfB[c], cstB], writes=[s1B])
                    for c in range(4):
                        S.op("pe", lambda e, s2=s2, c=c: e.matmul(s2, lhsT=ones, rhs=xsq_t[:, c, :], start=(c == 0), stop=(c == 3)), reads=[xsqB[c], cstB], writes=[s2B])
                    for g in range(4):
                        Pw(g)
                    S.op("sp", lambda e, zp=zp, G=G: e.dma_start(out=zpT_d[:, :, G * 512:(G + 1) * 512], in_=zp), reads=[zpB], dma=True)
                    mean, msq, var = ln_t[:, 0, :], ln_t[:, 1, :], ln_t[:, 2, :]
                    S.op("act", lambda e, s1=s1: e.activation(out=mean, in_=s1, func=AF.Copy, scale=1.0 / 512), reads=[s1B], writes=[lnB])
                    S.op("dve", lambda e: e.tensor_tensor(out=msq, in0=mean, in1=mean, op=ALU.mult), reads=[lnB], writes=[lnB])
                    S.op("dve", lambda e, s2=s2: e.scalar_tensor_tensor(out=var, in0=s2, scalar=1.0 / 512, in1=msq, op0=ALU.mult, op1=ALU.subtract), reads=[s2B, lnB], writes=[lnB])
                    S.op("act", lambda e: e.activation(out=var, in_=var, func=AF.Ln, bias=epsT[:, 0:1]), reads=[lnB], writes=[lnB])
                    S.op("act", lambda e: e.activation(out=var, in_=var, func=AF.Exp, scale=-0.5), reads=[lnB], writes=[lnB])
                    hs, hsB = hs_r.next()
                    for c in range(4):
                        t1, t1B = tt_r.next()
                        S.op("dve", lambda e, t1=t1, c=c: e.tensor_tensor(out=t1, in0=xc_t[:, c, :], in1=mean, op=ALU.subtract), reads=[xcB[c], lnB], writes=[t1B])
                        S.op("pool", lambda e, t1=t1: e.tensor_tensor(out=t1, in0=t1, in1=var, op=ALU.mult), reads=[lnB, t1B], writes=[t1B])
                        S.op("act", lambda e, t1=t1, c=c, hs=hs: e.activation(out=hs[:, c, :], in_=t1, func=AF.Silu, scale=vecs[:, V_LNG + c:V_LNG + c + 1], bias=vecs[:, V_LNB + c:V_LNB + c + 1]),
                             reads=[t1B, vB], writes=[hsB])
                    S.op("sp", lambda e, hs=hs, G=G: e.dma_start(out=hsT_d[:, :, G * 512:(G + 1) * 512], in_=hs), reads=[hsB], dma=True)
                    pb_prev = (pb, pbB)
                S.flush()

            with contextlib.ExitStack() as ph:
                w_g = sb(ph, "w_g", [128, 8, 3072], BF16)
                w_br = sb(ph, "w_br", [128, 3, 4, 1024], BF16)
                w_o = sb(ph, "w_o", [128, 8, 1024], BF16)
                wB = S.buf()
                with contextlib.ExitStack() as st_es:
                    stg = sb(st_es, "stg", [128, 6, 1024], F32)
                    sring = Ring(S, [stg[:, i, :] for i in range(6)])
                    load_w(sring, w_g, wB, w_in_d, l * 1024, 8, 3072, 3072, scale_col=V_GMIX)
                    load_w(sring, w_br[:, 0, :, :], wB, w_co_d, l * 512, 4, 0, 1024)
                    load_w(sring, w_br[:, 1, :, :], wB, w_ao_d, l * 512, 4, 0, 1024)
                    load_w(sring, w_br[:, 2, :, :], wB, w_po_d, l * 512, 4, 0, 1024)
                    load_w(sring, w_o, wB, w_o_d, l * 1024, 8, 0, 1024)
                    S.flush()
                xnT_t = sb(ph, "xnT", [128, 2, 8, 512], BF16)
                src_t = sb(ph, "srcs", [128, 2, 3, 4, 512], BF16)
                hin_t = sb(ph, "hin", [128, 2, 4, 1024], F32)
                mg_t = sb(ph, "mg", [128, 8, 512], F32)
                mgb_t = sb(ph, "mgb", [128, 8, 512], BF16)
                gate_t = sb(ph, "gate", [128, 2, 512], F32)
                tm_t = sb(ph, "tm", [128, 2, 512], F32)
                ho_t = sb(ph, "ho", [128, 2, 1024], F32)
                br = bank_ring()
                xnT_r = Ring(S, [xnT_t[:, i, :, :] for i in range(2)])
                src_r = Ring(S, [src_t[:, i, :, :, :] for i in range(2)])
                gate_r = Ring(S, [gate_t[:, i, :] for i in range(2)])
                tm_r = Ring(S, [tm_t[:, i, :] for i in range(2)])
                ho_r = Ring(S, [ho_t[:, i, :] for i in range(2)])
                hin_r = Ring(S, [hin_t[:, i, :, :] for i in range(2)])
                mgB = [S.buf() for _ in range(8)]
                mgbB = S.buf()
                srcs_d = [hsT_d, oT_d, zpT_d]
                def ldall(G):
                    xT, xTB = xnT_r.next()
                    S.op("sp", lambda e: e.dma_start(out=xT, in_=xnT_d[:, :, G * 512:(G + 1) * 512]), writes=[xTB], dma=True)
                    sr, srB = src_r.next()
                    for b3 in range(3):
                        S.op("sp", lambda e, b3=b3: e.dma_start(out=sr[:, b3, :, :], in_=srcs_d[b3][:, :, G * 512:(G + 1) * 512]), writes=[srB], dma=True)
                    hin, hinB = hin_r.next()
                    for tt in range(4):
                        r0 = G * 512 + tt * 128
                        S.op("sp", lambda e, tt=tt, r0=r0: e.dma_start(out=hin[:, tt, :], in_=src_d[r0:r0 + 128, :]), writes=[hinB], dma=True)
                    return xT, xTB, sr, srB, hin, hinB

                nxt = ldall(0)
                for G in range(NG):
                    xT, xTB, sr, srB, hin, hinB = nxt
                    if G + 1 < NG:
                        nxt = ldall(G + 1)
                    for dc in range(8):
                        for b3 in range(3):
                            psg, psgB = br.next()
                            col = b3 * 1024 + dc * 128
                            for kc in range(8):
                                S.op("pe", lambda e, psg=psg, kc=kc, col=col, xT=xT: e.matmul(psg, lhsT=w_g[:, kc, col:col + 128], rhs=xT[:, kc, :], start=(kc == 0), stop=(kc == 7)),
                                     reads=[wB, xTB], writes=[psgB])
                            gt, gtB = gate_r.next()
                            gcol = vecs[:, V_GB + b3 * 8 + dc: V_GB + b3 * 8 + dc + 1]
                            S.op("act", lambda e, gt=gt, psg=psg, gcol=gcol: e.activation(out=gt, in_=psg, func=AF.Sigmoid, bias=gcol), reads=[psgB, vB], writes=[gtB])
                            psy, psyB = br.next()
                            for kc in range(4):
                                S.op("pe", lambda e, psy=psy, kc=kc, b3=b3, dc=dc, sr=sr: e.matmul(psy, lhsT=w_br[:, b3, kc, dc * 128:(dc + 1) * 128], rhs=sr[:, b3, kc, :], start=(kc == 0), stop=(kc == 3)),
                                     reads=[wB, srB], writes=[psyB])
                            if b3 == 0:
                                S.op("dve", lambda e, dc=dc, psy=psy, gt=gt: e.tensor_tensor(out=mg_t[:, dc, :], in0=psy, in1=gt, op=ALU.mult), reads=[psyB, gtB], writes=[mgB[dc]])
                            else:
                                tm, tmB = tm_r.next()
                                S.op("dve", lambda e, tm=tm, psy=psy, gt=gt: e.tensor_tensor(out=tm, in0=psy, in1=gt, op=ALU.mult), reads=[psyB, gtB], writes=[tmB])
                                if b3 == 1:
                                    S.op("pool", lambda e, dc=dc, tm=tm: e.tensor_tensor(out=mg_t[:, dc, :], in0=mg_t[:, dc, :], in1=tm, op=ALU.add), reads=[tmB], writes=[mgB[dc]])
                                else:
                                    S.op("pool", lambda e, dc=dc, tm=tm: e.tensor_tensor(out=mgb_t[:, dc, :], in0=mg_t[:, dc, :], in1=tm, op=ALU.add), reads=[tmB, mgB[dc]], writes=[mgbB])
                    for tt in range(4):
                        ho, hoB = ho_r.next()
                        for dh in range(2):
                            pso, psoB = br.next()
                            for fc in range(8):
                                S.op("pe", lambda e, pso=pso, fc=fc, tt=tt, dh=dh: e.matmul(pso, lhsT=mgb_t[:, fc, tt * 128:(tt + 1) * 128], rhs=w_o[:, fc, dh * 512:(dh + 1) * 512], start=(fc == 0), stop=(fc == 7)),
                                     reads=[wB, mgbB], writes=[psoB])
                            S.op("dve", lambda e, ho=ho, pso=pso, tt=tt, dh=dh, hin=hin: e.tensor_tensor(out=ho[:, dh * 512:(dh + 1) * 512], in0=pso, in1=hin[:, tt, dh * 512:(dh + 1) * 512], op=ALU.add),
                                 reads=[psoB, hinB], writes=[hoB])
                        r0 = G * 512 + tt * 128
                        S.op("sp", lambda e, ho=ho, r0=r0: e.dma_start(out=hA_d[r0:r0 + 128, :], in_=ho), reads=[hoB], dma=True)
                S.flush()

            with contextlib.ExitStack() as ph:
                w1 = sb(ph, "w1", [128, 8, 4096], BF16)
                w2 = sb(ph, "w2", [128, 32, 1024], BF16)
                wB = S.buf()
                with contextlib.ExitStack() as st_es:
                    stg = sb(st_es, "stg", [128, 6, 1024], F32)
                    sring = Ring(S, [stg[:, i, :] for i in range(6)])
                    load_w(sring, w1, wB, w_m1_d, l * 1024, 8, 0, 4096, scale_col=V_GMLP)
                    load_w(sring, w2, wB, w_m2_d, l * 4096, 32, 0, 1024)
                    S.flush()
                HT = 256
                xin_t = sb(ph, "xin", [128, 4, 1024], F32)
                xn_t = sb(ph, "xn", [128, 2, 1024], F32)
                stat_t = sb(ph, "stat", [128, 2, 4], F32)
                xnT_t = sb(ph, "xnT", [128, 2, 8, HT], BF16)
                ff_t = sb(ph, "ff", [128, 2, 32, HT], BF16)
                rl_t = sb(ph, "rl", [128, 2, HT], F32)
                ho_t = sb(ph, "ho", [128, 2, 1024], F32)
                xin_r = Ring(S, [xin_t[:, i, :] for i in range(4)])
                xn_r = Ring(S, [xn_t[:, i, :] for i in range(2)])
                stat_r = Ring(S, [stat_t[:, i, :] for i in range(2)])
                xnT_r = Ring(S, [xnT_t[:, i, :, :] for i in range(2)])
                rl_r = Ring(S, [rl_t[:, i, :] for i in range(2)])
                ho_r = Ring(S, [ho_t[:, i, :] for i in range(2)])
                junkB = S.buf()
                ff_r = Ring(S, [ff_t[:, i, :, :] for i in range(2)])
                for it in ff_r.items:
                    pass
                ff_r.items = [(ap, [S.buf() for _ in range(32)]) for (ap, _) in ff_r.items]
                aps = []
                for p in psum[0:3]:
                    aps.append(p[:, 0:512])
                    aps.append(p[:, 512:1024])
                br = Ring(S, aps)
                ppB = S.buf()
                def ldh(H):
                    res = []
                    for tt in range(HT // 128):
                        r0 = H * HT + tt * 128
                        xin, xinB = xin_r.next()
                        S.op("sp", lambda e, xin=xin, r0=r0: e.dma_start(out=xin, in_=hA_d[r0:r0 + 128, :]), writes=[xinB], dma=True)
                        res.append((xin, xinB))
                    return res

                NH = T // HT

                def prep_norm(H, curx):
                    xT, xTB = xnT_r.next()
                    tiles = []
                    for tt in range(HT // 128):
                        xin, xinB = curx[tt]
                        xn, xnB = xn_r.next()
                        stt, sttB = stat_r.next()
                        rms_tile(None, xin, xinB, xn, xnB, stt, sttB, xn, xnB)
                        tiles.append((xin, xinB, xn, xnB))
                    return xT, xTB, tiles

                def prep_T(pr):
                    xT, xTB, tiles = pr
                    for tt, (xin, xinB, xn, xnB) in enumerate(tiles):
                        transpose_tile(xn, xnB, psum[3], ppB, xT, xTB, tt * 128, ("act", "dve"))

                ld_cur = ldh(0)
                pr_cur = prep_norm(0, ld_cur)
                prep_T(pr_cur)
                ld_nxt = ldh(1) if NH > 1 else None
                for H in range(NH):
                    xT, xTB, tiles = pr_cur
                    ff, ffB = ff_r.next()
                    if H + 1 < NH:
                        pr_nxt = prep_norm(H + 1, ld_nxt)
                    for fc in range(32):
                        ps, psB = br.next()
                        for kc in range(8):
                            S.op("pe", lambda e, ps=ps, kc=kc, fc=fc, xT=xT: e.matmul(ps[:, 0:HT], lhsT=w1[:, kc, fc * 128:(fc + 1) * 128], rhs=xT[:, kc, :], start=(kc == 0), stop=(kc == 7)),
                                 reads=[wB, xTB], writes=[psB])
                        rl, rlB = rl_r.next()
                        S.op("act", lambda e, rl=rl, ps=ps: e.activation(out=rl, in_=ps[:, 0:HT], func=AF.Relu), reads=[psB], writes=[rlB])
                        eng = "pool" if fc % 2 == 0 else "dve"
                        S.op(eng, lambda e, rl=rl, fc=fc, ff=ff: e.tensor_tensor(out=ff[:, fc, :], in0=rl, in1=rl, op=ALU.mult), reads=[rlB], writes=[ffB[fc]])
                    if H + 1 < NH:
                        prep_T(pr_nxt)
                    for tt in range(HT // 128):
                        ho, hoB = ho_r.next()
                        xin, xinB = tiles[tt][0], tiles[tt][1]
                        for dh in range(2):
                            pso, psoB = br.next()
                            for fc in range(32):
                                S.op("pe", lambda e, pso=pso, fc=fc, tt=tt, dh=dh, ff=ff: e.matmul(pso, lhsT=ff[:, fc, tt * 128:(tt + 1) * 128], rhs=w2[:, fc, dh * 512:(dh + 1) * 512], start=(fc == 0), stop=(fc == 31)),
                                     reads=[wB, ffB[fc]], writes=[psoB])
                            S.op("dve", lambda e, ho=ho, pso=pso, xin=xin, dh=dh: e.tensor_tensor(out=ho[:, dh * 512:(dh + 1) * 512], in0=pso, in1=xin[:, dh * 512:(dh + 1) * 512], op=ALU.add),
                                 reads=[psoB, xinB], writes=[hoB])
                        r0 = H * HT + tt * 128
                        S.op("sp", lambda e, ho=ho, r0=r0: e.dma_start(out=out_d[r0:r0 + 128, :], in_=ho), reads=[hoB], dma=True)
                    if H + 1 < NH:
                        pr_cur = pr_nxt
                        ld_nxt = ldh(H + 2) if H + 2 < NH else None
                S.flush()
    return nc


def host_prep(inp, L):
    f = lambda a: np.ascontiguousarray(np.asarray(a, dtype=np.float32))
    vecs = np.zeros((L, 128, NVEC), np.float32)
    for l in range(L):
        vecs[l, :, V_GB:V_GB + 24] = f(inp["gate_b"])[l].reshape(24, 128).T
        vecs[l, :, V_DWB:V_DWB + 4] = f(inp["conv_dw_b"])[l].reshape(4, 128).T
        vecs[l, :, V_LNG:V_LNG + 4] = f(inp["conv_ln_g"])[l].reshape(4, 128).T
        vecs[l, :, V_LNB:V_LNB + 4] = f(inp["conv_ln_b"])[l].reshape(4, 128).T
        vecs[l, :, V_PS:V_PS + 4] = f(inp["pool_scale"])[l].reshape(4, 128).T
        vecs[l, :, V_GQ] = np.tile(f(inp["q_norm_g"])[l], 2)
        vecs[l, :, V_GK] = np.tile(f(inp["k_norm_g"])[l], 2)
        vecs[l, :, V_GMIX:V_GMIX + 8] = f(inp["mix_norm_g"])[l].reshape(8, 128).T
        vecs[l, :, V_GMLP:V_GMLP + 8] = f(inp["mlp_norm_g"])[l].reshape(8, 128).T
        dw = f(inp["conv_dw"])[l]
        vecs[l, :, V_DW:V_DW + 124] = dw.reshape(31, 4, 128).transpose(2, 1, 0).reshape(128, 124)
    ident = np.eye(128, dtype=np.float32)
    jj, kk = np.meshgrid(np.arange(128), np.arange(128), indexing="ij")
    negtri = -(jj >= kk).astype(np.float32)
    negones = -np.ones((128, 128), np.float32)
    ones = np.ones((128, 128), np.float32)
    blk = (jj // 64 == kk // 64).astype(np.float32)
    cst = np.concatenate([ident, negtri, negones, ones, blk], axis=1).astype(ml_dtypes.bfloat16)
    p = np.arange(128)[:, None]
    q = np.arange(512)[None, :]
    masks = np.concatenate([(q > p + j * 128).astype(np.float32) for j in range(4)], axis=1).astype(ml_dtypes.bfloat16)
    icnt = np.zeros((128, 4, 16), np.float32)
    for g in range(4):
        w = 2 << g
        icnt[:, g, :] = 1.0 / np.minimum(np.arange(16) + 1, w).astype(np.float32)
    com = {
        "w_in": f(inp["w_in"]).reshape(L * 1024, 6144),
        "w_conv_out": f(inp["w_conv_out"]).reshape(L * 512, 1024),
        "w_att_out": f(inp["w_att_out"]).reshape(L * 512, 1024),
        "w_pool_out": f(inp["w_pool_out"]).reshape(L * 512, 1024),
        "pool_w": f(inp["pool_w"]).reshape(L * 512, 128),
        "w_o": f(inp["w_o"]).reshape(L * 1024, 1024),
        "w_mlp_in": f(inp["w_mlp_in"]).reshape(L * 1024, 4096),
        "w_mlp_out": f(inp["w_mlp_out"]).reshape(L * 4096, 1024),
        "vecs": vecs.reshape(L * 128, NVEC),
        "cst": cst,
        "identf": ident,
        "masks": masks,
        "icnt": icnt.reshape(128, 64),
    }
    return com


_NC_CACHE = {}


def run(inp, seqs, T, L):
    key = (T, L)
    if key not in _NC_CACHE:
        _NC_CACHE[key] = build(T, L)
    nc = _NC_CACHE[key]
    com = host_prep(inp, L)
    in_maps = []
    for c in range(8):
        m = dict(com)
        m["x"] = np.ascontiguousarray(seqs[c], dtype=np.float32)
        in_maps.append(m)
    res = run_bass_kernel_spmd(nc, in_maps, core_ids=list(range(8)))
    return [res.results[c]["y"] for c in range(8)]


def kernel(**inputs):
    x = np.asarray(inputs["x"], dtype=np.float32)
    B, T, D = x.shape
    L = np.asarray(inputs["w_in"]).shape[0]
    seqs = [x[c // 2] for c in range(8)]
    outs = run(inputs, seqs, T, L)
    return np.stack([outs[2 * b] for b in range(B)], axis=0).astype(np.float32)
```

```python
import contextlib
import numpy as np
import ml_dtypes
import concourse.bass as bass
import concourse.mybir as mybir
from concourse.bass_utils import run_bass_kernel_spmd

F32 = mybir.dt.float32
BF16 = mybir.dt.bfloat16
AF = mybir.ActivationFunctionType
ALU = mybir.AluOpType
AX = mybir.AxisListType

SAME_ENG_SYNC = True
EPS = 1e-6
NVEC = 24 + 4 + 4 + 4 + 4 + 1 + 1 + 8 + 8 + 124
V_GB, V_DWB, V_LNG, V_LNB, V_PS, V_GQ, V_GK, V_GMIX, V_GMLP, V_DW = 0, 24, 28, 32, 36, 40, 41, 42, 50, 58


class Buf:
    __slots__ = ("w", "r")

    def __init__(self):
        self.w = None
        self.r = {}


class Sched:
    BLK = dict(pe="tensor", act="scalar", dve="vector", pool="gpsimd", sp="sync")

    def __init__(self, nc, es, ndsem=28):
        self.nc = nc
        self.names = ["sp", "pe", "act", "dve", "pool"]
        self.sem = {e: es.enter_context(nc.semaphore("s_" + e)) for e in self.names}
        self.semval = {e: 0 for e in self.names}
        self.dsem = [es.enter_context(nc.semaphore("d%d" % i)) for i in range(ndsem)]
        self.dval = [0] * ndsem
        self.dlast = [None] * ndsem
        self.dnext = 0
        self.q = {e: [] for e in self.names}
        self.seen = {e: {} for e in self.names}
        self.bufs = []

    def buf(self):
        b = Buf()
        self.bufs.append(b)
        return b

    def op(self, eng, fn, reads=(), writes=(), dma=False):
        deps = set()
        for b in reads:
            if b.w is not None:
                deps.add(b.w)
        for b in writes:
            if b.w is not None:
                deps.add(b.w)
            deps.update(b.r.values())
        if dma:
            k = self.dnext
            self.dnext = (self.dnext + 1) % len(self.dsem)
            if self.dlast[k] is not None:
                deps.add(self.dlast[k])
            self.dval[k] += 16
            ev = ("d", k, self.dval[k])
            self.dlast[k] = ev
        else:
            ev = ("c", eng, len(self.q[eng]))
        self.q[eng].append(dict(fn=fn, deps=deps, ev=ev, dma=dma))
        key = (ev[0], ev[1])
        for b in reads:
            b.r[key] = ev
        for b in writes:
            b.w = ev
            b.r = {}
        return ev

    def flush(self):
        nc = self.nc
        obs = set()
        for e in self.names:
            for o in self.q[e]:
                nd = set()
                for d in o["deps"]:
                    if d[0] == "c" and d[1] == e and (e == "pe" or not SAME_ENG_SYNC):
                        continue
                    nd.add(d)
                    if d[0] == "c":
                        obs.add(d)
                o["deps"] = nd
        for e in self.names:
            for o in reversed(self.q[e]):
                if not o["dma"]:
                    obs.add(o["ev"])
                    break
        val = {}
        endval = {}
        for e in self.names:
            c = self.semval[e]
            for o in self.q[e]:
                if o["ev"] in obs:
                    c += 1
                    val[o["ev"]] = c
            endval[e] = c
        with nc.Block() as block:
            for e in self.names:
                def body(engine, e=e):
                    seen = self.seen[e]
                    for o in self.q[e]:
                        for d in sorted(o["deps"]):
                            if d[0] == "c":
                                key, v, sem = ("c", d[1]), val[d], self.sem[d[1]]
                            else:
                                key, v, sem = ("d", d[1]), d[2], self.dsem[d[1]]
                            if seen.get(key, 0) < v:
                                engine.wait_ge(sem, v)
                                seen[key] = v
                        ins = o["fn"](engine)
                        if o["dma"]:
                            ins.then_inc(self.dsem[o["ev"][1]], 16)
                        elif o["ev"] in val:
                            ins.then_inc(self.sem[e], 1)
                    for e2 in self.names:
                        if e2 != e and endval[e2] > seen.get(("c", e2), 0):
                            engine.wait_ge(self.sem[e2], endval[e2])
                            seen[("c", e2)] = endval[e2]
                    for k in range(len(self.dsem)):
                        if self.dval[k] > seen.get(("d", k), 0):
                            engine.wait_ge(self.dsem[k], self.dval[k])
                            seen[("d", k)] = self.dval[k]
                getattr(block, self.BLK[e])(body)
        self.semval = endval
        self.q = {e: [] for e in self.names}
        for b in self.bufs:
            b.w = None
            b.r = {}
        self.bufs = []
        self.dlast = [None] * len(self.dsem)


class Ring:
    def __init__(self, S, aps):
        self.items = [(ap, S.buf()) for ap in aps]
        self.i = 0

    def next(self):
        it = self.items[self.i]
        self.i = (self.i + 1) % len(self.items)
        return it


def build(T, L):
    NG = T // 512
    nc = bass.Bass("TRN2", target_bir_lowering=False)

    def din(name, shape, dt=F32):
        return nc.dram_tensor(name, shape, dt, kind="ExternalInput").ap()

    x_d = din("x", [T, 1024])
    w_in_d = din("w_in", [L * 1024, 6144])
    w_co_d = din("w_conv_out", [L * 512, 1024])
    w_ao_d = din("w_att_out", [L * 512, 1024])
    w_po_d = din("w_pool_out", [L * 512, 1024])
    pw_d = din("pool_w", [L * 512, 128])
    w_o_d = din("w_o", [L * 1024, 1024])
    w_m1_d = din("w_mlp_in", [L * 1024, 4096])
    w_m2_d = din("w_mlp_out", [L * 4096, 1024])
    vecs_d = din("vecs", [L * 128, NVEC])
    cst_d = din("cst", [128, 128 * 5], BF16)
    identf_d = din("identf", [128, 128])
    masks_d = din("masks", [128, 4 * 512], BF16)
    icnt_d = din("icnt", [128, 4 * 16])
    y_d = nc.dram_tensor("y", [T, 1024], F32, kind="ExternalOutput").ap()
    hA_d = nc.dram_tensor("hA", [T, 1024], F32).ap()
    hB_d = nc.dram_tensor("hB", [T, 1024], F32).ap()
    xnT_d = nc.dram_tensor("xnT", [128, 8 * T], BF16).ap().rearrange("p (c t) -> p c t", c=8)
    hsT_d = nc.dram_tensor("hsT", [128, 4 * T], BF16).ap().rearrange("p (c t) -> p c t", c=4)
    zpT_d = nc.dram_tensor("zpT", [128, 4 * T], BF16).ap().rearrange("p (c t) -> p c t", c=4)
    qT_d = nc.dram_tensor("qT", [128, 4 * T], BF16).ap().rearrange("p (c t) -> p c t", c=4)
    oT_d = nc.dram_tensor("oT", [128, 4 * T], BF16).ap().rearrange("p (c t) -> p c t", c=4)

    with contextlib.ExitStack() as es:
        S = Sched(nc, es)

        uid = [0]

        def sb(stack, name, shape, dt):
            uid[0] += 1
            return stack.enter_context(nc.sbuf_tensor("sb%d_%s" % (uid[0], name), shape, dt))

        cst = sb(es, "cst", [128, 5 * 128], BF16)
        identf = sb(es, "identf", [128, 128], F32)
        vecs = sb(es, "vecs", [128, NVEC], F32)
        epsT = sb(es, "epsT", [128, 1], F32)
        oneT = sb(es, "oneT", [128, 1], F32)
        gq8 = sb(es, "gq8", [128, 1], F32)
        psum = [es.enter_context(nc.psum_tensor("ps%d" % i, [128, 1024], F32)) for i in range(4)]
        negtri = cst[:, 128:256]
        negones = cst[:, 256:384]
        ones = cst[:, 384:512]
        blk = cst[:, 512:640]

        cstB = S.buf()
        S.op("sp", lambda e: e.dma_start(out=cst[:, :], in_=cst_d[:, :]), writes=[cstB], dma=True)
        S.op("sp", lambda e: e.dma_start(out=identf[:, :], in_=identf_d[:, :]), writes=[cstB], dma=True)
        S.op("dve", lambda e: e.memset(epsT[:, :], EPS), writes=[cstB])
        S.op("dve", lambda e: e.memset(oneT[:, :], 1.0), writes=[cstB])
        S.flush()

        def bank_ring():
            aps = []
            for p in psum:
                aps.append(p[:, 0:512])
                aps.append(p[:, 512:1024])
            return Ring(S, aps)

        cvt_rr = [0]

        def load_w(stage_ring, dst, dstB, src, rows0, kcs, col0, ncols, scale_col=None):
            for kc in range(kcs):
                for n0 in range(0, ncols, 1024):
                    n1 = min(ncols, n0 + 1024)
                    st, stB = stage_ring.next()
                    sap = src[rows0 + kc * 128: rows0 + (kc + 1) * 128, col0 + n0: col0 + n1]
                    S.op("sp", lambda e, st=st, sap=sap, n=n1 - n0: e.dma_start(out=st[:, 0:n], in_=sap),
                         writes=[stB], dma=True)
                    dap = dst[:, kc, n0:n1]
                    sin = st[:, 0:n1 - n0]
                    eng = ["dve", "act"][cvt_rr[0] % 2]
                    cvt_rr[0] += 1
                    if scale_col is None:
                        if eng == "act":
                            S.op("act", lambda e, dap=dap, sin=sin: e.copy(out=dap, in_=sin), reads=[stB], writes=[dstB])
                        else:
                            S.op(eng, lambda e, dap=dap, sin=sin: e.tensor_copy(out=dap, in_=sin), reads=[stB], writes=[dstB])
                    else:
                        sc = vecs[:, scale_col + kc: scale_col + kc + 1]
                        if eng == "act":
                            S.op("act", lambda e, dap=dap, sin=sin, sc=sc: e.activation(out=dap, in_=sin, func=AF.Copy, scale=sc),
                                 reads=[stB], writes=[dstB])
                        else:
                            S.op(eng, lambda e, dap=dap, sin=sin, sc=sc: e.tensor_scalar(out=dap, in0=sin, scalar1=sc, scalar2=None, op0=ALU.mult),
                                 reads=[stB], writes=[dstB])

        def rms_tile(src_rows, xin, xinB, junk, junkB, st, stB, xn, xnB):
            if src_rows is not None:
                S.op("sp", lambda e: e.dma_start(out=xin, in_=src_rows), writes=[xinB], dma=True)
            S.op("act", lambda e: e.activation(out=junk, in_=xin, func=AF.Square), reads=[xinB], writes=[junkB])
            S.op("dve", lambda e: e.tensor_reduce(out=st[:, 0:1], in_=junk, axis=AX.X, op=ALU.add), reads=[junkB], writes=[stB])
            S.op("act", lambda e: e.activation(out=st[:, 1:2], in_=st[:, 0:1], func=AF.Sqrt, scale=1.0 / 1024, bias=epsT[:, 0:1]),
                 reads=[stB], writes=[stB])
            S.op("dve", lambda e: e.reciprocal(out=st[:, 2:3], in_=st[:, 1:2]), reads=[stB], writes=[stB])
            S.op("dve", lambda e: e.tensor_scalar(out=xn, in0=xin, scalar1=st[:, 2:3], scalar2=None, op0=ALU.mult),
                 reads=[xinB, stB], writes=[xnB])

        def transpose_tile(xn, xnB, pp, ppB, dstT, dstTB, t0, evac):
            for c in range(8):
                S.op("pe", lambda e, c=c: e.transpose(out=pp[:, c * 128:(c + 1) * 128], in_=xn[:, c * 128:(c + 1) * 128], identity=identf[:, :]),
                     reads=[xnB, cstB], writes=[ppB])
            src = pp[:, :].rearrange("p (c t) -> p c t", c=8)
            for hh in range(2):
                d = dstT[:, hh * 4:(hh + 1) * 4, t0:t0 + 128]
                s_ = src[:, hh * 4:(hh + 1) * 4, :]
                if evac[hh] == "act":
                    S.op("act", lambda e, d=d, s_=s_: e.copy(out=d, in_=s_), reads=[ppB], writes=[dstTB])
                else:
                    S.op("dve", lambda e, d=d, s_=s_: e.tensor_copy(out=d, in_=s_), reads=[ppB], writes=[dstTB])

        for l in range(L):
            src_d = x_d if l == 0 else hB_d
            out_d = hB_d if l == L - 1 and False else (y_d if l == L - 1 else hB_d)
            vB = S.buf()
            S.op("sp", lambda e, l=l: e.dma_start(out=vecs[:, :], in_=vecs_d[l * 128:(l + 1) * 128, :]), writes=[vB], dma=True)
            S.op("dve", lambda e: e.tensor_scalar(out=gq8[:, :], in0=vecs[:, V_GQ:V_GQ + 1], scalar1=0.125, scalar2=None, op0=ALU.mult),
                 reads=[vB], writes=[vB])
            S.flush()

            with contextlib.ExitStack() as ph:
                KT = sb(ph, "KT", [128, 4, T], BF16)
                VV = sb(ph, "VV", [128, T // 128, 512], BF16)
                with contextlib.ExitStack() as pa:
                    wqkv = sb(pa, "wqkv", [128, 8, 1536], BF16)
                    wB = S.buf()
                    with contextlib.ExitStack() as st_es:
                        stg = sb(st_es, "stg", [128, 6, 1024], F32)
                        sring = Ring(S, [stg[:, i, :] for i in range(6)])
                        load_w(sring, wqkv, wB, w_in_d, l * 1024, 8, 0, 1536, scale_col=V_GMIX)
                        S.flush()
                    xin_t = sb(pa, "xin", [128, 2, 1024], F32)
                    xn_t = sb(pa, "xn", [128, 2, 1024], F32)
                    stat_t = sb(pa, "stat", [128, 2, 4], F32)
                    xnT_t = sb(pa, "xnT", [128, 2, 8, 512], BF16)
                    QT_t = sb(pa, "QT", [128, 2, 4, 512], BF16)
                    sq_t = sb(pa, "sq", [128, 2, 512], BF16)
                    sd_t = sb(pa, "sd", [128, 2, 512], F32)
                    xin_r = Ring(S, [xin_t[:, i, :] for i in range(2)])
                    xn_r = Ring(S, [xn_t[:, i, :] for i in range(2)])
                    stat_r = Ring(S, [stat_t[:, i, :] for i in range(2)])
                    xnT_r = Ring(S, [xnT_t[:, i, :, :] for i in range(2)])
                    QT_r = Ring(S, [QT_t[:, i, :, :] for i in range(2)])
                    sq_r = Ring(S, [sq_t[:, i, :] for i in range(2)])
                    sd_r = Ring(S, [sd_t[:, i, :] for i in range(2)])
                    a_r = bank_ring()
                    tp_r = Ring(S, [psum[3][:, :], psum[2][:, :]])
                    KTB = S.buf()
                    VB = S.buf()
                    def ldx(ti):
                        xin, xinB = xin_r.next()
                        S.op("sp", lambda e: e.dma_start(out=xin, in_=src_d[ti * 128:(ti + 1) * 128, :]), writes=[xinB], dma=True)
                        return xin, xinB

                    nxt = ldx(0)
                    for G in range(NG):
                        xnT, xnTB = xnT_r.next()
                        for tt in range(4):
                            xin, xinB = nxt
                            if G * 4 + tt + 1 < NG * 4:
                                nxt = ldx(G * 4 + tt + 1)
                            xn, xnB = xn_r.next()
                            stt, sttB = stat_r.next()
                            rms_tile(None, xin, xinB, xn, xnB, stt, sttB, xn, xnB)
                            pidx = 3 if (G * 4 + tt) % 2 == 0 else 2
                            pB0, pB1 = a_r.items[2 * pidx][1], a_r.items[2 * pidx + 1][1]
                            pp = psum[pidx]
                            for c in range(8):
                                S.op("pe", lambda e, c=c, xn=xn, pp=pp: e.transpose(out=pp[:, c * 128:(c + 1) * 128], in_=xn[:, c * 128:(c + 1) * 128], identity=identf[:, :]),
                                     reads=[xnB, cstB], writes=[pB0 if c < 4 else pB1])
                            srcp = pp[:, :].rearrange("p (c t) -> p c t", c=8)
                            S.op("act", lambda e, tt=tt, srcp=srcp, xnT=xnT: e.copy(out=xnT[:, 0:4, tt * 128:(tt + 1) * 128], in_=srcp[:, 0:4, :]),
                                 reads=[pB0], writes=[xnTB])
                            S.op("dve", lambda e, tt=tt, srcp=srcp, xnT=xnT: e.tensor_copy(out=xnT[:, 4:8, tt * 128:(tt + 1) * 128], in_=srcp[:, 4:8, :]),
                                 reads=[pB1], writes=[xnTB])
                        S.op("sp", lambda e, G=G, xnT=xnT: e.dma_start(out=xnT_d[:, :, G * 512:(G + 1) * 512], in_=xnT), reads=[xnTB], dma=True)
                        QT, QTB = QT_r.next()
                        for qk in range(2):
                            for c in range(4):
                                i0 = (qk * 4 + c) % 2
                                ps, psB = a_r.items[i0 * 2]
                                ps2, ps2B = a_r.items[i0 * 2 + 1]
                                col = qk * 512 + c * 128
                                for kc in range(8):
                                    S.op("pe", lambda e, ps=ps, kc=kc, col=col, xnT=xnT: e.matmul(ps, lhsT=wqkv[:, kc, col:col + 128], rhs=xnT[:, kc, :], start=(kc == 0), stop=(kc == 7)),
                                         reads=[wB, xnTB], writes=[psB])
                                sq, sqB = sq_r.next()
                                S.op("act", lambda e, sq=sq, ps=ps: e.activation(out=sq, in_=ps, func=AF.Square), reads=[psB], writes=[sqB])
                                S.op("pe", lambda e, ps2=ps2, sq=sq: e.matmul(ps2, lhsT=blk, rhs=sq, start=True, stop=True), reads=[sqB, cstB], writes=[ps2B])
                                sd, sdB = sd_r.next()
                                S.op("act", lambda e, sd=sd, ps2=ps2: e.activation(out=sd, in_=ps2, func=AF.Ln, scale=1.0 / 64, bias=epsT[:, 0:1]),
                                     reads=[ps2B], writes=[sdB])
                                S.op("act", lambda e, sd=sd: e.activation(out=sd, in_=sd, func=AF.Exp, scale=-0.5), reads=[sdB], writes=[sdB])
                                if qk == 0:
                                    S.op("dve", lambda e, c=c, ps=ps, sd=sd, QT=QT: e.scalar_tensor_tensor(out=QT[:, c, :], in0=ps, scalar=gq8[:, 0:1], in1=sd, op0=ALU.mult, op1=ALU.mult),
                                         reads=[psB, sdB, vB], writes=[QTB])
                                else:
                                    S.op("dve", lambda e, c=c, ps=ps, sd=sd, G=G: e.scalar_tensor_tensor(out=KT[:, c, G * 512:(G + 1) * 512], in0=ps, scalar=vecs[:, V_GK:V_GK + 1], in1=sd, op0=ALU.mult, op1=ALU.mult),
                                         reads=[psB, sdB, vB], writes=[KTB])
                        S.op("sp", lambda e, G=G, QT=QT: e.dma_start(out=qT_d[:, :, G * 512:(G + 1) * 512], in_=QT), reads=[QTB], dma=True)
                        for tt in range(4):
                            ps, psB = a_r.items[tt % 4]
                            for kc in range(8):
                                S.op("pe", lambda e, ps=ps, kc=kc, tt=tt, xnT=xnT: e.matmul(ps, lhsT=xnT[:, kc, tt * 128:(tt + 1) * 128], rhs=wqkv[:, kc, 1024:1536], start=(kc == 0), stop=(kc == 7)),
                                     reads=[wB, xnTB], writes=[psB])
                            if tt % 2 == 0:
                                S.op("act", lambda e, ps=ps, tt=tt, G=G: e.copy(out=VV[:, G * 4 + tt, :], in_=ps), reads=[psB], writes=[VB])
                            else:
                                S.op("dve", lambda e, ps=ps, tt=tt, G=G: e.tensor_copy(out=VV[:, G * 4 + tt, :], in_=ps), reads=[psB], writes=[VB])
                    S.flush()

                with contextlib.ExitStack() as pb_:
                    masks = sb(pb_, "masks", [128, 4, 512], BF16)
                    QT_t = sb(pb_, "QTb", [128, 2, 4, 512], BF16)
                    e_t = sb(pb_, "ee", [128, 3, 2, 512], BF16)
                    lp_t = sb(pb_, "lp", [128, 3, 2, 512], BF16)
                    sr_t = sb(pb_, "sr", [128, 4, 2, 512], BF16)
                    p_t = sb(pb_, "pp", [128, 2, 2, 512], BF16)
                    at_t = sb(pb_, "at", [128, 3, 2, 512], BF16)
                    ot_t = sb(pb_, "ot", [128, 4, 512], BF16)
                    mB = S.buf()
                    S.op("sp", lambda e: e.dma_start(out=masks[:, :, :], in_=masks_d.rearrange("p (j q) -> p j q", j=4)), writes=[mB], dma=True)
                    QT_r = Ring(S, [QT_t[:, i, :, :] for i in range(2)])
                    e_r = Ring(S, [e_t[:, i, :, :] for i in range(3)])
                    lp_r = Ring(S, [lp_t[:, i, :, :] for i in range(3)])
                    sr_r = Ring(S, [sr_t[:, i, :, :] for i in range(4)])
                    p_r = Ring(S, [p_t[:, i, :, :] for i in range(2)])
                    at_r = Ring(S, [at_t[:, i, :, :] for i in range(3)])
                    ot_r = Ring(S, [ot_t[0:64, i, :] for i in range(4)])
                    zB = [[S.buf(), S.buf()], [S.buf(), S.buf()]]
                    zP = [psum[0], psum[3]]
                    wBk = [S.buf(), S.buf()]
                    wP = psum[1]
                    oBk = [S.buf(), S.buf()]
                    oP = [psum[2][0:64, 0:512], psum[2][0:64, 512:1024]]
                    zi = [0]

                    def loadq(G):
                        QT, QTB = QT_r.next()
                        S.op("sp", lambda e: e.dma_start(out=QT, in_=qT_d[:, :, G * 512:(G + 1) * 512]), writes=[QTB], dma=True)
                        return QT, QTB

                    qnext = loadq(0)
                    for G in range(NG):
                        QT, QTB = qnext
                        if G + 1 < NG:
                            qnext = loadq(G + 1)
                        jmax = 4 * G + 3
                        units = [(hp, j) for hp in range(4) for j in range(jmax, -1, -1)]
                        NU = len(units)
                        st_ = [dict() for _ in range(NU)]
                        cur_sr = {}

                        def stage1a(u, G=G, QT=QT, QTB=QTB, units=units, st_=st_):
                            hp, j = units[u]
                            d = st_[u]
                            k = zi[0] % 2
                            zi[0] += 1
                            z = zP[k]
                            c0 = max(j - 4 * G, 0) * 128
                            d["c0"] = c0
                            for hh in range(2):
                                kt = KT[hh * 64:(hh + 1) * 64, hp, j * 128:(j + 1) * 128]
                                qs = QT[hh * 64:(hh + 1) * 64, hp, c0:512]
                                zz = z[:, hh * 512 + c0:(hh + 1) * 512]
                                S.op("pe", lambda e, zz=zz, kt=kt, qs=qs: e.matmul(zz, lhsT=kt, rhs=qs, start=True, stop=True), reads=[QTB], writes=[zB[k][hh]])
                            ee, eeB = e_r.next()
                            z3 = z[:, :].rearrange("p (a b) -> p a b", a=2)
                            S.op("act", lambda e: e.activation(out=ee[:, :, c0:512], in_=z3[:, :, c0:512], func=AF.Exp), reads=[zB[k][0], zB[k][1]], writes=[eeB])
                            if j >= 4 * G:
                                mk = masks[:, j - 4 * G, c0:c0 + 128]
                                for hh in range(2):
                                    S.op("dve", lambda e, hh=hh: e.tensor_tensor(out=ee[:, hh, c0:c0 + 128], in0=ee[:, hh, c0:c0 + 128], in1=mk, op=ALU.mult), reads=[mB], writes=[eeB])
                            d["ee"], d["eeB"] = ee, eeB

                        def stage1b(u, units=units, st_=st_, cur_sr=cur_sr, jmax=jmax):
                            hp, j = units[u]
                            d = st_[u]
                            c0 = d["c0"]
                            ee, eeB = d["ee"], d["eeB"]
                            lp, lpB = lp_r.next()
                            S.op("act", lambda e: e.activation(out=lp[:, :, c0:512], in_=ee[:, :, c0:512], func=AF.Ln, bias=oneT[:, 0:1]), reads=[eeB], writes=[lpB])
                            d["lp"], d["lpB"] = lp, lpB
                            d["sr_in"] = cur_sr.get(hp)
                            if j > 0:
                                sr, srB = sr_r.next()
                                if c0 > 0:
                                    S.op("pool", lambda e: e.memset(sr[:, :, 0:c0], 0.0), writes=[srB])
                                if j == jmax:
                                    S.op("dve", lambda e: e.tensor_copy(out=sr[:, :, c0:512], in_=lp[:, :, c0:512]), reads=[lpB], writes=[srB])
                                else:
                                    psr, psrB = cur_sr[hp]
                                    S.op("dve", lambda e: e.tensor_tensor(out=sr[:, :, c0:512], in0=psr[:, :, c0:512], in1=lp[:, :, c0:512], op=ALU.add), reads=[lpB, psrB], writes=[srB])
                                cur_sr[hp] = (sr, srB)

                        def stage2(u, units=units, st_=st_, jmax=jmax):
                            hp, j = units[u]
                            d = st_[u]
                            c0 = d["c0"]
                            lp, lpB = d["lp"], d["lpB"]
                            ee, eeB = d["ee"], d["eeB"]
                            last = (j == jmax)
                            for hh in range(2):
                                ww = wP[:, hh * 512 + c0:(hh + 1) * 512]
                                S.op("pe", lambda e, ww=ww, hh=hh: e.matmul(ww, lhsT=negtri, rhs=lp[:, hh, c0:512], start=True, stop=last), reads=[lpB, cstB], writes=[wBk[hh]])
                                if not last:
                                    psr, psrB = d["sr_in"]
                                    S.op("pe", lambda e, ww=ww, hh=hh, psr=psr: e.matmul(ww, lhsT=negones, rhs=psr[:, hh, c0:512], start=False, stop=True), reads=[psrB, cstB], writes=[wBk[hh]])
                            pp_, ppB_ = p_r.next()
                            w3 = wP[:, :].rearrange("p (a b) -> p a b", a=2)
                            S.op("act", lambda e: e.activation(out=pp_[:, :, c0:512], in_=w3[:, :, c0:512], func=AF.Exp), reads=[wBk[0], wBk[1]], writes=[ppB_])
                            at, atB = at_r.next()
                            if c0 > 0:
                                S.op("pool", lambda e: e.memset(at[:, :, 0:c0], 0.0), writes=[atB])
                            S.op("dve", lambda e: e.tensor_tensor(out=at[:, :, c0:512], in0=pp_[:, :, c0:512], in1=ee[:, :, c0:512], op=ALU.mult), reads=[ppB_, eeB], writes=[atB])
                            d["at"], d["atB"] = at, atB

                        def stage3(u, G=G, units=units, st_=st_, jmax=jmax):
                            hp, j = units[u]
                            d = st_[u]
                            at, atB = d["at"], d["atB"]
                            first = (j == jmax)
                            for hh in range(2):
                                h = 2 * hp + hh
                                o = oP[hh]
                                S.op("pe", lambda e, o=o, h=h, hh=hh: e.matmul(o, lhsT=VV[:, j, h * 64:(h + 1) * 64], rhs=at[:, hh, :], start=first, stop=(j == 0)),
                                     reads=[atB], writes=[oBk[hh]])
                            if j == 0:
                                for hh in range(2):
                                    ot, otB = ot_r.next()
                                    o = oP[hh]
                                    dst = oT_d[hh * 64:(hh + 1) * 64, hp, G * 512:(G + 1) * 512]
                                    if hh == 0:
                                        S.op("act", lambda e, ot=ot, o=o: e.copy(out=ot, in_=o), reads=[oBk[hh]], writes=[otB])
                                    else:
                                        S.op("dve", lambda e, ot=ot, o=o: e.tensor_copy(out=ot, in_=o), reads=[oBk[hh]], writes=[otB])
                                    S.op("sp", lambda e, dst=dst, ot=ot: e.dma_start(out=dst, in_=ot), reads=[otB], dma=True)
                            st_[u] = None

                        SK2, SK3 = 2, 4
                        for i in range(NU + SK3):
                            if i < NU:
                                stage1a(i)
                            if 0 <= i - SK2 < NU:
                                stage2(i - SK2)
                            if i < NU:
                                stage1b(i)
                            if 0 <= i - SK3 < NU:
                                stage3(i - SK3)
                    S.flush()

            with contextlib.ExitStack() as ph:
                w_u = sb(ph, "w_u", [128, 8, 1024], BF16)
                w_p = sb(ph, "w_p", [128, 8, 512], BF16)
                pw = sb(ph, "pw", [128, 4, 128], BF16)
                diag_t = sb(ph, "diag", [128, 4, 31, 128], BF16)
                wB = S.buf()
                with contextlib.ExitStack() as st_es:
                    stg = sb(st_es, "stg", [128, 6, 1024], F32)
                    sring = Ring(S, [stg[:, i, :] for i in range(6)])
                    load_w(sring, w_u, wB, w_in_d, l * 1024, 8, 1536, 1024, scale_col=V_GMIX)
                    load_w(sring, w_p, wB, w_in_d, l * 1024, 8, 2560, 512, scale_col=V_GMIX)
                    load_w(sring, pw, wB, pw_d, l * 512, 4, 0, 128)
                    for c in range(4):
                        for k in range(31):
                            eng = "dve" if (k % 2 == 0) else "pool"
                            S.op(eng, lambda e, c=c, k=k: e.tensor_scalar(out=diag_t[:, c, k, :], in0=cst[:, 0:128], scalar1=vecs[:, V_DW + c * 31 + k: V_DW + c * 31 + k + 1], scalar2=None, op0=ALU.mult),
                                 reads=[vB, cstB], writes=[wB])
                    S.flush()
                xnT_t = sb(ph, "xnT", [128, 2, 8, 512], BF16)
                cb_t = sb(ph, "cb", [128, 2, 4, 542], BF16)
                sig_t = sb(ph, "sig", [128, 2, 512], F32)
                xc_t = sb(ph, "xc", [128, 4, 512], F32)
                xbf_t = sb(ph, "xbf", [128, 4, 512], BF16)
                xsq_t = sb(ph, "xsq", [128, 4, 512], BF16)
                ln_t = sb(ph, "ln", [128, 4, 512], F32)
                tt_t = sb(ph, "tt", [128, 2, 512], F32)
                hs_t = sb(ph, "hs", [128, 2, 4, 512], BF16)
                pb_t = sb(ph, "pb", [128, 2, 4, 527], F32)
                pa_t = sb(ph, "pa", [128, 2, 527], F32)
                yp_t = sb(ph, "yp", [128, 4, 512], BF16)
                zp_t = sb(ph, "zp", [128, 2, 4, 512], BF16)
                icnt = sb(ph, "icnt", [128, 4, 16], F32)
                t16 = sb(ph, "t16", [128, 16], F32)
                iB = S.buf()
                S.op("sp", lambda e: e.dma_start(out=icnt[:, :, :], in_=icnt_d.rearrange("p (g t) -> p g t", g=4)), writes=[iB], dma=True)
                br = bank_ring()
                xnT_r = Ring(S, [xnT_t[:, i, :, :] for i in range(2)])
                cb_r = Ring(S, [cb_t[:, i, :, :] for i in range(2)])
                sig_r = Ring(S, [sig_t[:, i, :] for i in range(2)])
                tt_r = Ring(S, [tt_t[:, i, :] for i in range(2)])
                hs_r = Ring(S, [hs_t[:, i, :, :] for i in range(2)])
                pb_r = Ring(S, [pb_t[:, i, :, :] for i in range(2)])
                pa_r = Ring(S, [pa_t[:, i, :] for i in range(2)])
                zp_r = Ring(S, [zp_t[:, i, :, :] for i in range(2)])
                lnB, t16B = S.buf(), S.buf()
                xcB = [S.buf() for _ in range(4)]
                xbfB = [S.buf() for _ in range(4)]
                xsqB = [S.buf() for _ in range(4)]
                ypB = [S.buf() for _ in range(4)]
                cb_prev = None
                pb_prev = None

                def ldxT(G):
                    xT, xTB = xnT_r.next()
                    S.op("sp", lambda e: e.dma_start(out=xT, in_=xnT_d[:, :, G * 512:(G + 1) * 512]), writes=[xTB], dma=True)
                    return xT, xTB

                nxt = ldxT(0)
                for G in range(NG):
                    xT, xTB = nxt
                    if G + 1 < NG:
                        nxt = ldxT(G + 1)
                    cb, _ = cb_r.next()
                    cbB = [S.buf() for _ in range(4)]
                    pb, _ = pb_r.next()
                    pbB = [S.buf() for _ in range(4)]
                    zp, zpB = zp_r.next()

                    def Pj(c, G=G, cb=cb, cbB=cbB, xT=xT, xTB=xTB, cb_prev=cb_prev):
                        if G == 0:
                            S.op("pool", lambda e: e.memset(cb[:, c, 0:30], 0.0), writes=[cbB[c]])
                        else:
                            pcb, pcbB = cb_prev
                            S.op("pool", lambda e: e.tensor_copy(out=cb[:, c, 0:30], in_=pcb[:, c, 512:542]), reads=[pcbB[c]], writes=[cbB[c]])
                        psa, psaB = br.next()
                        psg, psgB = br.next()
                        for kc in range(8):
                            S.op("pe", lambda e, kc=kc: e.matmul(psa, lhsT=w_u[:, kc, c * 128:(c + 1) * 128], rhs=xT[:, kc, :], start=(kc == 0), stop=(kc == 7)),
                                 reads=[wB, xTB], writes=[psaB])
                        for kc in range(8):
                            S.op("pe", lambda e, kc=kc: e.matmul(psg, lhsT=w_u[:, kc, 512 + c * 128:512 + (c + 1) * 128], rhs=xT[:, kc, :], start=(kc == 0), stop=(kc == 7)),
                                 reads=[wB, xTB], writes=[psgB])
                        sg, sgB = sig_r.next()
                        S.op("act", lambda e: e.activation(out=sg, in_=psg, func=AF.Sigmoid), reads=[psgB], writes=[sgB])
                        S.op("dve", lambda e: e.tensor_tensor(out=cb[:, c, 30:542], in0=psa, in1=sg, op=ALU.mult), reads=[psaB, sgB], writes=[cbB[c]])

                    def Cv(c, cb=cb, cbB=cbB):
                        psc, pscB = br.next()
                        for k in range(31):
                            S.op("pe", lambda e, k=k: e.matmul(psc, lhsT=diag_t[:, c, k, :], rhs=cb[:, c, k:k + 512], start=(k == 0), stop=(k == 30)),
                                 reads=[wB, cbB[c]], writes=[pscB])
                        bcol = vecs[:, V_DWB + c:V_DWB + c + 1]
                        S.op("act", lambda e: e.activation(out=xc_t[:, c, :], in_=psc, func=AF.Identity, bias=bcol), reads=[pscB, vB], writes=[xcB[c]])
                        S.op("act", lambda e: e.activation(out=xsq_t[:, c, :], in_=psc, func=AF.Square, bias=bcol), reads=[pscB, vB], writes=[xsqB[c]])
                        S.op("dve", lambda e: e.tensor_copy(out=xbf_t[:, c, :], in_=xc_t[:, c, :]), reads=[xcB[c]], writes=[xbfB[c]])

                    def Pp(g, G=G, pb=pb, pbB=pbB, xT=xT, xTB=xTB, pb_prev=pb_prev):
                        w = 2 << g
                        if G == 0:
                            S.op("pool", lambda e: e.memset(pb[:, g, 0:15], 0.0), writes=[pbB[g]])
                        else:
                            ppb, ppbB = pb_prev
                            S.op("pool", lambda e: e.tensor_copy(out=pb[:, g, 0:15], in_=ppb[:, g, 512:527]), reads=[ppbB[g]], writes=[pbB[g]])
                        psp, pspB = br.next()
                        for kc in range(8):
                            S.op("pe", lambda e, kc=kc: e.matmul(psp, lhsT=w_p[:, kc, g * 128:(g + 1) * 128], rhs=xT[:, kc, :], start=(kc == 0), stop=(kc == 7)),
                                 reads=[wB, xTB], writes=[pspB])
                        S.op("act", lambda e: e.copy(out=pb[:, g, 15:527], in_=psp), reads=[pspB], writes=[pbB[g]])
                        cur = pb[:, g, :]
                        curB = pbB[g]
                        m = 1
                        while m < w:
                            nx, nxB = pa_r.next()
                            S.op("dve", lambda e, nx=nx, cur=cur, m=m: e.tensor_tensor(out=nx[:, m:527], in0=cur[:, m:527], in1=cur[:, 0:527 - m], op=ALU.add),
                                 reads=[curB], writes=[nxB])
                            cur, curB = nx, nxB
                            m *= 2
                        S.op("dve", lambda e, cur=cur: e.scalar_tensor_tensor(out=yp_t[:, g, :], in0=cur[:, 15:527], scalar=1.0 / w, in1=pb[:, g, 15:527], op0=ALU.mult, op1=ALU.subtract),
                             reads=[curB, pbB[g]], writes=[ypB[g]])
                        if G == 0:
                            S.op("dve", lambda e, cur=cur: e.tensor_tensor(out=t16[:, :], in0=cur[:, 15:31], in1=icnt[:, g, :], op=ALU.mult), reads=[curB, iB], writes=[t16B])
                            S.op("dve", lambda e: e.tensor_tensor(out=yp_t[:, g, 0:16], in0=t16[:, :], in1=pb[:, g, 15:31], op=ALU.subtract), reads=[t16B, pbB[g]], writes=[ypB[g]])

                    def Pw(g, zp=zp, zpB=zpB):
                        psq, psqB = br.next()
                        S.op("pe", lambda e: e.matmul(psq, lhsT=pw[:, g, :], rhs=yp_t[:, g, :], start=True, stop=True), reads=[wB, ypB[g]], writes=[psqB])
                        S.op("act", lambda e: e.activation(out=zp[:, g, :], in_=psq, func=AF.Copy, scale=vecs[:, V_PS + g:V_PS + g + 1]), reads=[psqB, vB], writes=[zpB])

                    Pj(0)
                    Pj(1)
                    Cv(0)
                    Pp(0)
                    Pj(2)
                    Cv(1)
                    Pp(1)
                    Pj(3)
                    Cv(2)
                    Pp(2)
                    Cv(3)
                    Pp(3)
                    cb_prev = (cb, cbB)
                    s1, s1B = br.next()
                    s2, s2B = br.next()
                    for c in range(4):
                        S.op("pe", lambda e, s1=s1, c=c: e.matmul(s1, lhsT=ones, rhs=xbf_t[:, c, :], start=(c == 0), stop=(c == 3)), reads=[xbfB[c], cstB], writes=[s1B])
                    for c in range(4):
                        S.op("pe", lambda e, s2=s2, c=c: e.matmul(s2, lhsT=ones, rhs=xsq_t[:, c, :], start=(c == 0), stop=(c == 3)), reads=[xsqB[c], cstB], writes=[s2B])
                    for g in range(4):
                        Pw(g)
                    S.op("sp", lambda e, zp=zp, G=G: e.dma_start(out=zpT_d[:, :, G * 512:(G + 1) * 512], in_=zp), reads=[zpB], dma=True)
                    mean, msq, var = ln_t[:, 0, :], ln_t[:, 1, :], ln_t[:, 2, :]
                    S.op("act", lambda e, s1=s1: e.activation(out=mean, in_=s1, func=AF.Copy, scale=1.0 / 512), reads=[s1B], writes=[lnB])
                    S.op("dve", lambda e: e.tensor_tensor(out=msq, in0=mean, in1=mean, op=ALU.mult), reads=[lnB], writes=[lnB])
                    S.op("dve", lambda e, s2=s2: e.scalar_tensor_tensor(out=var, in0=s2, scalar=1.0 / 512, in1=msq, op0=ALU.mult, op1=ALU.subtract), reads=[s2B, lnB], writes=[lnB])
                    S.op("act", lambda e: e.activation(out=var, in_=var, func=AF.Ln, bias=epsT[:, 0:1]), reads=[lnB], writes=[lnB])
                    S.op("act", lambda e: e.activation(out=var, in_=var, func=AF.Exp, scale=-0.5), reads=[lnB], writes=[lnB])
                    hs, hsB = hs_r.next()
                    for c in range(4):
                        t1, t1B = tt_r.next()
                        S.op("dve", lambda e, t1=t1, c=c: e.tensor_tensor(out=t1, in0=xc_t[:, c, :], in1=mean, op=ALU.subtract), reads=[xcB[c], lnB], writes=[t1B])
                        S.op("pool", lambda e, t1=t1: e.tensor_tensor(out=t1, in0=t1, in1=var, op=ALU.mult), reads=[lnB, t1B], writes=[t1B])
                        S.op("act", lambda e, t1=t1, c=c, hs=hs: e.activation(out=hs[:, c, :], in_=t1, func=AF.Silu, scale=vecs[:, V_LNG + c:V_LNG + c + 1], bias=vecs[:, V_LNB + c:V_LNB + c + 1]),
                             reads=[t1B, vB], writes=[hsB])
                    S.op("sp", lambda e, hs=hs, G=G: e.dma_start(out=hsT_d[:, :, G * 512:(G + 1) * 512], in_=hs), reads=[hsB], dma=True)
                    pb_prev = (pb, pbB)
                S.flush()

            with contextlib.ExitStack() as ph:
                w_g = sb(ph, "w_g", [128, 8, 3072], BF16)
                w_br = sb(ph, "w_br", [128, 3, 4, 1024], BF16)
                w_o = sb(ph, "w_o", [128, 8, 1024], BF16)
                wB = S.buf()
                with contextlib.ExitStack() as st_es:
                    stg = sb(st_es, "stg", [128, 6, 1024], F32)
                    sring = Ring(S, [stg[:, i, :] for i in range(6)])
                    load_w(sring, w_g, wB, w_in_d, l * 1024, 8, 3072, 3072, scale_col=V_GMIX)
                    load_w(sring, w_br[:, 0, :, :], wB, w_co_d, l * 512, 4, 0, 1024)
                    load_w(sring, w_br[:, 1, :, :], wB, w_ao_d, l * 512, 4, 0, 1024)
                    load_w(sring, w_br[:, 2, :, :], wB, w_po_d, l * 512, 4, 0, 1024)
                    load_w(sring, w_o, wB, w_o_d, l * 1024, 8, 0, 1024)
                    S.flush()
                xnT_t = sb(ph, "xnT", [128, 2, 8, 512], BF16)
                src_t = sb(ph, "srcs", [128, 2, 3, 4, 512], BF16)
                hin_t = sb(ph, "hin", [128, 2, 4, 1024], F32)
                mg_t = sb(ph, "mg", [128, 8, 512], F32)
                mgb_t = sb(ph, "mgb", [128, 8, 512], BF16)
                gate_t = sb(ph, "gate", [128, 2, 512], F32)
                tm_t = sb(ph, "tm", [128, 2, 512], F32)
                ho_t = sb(ph, "ho", [128, 2, 1024], F32)
                br = bank_ring()
                xnT_r = Ring(S, [xnT_t[:, i, :, :] for i in range(2)])
                src_r = Ring(S, [src_t[:, i, :, :, :] for i in range(2)])
                gate_r = Ring(S, [gate_t[:, i, :] for i in range(2)])
                tm_r = Ring(S, [tm_t[:, i, :] for i in range(2)])
                ho_r = Ring(S, [ho_t[:, i, :] for i in range(2)])
                hin_r = Ring(S, [hin_t[:, i, :, :] for i in range(2)])
                mgB = [S.buf() for _ in range(8)]
                mgbB = S.buf()
                srcs_d = [hsT_d, oT_d, zpT_d]
                def ldall(G):
                    xT, xTB = xnT_r.next()
                    S.op("sp", lambda e: e.dma_start(out=xT, in_=xnT_d[:, :, G * 512:(G + 1) * 512]), writes=[xTB], dma=True)
                    sr, srB = src_r.next()
                    for b3 in range(3):
                        S.op("sp", lambda e, b3=b3: e.dma_start(out=sr[:, b3, :, :], in_=srcs_d[b3][:, :, G * 512:(G + 1) * 512]), writes=[srB], dma=True)
                    hin, hinB = hin_r.next()
                    for tt in range(4):
                        r0 = G * 512 + tt * 128
                        S.op("sp", lambda e, tt=tt, r0=r0: e.dma_start(out=hin[:, tt, :], in_=src_d[r0:r0 + 128, :]), writes=[hinB], dma=True)
                    return xT, xTB, sr, srB, hin, hinB

                nxt = ldall(0)
                for G in range(NG):
                    xT, xTB, sr, srB, hin, hinB = nxt
                    if G + 1 < NG:
                        nxt = ldall(G + 1)
                    for dc in range(8):
                        for b3 in range(3):
                            psg, psgB = br.next()
                            col = b3 * 1024 + dc * 128
                            for kc in range(8):
                                S.op("pe", lambda e, psg=psg, kc=kc, col=col, xT=xT: e.matmul(psg, lhsT=w_g[:, kc, col:col + 128], rhs=xT[:, kc, :], start=(kc == 0), stop=(kc == 7)),
                                     reads=[wB, xTB], writes=[psgB])
                            gt, gtB = gate_r.next()
                            gcol = vecs[:, V_GB + b3 * 8 + dc: V_GB + b3 * 8 + dc + 1]
                            S.op("act", lambda e, gt=gt, psg=psg, gcol=gcol: e.activation(out=gt, in_=psg, func=AF.Sigmoid, bias=gcol), reads=[psgB, vB], writes=[gtB])
                            psy, psyB = br.next()
                            for kc in range(4):
                                S.op("pe", lambda e, psy=psy, kc=kc, b3=b3, dc=dc, sr=sr: e.matmul(psy, lhsT=w_br[:, b3, kc, dc * 128:(dc + 1) * 128], rhs=sr[:, b3, kc, :], start=(kc == 0), stop=(kc == 3)),
                                     reads=[wB, srB], writes=[psyB])
                            if b3 == 0:
                                S.op("dve", lambda e, dc=dc, psy=psy, gt=gt: e.tensor_tensor(out=mg_t[:, dc, :], in0=psy, in1=gt, op=ALU.mult), reads=[psyB, gtB], writes=[mgB[dc]])
                            else:
                                tm, tmB = tm_r.next()
                                S.op("dve", lambda e, tm=tm, psy=psy, gt=gt: e.tensor_tensor(out=tm, in0=psy, in1=gt, op=ALU.mult), reads=[psyB, gtB], writes=[tmB])
                                if b3 == 1:
                                    S.op("pool", lambda e, dc=dc, tm=tm: e.tensor_tensor(out=mg_t[:, dc, :], in0=mg_t[:, dc, :], in1=tm, op=ALU.add), reads=[tmB], writes=[mgB[dc]])
                                else:
                                    S.op("pool", lambda e, dc=dc, tm=tm: e.tensor_tensor(out=mgb_t[:, dc, :], in0=mg_t[:, dc, :], in1=tm, op=ALU.add), reads=[tmB, mgB[dc]], writes=[mgbB])
                    for tt in range(4):
                        ho, hoB = ho_r.next()
                        for dh in range(2):
                            pso, psoB = br.next()
                            for fc in range(8):
                                S.op("pe", lambda e, pso=pso, fc=fc, tt=tt, dh=dh: e.matmul(pso, lhsT=mgb_t[:, fc, tt * 128:(tt + 1) * 128], rhs=w_o[:, fc, dh * 512:(dh + 1) * 512], start=(fc == 0), stop=(fc == 7)),
                                     reads=[wB, mgbB], writes=[psoB])
                            S.op("dve", lambda e, ho=ho, pso=pso, tt=tt, dh=dh, hin=hin: e.tensor_tensor(out=ho[:, dh * 512:(dh + 1) * 512], in0=pso, in1=hin[:, tt, dh * 512:(dh + 1) * 512], op=ALU.add),
                                 reads=[psoB, hinB], writes=[hoB])
                        r0 = G * 512 + tt * 128
                        S.op("sp", lambda e, ho=ho, r0=r0: e.dma_start(out=hA_d[r0:r0 + 128, :], in_=ho), reads=[hoB], dma=True)
                S.flush()

            with contextlib.ExitStack() as ph:
                w1 = sb(ph, "w1", [128, 8, 4096], BF16)
                w2 = sb(ph, "w2", [128, 32, 1024], BF16)
                wB = S.buf()
                with contextlib.ExitStack() as st_es:
                    stg = sb(st_es, "stg", [128, 6, 1024], F32)
                    sring = Ring(S, [stg[:, i, :] for i in range(6)])
                    load_w(sring, w1, wB, w_m1_d, l * 1024, 8, 0, 4096, scale_col=V_GMLP)
                    load_w(sring, w2, wB, w_m2_d, l * 4096, 32, 0, 1024)
                    S.flush()
                HT = 256
                xin_t = sb(ph, "xin", [128, 4, 1024], F32)
                xn_t = sb(ph, "xn", [128, 2, 1024], F32)
                stat_t = sb(ph, "stat", [128, 2, 4], F32)
                xnT_t = sb(ph, "xnT", [128, 2, 8, HT], BF16)
                ff_t = sb(ph, "ff", [128, 2, 32, HT], BF16)
                rl_t = sb(ph, "rl", [128, 2, HT], F32)
                ho_t = sb(ph, "ho", [128, 2, 1024], F32)
                xin_r = Ring(S, [xin_t[:, i, :] for i in range(4)])
                xn_r = Ring(S, [xn_t[:, i, :] for i in range(2)])
                stat_r = Ring(S, [stat_t[:, i, :] for i in range(2)])
                xnT_r = Ring(S, [xnT_t[:, i, :, :] for i in range(2)])
                rl_r = Ring(S, [rl_t[:, i, :] for i in range(2)])
                ho_r = Ring(S, [ho_t[:, i, :] for i in range(2)])
                junkB = S.buf()
                ff_r = Ring(S, [ff_t[:, i, :, :] for i in range(2)])
                for it in ff_r.items:
                    pass
                ff_r.items = [(ap, [S.buf() for _ in range(32)]) for (ap, _) in ff_r.items]
                aps = []
                for p in psum[0:3]:
                    aps.append(p[:, 0:512])
                    aps.append(p[:, 512:1024])
                br = Ring(S, aps)
                ppB = S.buf()
                def ldh(H):
                    res = []
                    for tt in range(HT // 128):
                        r0 = H * HT + tt * 128
                        xin, xinB = xin_r.next()
                        S.op("sp", lambda e, xin=xin, r0=r0: e.dma_start(out=xin, in_=hA_d[r0:r0 + 128, :]), writes=[xinB], dma=True)
                        res.append((xin, xinB))
                    return res

                NH = T // HT

                def prep_norm(H, curx):
                    xT, xTB = xnT_r.next()
                    tiles = []
                    for tt in range(HT // 128):
                        xin, xinB = curx[tt]
                        xn, xnB = xn_r.next()
                        stt, sttB = stat_r.next()
                        rms_tile(None, xin, xinB, xn, xnB, stt, sttB, xn, xnB)
                        tiles.append((xin, xinB, xn, xnB))
                    return xT, xTB, tiles

                def prep_T(pr):
                    xT, xTB, tiles = pr
                    for tt, (xin, xinB, xn, xnB) in enumerate(tiles):
                        transpose_tile(xn, xnB, psum[3], ppB, xT, xTB, tt * 128, ("act", "dve"))

                ld_cur = ldh(0)
                pr_cur = prep_norm(0, ld_cur)
                prep_T(pr_cur)
                ld_nxt = ldh(1) if NH > 1 else None
                for H in range(NH):
                    xT, xTB, tiles = pr_cur
                    ff, ffB = ff_r.next()
                    if H + 1 < NH:
                        pr_nxt = prep_norm(H + 1, ld_nxt)
                    for fc in range(32):
                        ps, psB = br.next()
                        for kc in range(8):
                            S.op("pe", lambda e, ps=ps, kc=kc, fc=fc, xT=xT: e.matmul(ps[:, 0:HT], lhsT=w1[:, kc, fc * 128:(fc + 1) * 128], rhs=xT[:, kc, :], start=(kc == 0), stop=(kc == 7)),
                                 reads=[wB, xTB], writes=[psB])
                        rl, rlB = rl_r.next()
                        S.op("act", lambda e, rl=rl, ps=ps: e.activation(out=rl, in_=ps[:, 0:HT], func=AF.Relu), reads=[psB], writes=[rlB])
                        eng = "pool" if fc % 2 == 0 else "dve"
                        S.op(eng, lambda e, rl=rl, fc=fc, ff=ff: e.tensor_tensor(out=ff[:, fc, :], in0=rl, in1=rl, op=ALU.mult), reads=[rlB], writes=[ffB[fc]])
                    if H + 1 < NH:
                        prep_T(pr_nxt)
                    for tt in range(HT // 128):
                        ho, hoB = ho_r.next()
                        xin, xinB = tiles[tt][0], tiles[tt][1]
                        for dh in range(2):
                            pso, psoB = br.next()
                            for fc in range(32):
                                S.op("pe", lambda e, pso=pso, fc=fc, tt=tt, dh=dh, ff=ff: e.matmul(pso, lhsT=ff[:, fc, tt * 128:(tt + 1) * 128], rhs=w2[:, fc, dh * 512:(dh + 1) * 512], start=(fc == 0), stop=(fc == 31)),
                                     reads=[wB, ffB[fc]], writes=[psoB])
                            S.op("dve", lambda e, ho=ho, pso=pso, xin=xin, dh=dh: e.tensor_tensor(out=ho[:, dh * 512:(dh + 1) * 512], in0=pso, in1=xin[:, dh * 512:(dh + 1) * 512], op=ALU.add),
                                 reads=[psoB, xinB], writes=[hoB])
                        r0 = H * HT + tt * 128
                        S.op("sp", lambda e, ho=ho, r0=r0: e.dma_start(out=out_d[r0:r0 + 128, :], in_=ho), reads=[hoB], dma=True)
                    if H + 1 < NH:
                        pr_cur = pr_nxt
                        ld_nxt = ldh(H + 2) if H + 2 < NH else None
                S.flush()
    return nc


def host_prep(inp, L):
    f = lambda a: np.ascontiguousarray(np.asarray(a, dtype=np.float32))
    vecs = np.zeros((L, 128, NVEC), np.float32)
    for l in range(L):
        vecs[l, :, V_GB:V_GB + 24] = f(inp["gate_b"])[l].reshape(24, 128).T
        vecs[l, :, V_DWB:V_DWB + 4] = f(inp["conv_dw_b"])[l].reshape(4, 128).T
        vecs[l, :, V_LNG:V_LNG + 4] = f(inp["conv_ln_g"])[l].reshape(4, 128).T
        vecs[l, :, V_LNB:V_LNB + 4] = f(inp["conv_ln_b"])[l].reshape(4, 128).T
        vecs[l, :, V_PS:V_PS + 4] = f(inp["pool_scale"])[l].reshape(4, 128).T
        vecs[l, :, V_GQ] = np.tile(f(inp["q_norm_g"])[l], 2)
        vecs[l, :, V_GK] = np.tile(f(inp["k_norm_g"])[l], 2)
        vecs[l, :, V_GMIX:V_GMIX + 8] = f(inp["mix_norm_g"])[l].reshape(8, 128).T
        vecs[l, :, V_GMLP:V_GMLP + 8] = f(inp["mlp_norm_g"])[l].reshape(8, 128).T
        dw = f(inp["conv_dw"])[l]
        vecs[l, :, V_DW:V_DW + 124] = dw.reshape(31, 4, 128).transpose(2, 1, 0).reshape(128, 124)
    ident = np.eye(128, dtype=np.float32)
    jj, kk = np.meshgrid(np.arange(128), np.arange(128), indexing="ij")
    negtri = -(jj >= kk).astype(np.float32)
    negones = -np.ones((128, 128), np.float32)
    ones = np.ones((128, 128), np.float32)
    blk = (jj // 64 == kk // 64).astype(np.float32)
    cst = np.concatenate([ident, negtri, negones, ones, blk], axis=1).astype(ml_dtypes.bfloat16)
    p = np.arange(128)[:, None]
    q = np.arange(512)[None, :]
    masks = np.concatenate([(q > p + j * 128).astype(np.float32) for j in range(4)], axis=1).astype(ml_dtypes.bfloat16)
    icnt = np.zeros((128, 4, 16), np.float32)
    for g in range(4):
        w = 2 << g
        icnt[:, g, :] = 1.0 / np.minimum(np.arange(16) + 1, w).astype(np.float32)
    com = {
        "w_in": f(inp["w_in"]).reshape(L * 1024, 6144),
        "w_conv_out": f(inp["w_conv_out"]).reshape(L * 512, 1024),
        "w_att_out": f(inp["w_att_out"]).reshape(L * 512, 1024),
        "w_pool_out": f(inp["w_pool_out"]).reshape(L * 512, 1024),
        "pool_w": f(inp["pool_w"]).reshape(L * 512, 128),
        "w_o": f(inp["w_o"]).reshape(L * 1024, 1024),
        "w_mlp_in": f(inp["w_mlp_in"]).reshape(L * 1024, 4096),
        "w_mlp_out": f(inp["w_mlp_out"]).reshape(L * 4096, 1024),
        "vecs": vecs.reshape(L * 128, NVEC),
        "cst": cst,
        "identf": ident,
        "masks": masks,
        "icnt": icnt.reshape(128, 64),
    }
    return com


_NC_CACHE = {}


def run(inp, seqs, T, L):
    key = (T, L)
    if key not in _NC_CACHE:
        _NC_CACHE[key] = build(T, L)
    nc = _NC_CACHE[key]
    com = host_prep(inp, L)
    in_maps = []
    for c in range(8):
        m = dict(com)
        m["x"] = np.ascontiguousarray(seqs[c], dtype=np.float32)
        in_maps.append(m)
    res = run_bass_kernel_spmd(nc, in_maps, core_ids=list(range(8)))
    return [res.results[c]["y"] for c in range(8)]


def kernel(**inputs):
    x = np.asarray(inputs["x"], dtype=np.float32)
    B, T, D = x.shape
    L = np.asarray(inputs["w_in"]).shape[0]
    seqs = [x[c // 2] for c in range(8)]
    outs = run(inputs, seqs, T, L)
    return np.stack([outs[2 * b] for b in range(B)], axis=0).astype(np.float32)
```

```python
import contextlib
import numpy as np
import ml_dtypes
import concourse.bass as bass
import concourse.mybir as mybir
from concourse.bass_utils import run_bass_kernel_spmd

F32 = mybir.dt.float32
BF16 = mybir.dt.bfloat16
AF = mybir.ActivationFunctionType
ALU = mybir.AluOpType
AX = mybir.AxisListType

SAME_ENG_SYNC = True
EPS = 1e-6
NVEC = 24 + 4 + 4 + 4 + 4 + 1 + 1 + 8 + 8 + 124
V_GB, V_DWB, V_LNG, V_LNB, V_PS, V_GQ, V_GK, V_GMIX, V_GMLP, V_DW = 0, 24, 28, 32, 36, 40, 41, 42, 50, 58


class Buf:
    __slots__ = ("w", "r")

    def __init__(self):
        self.w = None
        self.r = {}


class Sched:
    BLK = dict(pe="tensor", act="scalar", dve="vector", pool="gpsimd", sp="sync")

    def __init__(self, nc, es, ndsem=28):
        self.nc = nc
        self.names = ["sp", "pe", "act", "dve", "pool"]
        self.sem = {e: es.enter_context(nc.semaphore("s_" + e)) for e in self.names}
        self.semval = {e: 0 for e in self.names}
        self.dsem = [es.enter_context(nc.semaphore("d%d" % i)) for i in range(ndsem)]
        self.dval = [0] * ndsem
        self.dlast = [None] * ndsem
        self.dnext = 0
        self.q = {e: [] for e in self.names}
        self.seen = {e: {} for e in self.names}
        self.bufs = []

    def buf(self):
        b = Buf()
        self.bufs.append(b)
        return b

    def op(self, eng, fn, reads=(), writes=(), dma=False):
        deps = set()
        for b in reads:
            if b.w is not None:
                deps.add(b.w)
        for b in writes:
            if b.w is not None:
                deps.add(b.w)
            deps.update(b.r.values())
        if dma:
            k = self.dnext
            self.dnext = (self.dnext + 1) % len(self.dsem)
            if self.dlast[k] is not None:
                deps.add(self.dlast[k])
            self.dval[k] += 16
            ev = ("d", k, self.dval[k])
            self.dlast[k] = ev
        else:
            ev = ("c", eng, len(self.q[eng]))
        self.q[eng].append(dict(fn=fn, deps=deps, ev=ev, dma=dma))
        key = (ev[0], ev[1])
        for b in reads:
            b.r[key] = ev
        for b in writes:
            b.w = ev
            b.r = {}
        return ev

    def flush(self):
        nc = self.nc
        obs = set()
        for e in self.names:
            for o in self.q[e]:
                nd = set()
                for d in o["deps"]:
                    if d[0] == "c" and d[1] == e and (e == "pe" or not SAME_ENG_SYNC):
                        continue
                    nd.add(d)
                    if d[0] == "c":
                        obs.add(d)
                o["deps"] = nd
        for e in self.names:
            for o in reversed(self.q[e]):
                if not o["dma"]:
                    obs.add(o["ev"])
                    break
        val = {}
        endval = {}
        for e in self.names:
            c = self.semval[e]
            for o in self.q[e]:
                if o["ev"] in obs:
                    c += 1
                    val[o["ev"]] = c
            endval[e] = c
        with nc.Block() as block:
            for e in self.names:
                def body(engine, e=e):
                    seen = self.seen[e]
                    for o in self.q[e]:
                        for d in sorted(o["deps"]):
                            if d[0] == "c":
                                key, v, sem = ("c", d[1]), val[d], self.sem[d[1]]
                            else:
                                key, v, sem = ("d", d[1]), d[2], self.dsem[d[1]]
                            if seen.get(key, 0) < v:
                                engine.wait_ge(sem, v)
                                seen[key] = v
                        ins = o["fn"](engine)
                        if o["dma"]:
                            ins.then_inc(self.dsem[o["ev"][1]], 16)
                        elif o["ev"] in val:
                            ins.then_inc(self.sem[e], 1)
                    for e2 in self.names:
                        if e2 != e and endval[e2] > seen.get(("c", e2), 0):
                            engine.wait_ge(self.sem[e2], endval[e2])
                            seen[("c", e2)] = endval[e2]
                    for k in range(len(self.dsem)):
                        if self.dval[k] > seen.get(("d", k), 0):
                            engine.wait_ge(self.dsem[k], self.dval[k])
                            seen[("d", k)] = self.dval[k]
                getattr(block, self.BLK[e])(body)
        self.semval = endval
        self.q = {e: [] for e in self.names}
        for b in self.bufs:
            b.w = None
            b.r = {}
        self.bufs = []
        self.dlast = [None] * len(self.dsem)


class Ring:
    def __init__(self, S, aps):
        self.items = [(ap, S.buf()) for ap in aps]
        self.i = 0

    def next(self):
        it = self.items[self.i]
        self.i = (self.i + 1) % len(self.items)
        return it


def build(T, L):
    NG = T // 512
    nc = bass.Bass("TRN2", target_bir_lowering=False)

    def din(name, shape, dt=F32):
        return nc.dram_tensor(name, shape, dt, kind="ExternalInput").ap()

    x_d = din("x", [T, 1024])
    w_in_d = din("w_in", [L * 1024, 6144])
    w_co_d = din("w_conv_out", [L * 512, 1024])
    w_ao_d = din("w_att_out", [L * 512, 1024])
    w_po_d = din("w_pool_out", [L * 512, 1024])
    pw_d = din("pool_w", [L * 512, 128])
    w_o_d = din("w_o", [L * 1024, 1024])
    w_m1_d = din("w_mlp_in", [L * 1024, 4096])
    w_m2_d = din("w_mlp_out", [L * 4096, 1024])
    vecs_d = din("vecs", [L * 128, NVEC])
    cst_d = din("cst", [128, 128 * 5], BF16)
    identf_d = din("identf", [128, 128])
    masks_d = din("masks", [128, 4 * 512], BF16)
    icnt_d = din("icnt", [128, 4 * 16])
    y_d = nc.dram_tensor("y", [T, 1024], F32, kind="ExternalOutput").ap()
    hA_d = nc.dram_tensor("hA", [T, 1024], F32).ap()
    hB_d = nc.dram_tensor("hB", [T, 1024], F32).ap()
    xnT_d = nc.dram_tensor("xnT", [128, 8 * T], BF16).ap().rearrange("p (c t) -> p c t", c=8)
    hsT_d = nc.dram_tensor("hsT", [128, 4 * T], BF16).ap().rearrange("p (c t) -> p c t", c=4)
    zpT_d = nc.dram_tensor("zpT", [128, 4 * T], BF16).ap().rearrange("p (c t) -> p c t", c=4)
    qT_d = nc.dram_tensor("qT", [128, 4 * T], BF16).ap().rearrange("p (c t) -> p c t", c=4)
    oT_d = nc.dram_tensor("oT", [128, 4 * T], BF16).ap().rearrange("p (c t) -> p c t", c=4)

    with contextlib.ExitStack() as es:
        S = Sched(nc, es)

        uid = [0]

        def sb(stack, name, shape, dt):
            uid[0] += 1
            return stack.enter_context(nc.sbuf_tensor("sb%d_%s" % (uid[0], name), shape, dt))

        cst = sb(es, "cst", [128, 5 * 128], BF16)
        identf = sb(es, "identf", [128, 128], F32)
        vecs = sb(es, "vecs", [128, NVEC], F32)
        epsT = sb(es, "epsT", [128, 1], F32)
        oneT = sb(es, "oneT", [128, 1], F32)
        gq8 = sb(es, "gq8", [128, 1], F32)
        psum = [es.enter_context(nc.psum_tensor("ps%d" % i, [128, 1024], F32)) for i in range(4)]
        negtri = cst[:, 128:256]
        negones = cst[:, 256:384]
        ones = cst[:, 384:512]
        blk = cst[:, 512:640]

        cstB = S.buf()
        S.op("sp", lambda e: e.dma_start(out=cst[:, :], in_=cst_d[:, :]), writes=[cstB], dma=True)
        S.op("sp", lambda e: e.dma_start(out=identf[:, :], in_=identf_d[:, :]), writes=[cstB], dma=True)
        S.op("dve", lambda e: e.memset(epsT[:, :], EPS), writes=[cstB])
        S.op("dve", lambda e: e.memset(oneT[:, :], 1.0), writes=[cstB])
        S.flush()

        def bank_ring():
            aps = []
            for p in psum:
                aps.append(p[:, 0:512])
                aps.append(p[:, 512:1024])
            return Ring(S, aps)

        cvt_rr = [0]

        def load_w(stage_ring, dst, dstB, src, rows0, kcs, col0, ncols, scale_col=None):
            for kc in range(kcs):
                for n0 in range(0, ncols, 1024):
                    n1 = min(ncols, n0 + 1024)
                    st, stB = stage_ring.next()
                    sap = src[rows0 + kc * 128: rows0 + (kc + 1) * 128, col0 + n0: col0 + n1]
                    S.op("sp", lambda e, st=st, sap=sap, n=n1 - n0: e.dma_start(out=st[:, 0:n], in_=sap),
                         writes=[stB], dma=True)
                    dap = dst[:, kc, n0:n1]
                    sin = st[:, 0:n1 - n0]
                    eng = ["dve", "act"][cvt_rr[0] % 2]
                    cvt_rr[0] += 1
                    if scale_col is None:
                        if eng == "act":
                            S.op("act", lambda e, dap=dap, sin=sin: e.copy(out=dap, in_=sin), reads=[stB], writes=[dstB])
                        else:
                            S.op(eng, lambda e, dap=dap, sin=sin: e.tensor_copy(out=dap, in_=sin), reads=[stB], writes=[dstB])
                    else:
                        sc = vecs[:, scale_col + kc: scale_col + kc + 1]
                        if eng == "act":
                            S.op("act", lambda e, dap=dap, sin=sin, sc=sc: e.activation(out=dap, in_=sin, func=AF.Copy, scale=sc),
                                 reads=[stB], writes=[dstB])
                        else:
                            S.op(eng, lambda e, dap=dap, sin=sin, sc=sc: e.tensor_scalar(out=dap, in0=sin, scalar1=sc, scalar2=None, op0=ALU.mult),
                                 reads=[stB], writes=[dstB])

        def rms_tile(src_rows, xin, xinB, junk, junkB, st, stB, xn, xnB):
            if src_rows is not None:
                S.op("sp", lambda e: e.dma_start(out=xin, in_=src_rows), writes=[xinB], dma=True)
            S.op("act", lambda e: e.activation(out=junk, in_=xin, func=AF.Square), reads=[xinB], writes=[junkB])
            S.op("dve", lambda e: e.tensor_reduce(out=st[:, 0:1], in_=junk, axis=AX.X, op=ALU.add), reads=[junkB], writes=[stB])
            S.op("act", lambda e: e.activation(out=st[:, 1:2], in_=st[:, 0:1], func=AF.Sqrt, scale=1.0 / 1024, bias=epsT[:, 0:1]),
                 reads=[stB], writes=[stB])
            S.op("dve", lambda e: e.reciprocal(out=st[:, 2:3], in_=st[:, 1:2]), reads=[stB], writes=[stB])
            S.op("dve", lambda e: e.tensor_scalar(out=xn, in0=xin, scalar1=st[:, 2:3], scalar2=None, op0=ALU.mult),
                 reads=[xinB, stB], writes=[xnB])

        def transpose_tile(xn, xnB, pp, ppB, dstT, dstTB, t0, evac):
            for c in range(8):
                S.op("pe", lambda e, c=c: e.transpose(out=pp[:, c * 128:(c + 1) * 128], in_=xn[:, c * 128:(c + 1) * 128], identity=identf[:, :]),
                     reads=[xnB, cstB], writes=[ppB])
            src = pp[:, :].rearrange("p (c t) -> p c t", c=8)
            for hh in range(2):
                d = dstT[:, hh * 4:(hh + 1) * 4, t0:t0 + 128]
                s_ = src[:, hh * 4:(hh + 1) * 4, :]
                if evac[hh] == "act":
                    S.op("act", lambda e, d=d, s_=s_: e.copy(out=d, in_=s_), reads=[ppB], writes=[dstTB])
                else:
                    S.op("dve", lambda e, d=d, s_=s_: e.tensor_copy(out=d, in_=s_), reads=[ppB], writes=[dstTB])

        for l in range(L):
            src_d = x_d if l == 0 else hB_d
            out_d = hB_d if l == L - 1 and False else (y_d if l == L - 1 else hB_d)
            vB = S.buf()
            S.op("sp", lambda e, l=l: e.dma_start(out=vecs[:, :], in_=vecs_d[l * 128:(l + 1) * 128, :]), writes=[vB], dma=True)
            S.op("dve", lambda e: e.tensor_scalar(out=gq8[:, :], in0=vecs[:, V_GQ:V_GQ + 1], scalar1=0.125, scalar2=None, op0=ALU.mult),
                 reads=[vB], writes=[vB])
            S.flush()

            with contextlib.ExitStack() as ph:
                KT = sb(ph, "KT", [128, 4, T], BF16)
                VV = sb(ph, "VV", [128, T // 128, 512], BF16)
                with contextlib.ExitStack() as pa:
                    wqkv = sb(pa, "wqkv", [128, 8, 1536], BF16)
                    wB = S.buf()
                    with contextlib.ExitStack() as st_es:
                        stg = sb(st_es, "stg", [128, 6, 1024], F32)
                        sring = Ring(S, [stg[:, i, :] for i in range(6)])
                        load_w(sring, wqkv, wB, w_in_d, l * 1024, 8, 0, 1536, scale_col=V_GMIX)
                        S.flush()
                    xin_t = sb(pa, "xin", [128, 2, 1024], F32)
                    xn_t = sb(pa, "xn", [128, 2, 1024], F32)
                    stat_t = sb(pa, "stat", [128, 2, 4], F32)
                    xnT_t = sb(pa, "xnT", [128, 2, 8, 512], BF16)
                    QT_t = sb(pa, "QT", [128, 2, 4, 512], BF16)
                    sq_t = sb(pa, "sq", [128, 2, 512], BF16)
                    sd_t = sb(pa, "sd", [128, 2, 512], F32)
                    xin_r = Ring(S, [xin_t[:, i, :] for i in range(2)])
                    xn_r = Ring(S, [xn_t[:, i, :] for i in range(2)])
                    stat_r = Ring(S, [stat_t[:, i, :] for i in range(2)])
                    xnT_r = Ring(S, [xnT_t[:, i, :, :] for i in range(2)])
                    QT_r = Ring(S, [QT_t[:, i, :, :] for i in range(2)])
                    sq_r = Ring(S, [sq_t[:, i, :] for i in range(2)])
                    sd_r = Ring(S, [sd_t[:, i, :] for i in range(2)])
                    a_r = bank_ring()
                    tp_r = Ring(S, [psum[3][:, :], psum[2][:, :]])
                    KTB = S.buf()
                    VB = S.buf()
                    def ldx(ti):
                        xin, xinB = xin_r.next()
                        S.op("sp", lambda e: e.dma_start(out=xin, in_=src_d[ti * 128:(ti + 1) * 128, :]), writes=[xinB], dma=True)
                        return xin, xinB

                    nxt = ldx(0)
                    for G in range(NG):
                        xnT, xnTB = xnT_r.next()
                        for tt in range(4):
                            xin, xinB = nxt
                            if G * 4 + tt + 1 < NG * 4:
                                nxt = ldx(G * 4 + tt + 1)
                            xn, xnB = xn_r.next()
                            stt, sttB = stat_r.next()
                            rms_tile(None, xin, xinB, xn, xnB, stt, sttB, xn, xnB)
                            pidx = 3 if (G * 4 + tt) % 2 == 0 else 2
                            pB0, pB1 = a_r.items[2 * pidx][1], a_r.items[2 * pidx + 1][1]
                            pp = psum[pidx]
                            for c in range(8):
                                S.op("pe", lambda e, c=c, xn=xn, pp=pp: e.transpose(out=pp[:, c * 128:(c + 1) * 128], in_=xn[:, c * 128:(c + 1) * 128], identity=identf[:, :]),
                                     reads=[xnB, cstB], writes=[pB0 if c < 4 else pB1])
                            srcp = pp[:, :].rearrange("p (c t) -> p c t", c=8)
                            S.op("act", lambda e, tt=tt, srcp=srcp, xnT=xnT: e.copy(out=xnT[:, 0:4, tt * 128:(tt + 1) * 128], in_=srcp[:, 0:4, :]),
                                 reads=[pB0], writes=[xnTB])
                            S.op("dve", lambda e, tt=tt, srcp=srcp, xnT=xnT: e.tensor_copy(out=xnT[:, 4:8, tt * 128:(tt + 1) * 128], in_=srcp[:, 4:8, :]),
                                 reads=[pB1], writes=[xnTB])
                        S.op("sp", lambda e, G=G, xnT=xnT: e.dma_start(out=xnT_d[:, :, G * 512:(G + 1) * 512], in_=xnT), reads=[xnTB], dma=True)
                        QT, QTB = QT_r.next()
                        for qk in range(2):
                            for c in range(4):
                                i0 = (qk * 4 + c) % 2
                                ps, psB = a_r.items[i0 * 2]
                                ps2, ps2B = a_r.items[i0 * 2 + 1]
                                col = qk * 512 + c * 128
                                for kc in range(8):
                                    S.op("pe", lambda e, ps=ps, kc=kc, col=col, xnT=xnT: e.matmul(ps, lhsT=wqkv[:, kc, col:col + 128], rhs=xnT[:, kc, :], start=(kc == 0), stop=(kc == 7)),
                                         reads=[wB, xnTB], writes=[psB])
                                sq, sqB = sq_r.next()
                                S.op("act", lambda e, sq=sq, ps=ps: e.activation(out=sq, in_=ps, func=AF.Square), reads=[psB], writes=[sqB])
                                S.op("pe", lambda e, ps2=ps2, sq=sq: e.matmul(ps2, lhsT=blk, rhs=sq, start=True, stop=True), reads=[sqB, cstB], writes=[ps2B])
                                sd, sdB = sd_r.next()
                                S.op("act", lambda e, sd=sd, ps2=ps2: e.activation(out=sd, in_=ps2, func=AF.Ln, scale=1.0 / 64, bias=epsT[:, 0:1]),
                                     reads=[ps2B], writes=[sdB])
                                S.op("act", lambda e, sd=sd: e.activation(out=sd, in_=sd, func=AF.Exp, scale=-0.5), reads=[sdB], writes=[sdB])
                                if qk == 0:
                                    S.op("dve", lambda e, c=c, ps=ps, sd=sd, QT=QT: e.scalar_tensor_tensor(out=QT[:, c, :], in0=ps, scalar=gq8[:, 0:1], in1=sd, op0=ALU.mult, op1=ALU.mult),
                                         reads=[psB, sdB, vB], writes=[QTB])
                                else:
                                    S.op("dve", lambda e, c=c, ps=ps, sd=sd, G=G: e.scalar_tensor_tensor(out=KT[:, c, G * 512:(G + 1) * 512], in0=ps, scalar=vecs[:, V_GK:V_GK + 1], in1=sd, op0=ALU.mult, op1=ALU.mult),
                                         reads=[psB, sdB, vB], writes=[KTB])
                        S.op("sp", lambda e, G=G, QT=QT: e.dma_start(out=qT_d[:, :, G * 512:(G + 1) * 512], in_=QT), reads=[QTB], dma=True)
                        for tt in range(4):
                            ps, psB = a_r.items[tt % 4]
                            for kc in range(8):
                                S.op("pe", lambda e, ps=ps, kc=kc, tt=tt, xnT=xnT: e.matmul(ps, lhsT=xnT[:, kc, tt * 128:(tt + 1) * 128], rhs=wqkv[:, kc, 1024:1536], start=(kc == 0), stop=(kc == 7)),
                                     reads=[wB, xnTB], writes=[psB])
                            if tt % 2 == 0:
                                S.op("act", lambda e, ps=ps, tt=tt, G=G: e.copy(out=VV[:, G * 4 + tt, :], in_=ps), reads=[psB], writes=[VB])
                            else:
                                S.op("dve", lambda e, ps=ps, tt=tt, G=G: e.tensor_copy(out=VV[:, G * 4 + tt, :], in_=ps), reads=[psB], writes=[VB])
                    S.flush()

                with contextlib.ExitStack() as pb_:
                    masks = sb(pb_, "masks", [128, 4, 512], BF16)
                    QT_t = sb(pb_, "QTb", [128, 2, 4, 512], BF16)
                    e_t = sb(pb_, "ee", [128, 3, 2, 512], BF16)
                    lp_t = sb(pb_, "lp", [128, 3, 2, 512], BF16)
                    sr_t = sb(pb_, "sr", [128, 4, 2, 512], BF16)
                    p_t = sb(pb_, "pp", [128, 2, 2, 512], BF16)
                    at_t = sb(pb_, "at", [128, 3, 2, 512], BF16)
                    ot_t = sb(pb_, "ot", [128, 4, 512], BF16)
                    mB = S.buf()
                    S.op("sp", lambda e: e.dma_start(out=masks[:, :, :], in_=masks_d.rearrange("p (j q) -> p j q", j=4)), writes=[mB], dma=True)
                    QT_r = Ring(S, [QT_t[:, i, :, :] for i in range(2)])
                    e_r = Ring(S, [e_t[:, i, :, :] for i in range(3)])
                    lp_r = Ring(S, [lp_t[:, i, :, :] for i in range(3)])
                    sr_r = Ring(S, [sr_t[:, i, :, :] for i in range(4)])
                    p_r = Ring(S, [p_t[:, i, :, :] for i in range(2)])
                    at_r = Ring(S, [at_t[:, i, :, :] for i in range(3)])
                    ot_r = Ring(S, [ot_t[0:64, i, :] for i in range(4)])
                    zB = [[S.buf(), S.buf()], [S.buf(), S.buf()]]
                    zP = [psum[0], psum[3]]
                    wBk = [S.buf(), S.buf()]
                    wP = psum[1]
                    oBk = [S.buf(), S.buf()]
                    oP = [psum[2][0:64, 0:512], psum[2][0:64, 512:1024]]
                    zi = [0]

                    def loadq(G):
                        QT, QTB = QT_r.next()
                        S.op("sp", lambda e: e.dma_start(out=QT, in_=qT_d[:, :, G * 512:(G + 1) * 512]), writes=[QTB], dma=True)
                        return QT, QTB

                    qnext = loadq(0)
                    for G in range(NG):
                        QT, QTB = qnext
                        if G + 1 < NG:
                            qnext = loadq(G + 1)
                        jmax = 4 * G + 3
                        units = [(hp, j) for hp in range(4) for j in range(jmax, -1, -1)]
                        NU = len(units)
                        st_ = [dict() for _ in range(NU)]
                        cur_sr = {}

                        def stage1a(u, G=G, QT=QT, QTB=QTB, units=units, st_=st_):
                            hp, j = units[u]
                            d = st_[u]
                            k = zi[0] % 2
                            zi[0] += 1
                            z = zP[k]
                            c0 = max(j - 4 * G, 0) * 128
                            d["c0"] = c0
                            for hh in range(2):
                                kt = KT[hh * 64:(hh + 1) * 64, hp, j * 128:(j + 1) * 128]
                                qs = QT[hh * 64:(hh + 1) * 64, hp, c0:512]
                                zz = z[:, hh * 512 + c0:(hh + 1) * 512]
                                S.op("pe", lambda e, zz=zz, kt=kt, qs=qs: e.matmul(zz, lhsT=kt, rhs=qs, start=True, stop=True), reads=[QTB], writes=[zB[k][hh]])
                            ee, eeB = e_r.next()
                            z3 = z[:, :].rearrange("p (a b) -> p a b", a=2)
                            S.op("act", lambda e: e.activation(out=ee[:, :, c0:512], in_=z3[:, :, c0:512], func=AF.Exp), reads=[zB[k][0], zB[k][1]], writes=[eeB])
                            if j >= 4 * G:
                                mk = masks[:, j - 4 * G, c0:c0 + 128]
                                for hh in range(2):
                                    S.op("dve", lambda e, hh=hh: e.tensor_tensor(out=ee[:, hh, c0:c0 + 128], in0=ee[:, hh, c0:c0 + 128], in1=mk, op=ALU.mult), reads=[mB], writes=[eeB])
                            d["ee"], d["eeB"] = ee, eeB

                        def stage1b(u, units=units, st_=st_, cur_sr=cur_sr, jmax=jmax):
                            hp, j = units[u]
                            d = st_[u]
                            c0 = d["c0"]
                            ee, eeB = d["ee"], d["eeB"]
                            lp, lpB = lp_r.next()
                            S.op("act", lambda e: e.activation(out=lp[:, :, c0:512], in_=ee[:, :, c0:512], func=AF.Ln, bias=oneT[:, 0:1]), reads=[eeB], writes=[lpB])
                            d["lp"], d["lpB"] = lp, lpB
                            d["sr_in"] = cur_sr.get(hp)
                            if j > 0:
                                sr, srB = sr_r.next()
                                if c0 > 0:
                                    S.op("pool", lambda e: e.memset(sr[:, :, 0:c0], 0.0), writes=[srB])
                                if j == jmax:
                                    S.op("dve", lambda e: e.tensor_copy(out=sr[:, :, c0:512], in_=lp[:, :, c0:512]), reads=[lpB], writes=[srB])
                                else:
                                    psr, psrB = cur_sr[hp]
                                    S.op("dve", lambda e: e.tensor_tensor(out=sr[:, :, c0:512], in0=psr[:, :, c0:512], in1=lp[:, :, c0:512], op=ALU.add), reads=[lpB, psrB], writes=[srB])
                                cur_sr[hp] = (sr, srB)

                        def stage2(u, units=units, st_=st_, jmax=jmax):
                            hp, j = units[u]
                            d = st_[u]
                            c0 = d["c0"]
                            lp, lpB = d["lp"], d["lpB"]
                            ee, eeB = d["ee"], d["eeB"]
                            last = (j == jmax)
                            for hh in range(2):
                                ww = wP[:, hh * 512 + c0:(hh + 1) * 512]
                                S.op("pe", lambda e, ww=ww, hh=hh: e.matmul(ww, lhsT=negtri, rhs=lp[:, hh, c0:512], start=True, stop=last), reads=[lpB, cstB], writes=[wBk[hh]])
                                if not last:
                                    psr, psrB = d["sr_in"]
                                    S.op("pe", lambda e, ww=ww, hh=hh, psr=psr: e.matmul(ww, lhsT=negones, rhs=psr[:, hh, c0:512], start=False, stop=True), reads=[psrB, cstB], writes=[wBk[hh]])
                            pp_, ppB_ = p_r.next()
                            w3 = wP[:, :].rearrange("p (a b) -> p a b", a=2)
                            S.op("act", lambda e: e.activation(out=pp_[:, :, c0:512], in_=w3[:, :, c0:512], func=AF.Exp), reads=[wBk[0], wBk[1]], writes=[ppB_])
                            at, atB = at_r.next()
                            if c0 > 0:
                                S.op("pool", lambda e: e.memset(at[:, :, 0:c0], 0.0), writes=[atB])
                            S.op("dve", lambda e: e.tensor_tensor(out=at[:, :, c0:512], in0=pp_[:, :, c0:512], in1=ee[:, :, c0:512], op=ALU.mult), reads=[ppB_, eeB], writes=[atB])
                            d["at"], d["atB"] = at, atB

                        def stage3(u, G=G, units=units, st_=st_, jmax=jmax):
                            hp, j = units[u]
                            d = st_[u]
                            at, atB = d["at"], d["atB"]
                            first = (j == jmax)
                            for hh in range(2):
                                h = 2 * hp + hh
                                o = oP[hh]
                                S.op("pe", lambda e, o=o, h=h, hh=hh: e.matmul(o, lhsT=VV[:, j, h * 64:(h + 1) * 64], rhs=at[:, hh, :], start=first, stop=(j == 0)),
                                     reads=[atB], writes=[oBk[hh]])
                            if j == 0:
                                for hh in range(2):
                                    ot, otB = ot_r.next()
                                    o = oP[hh]
                                    dst = oT_d[hh * 64:(hh + 1) * 64, hp, G * 512:(G + 1) * 512]
                                    S.op("dve", lambda e, ot=ot, o=o: e.tensor_copy(out=ot, in_=o), reads=[oBk[hh]], writes=[otB])
                                    S.op("sp", lambda e, dst=dst, ot=ot: e.dma_start(out=dst, in_=ot), reads=[otB], dma=True)
                            st_[u] = None

                        SK2, SK3 = 2, 4
                        for i in range(NU + SK3):
                            if i < NU:
                                stage1a(i)
                            if 0 <= i - SK2 < NU:
                                stage2(i - SK2)
                            if i < NU:
                                stage1b(i)
                            if 0 <= i - SK3 < NU:
                                stage3(i - SK3)
                    S.flush()

            with contextlib.ExitStack() as ph:
                w_u = sb(ph, "w_u", [128, 8, 1024], BF16)
                w_p = sb(ph, "w_p", [128, 8, 512], BF16)
                pw = sb(ph, "pw", [128, 4, 128], BF16)
                diag_t = sb(ph, "diag", [128, 4, 31, 128], BF16)
                wB = S.buf()
                with contextlib.ExitStack() as st_es:
                    stg = sb(st_es, "stg", [128, 6, 1024], F32)
                    sring = Ring(S, [stg[:, i, :] for i in range(6)])
                    load_w(sring, w_u, wB, w_in_d, l * 1024, 8, 1536, 1024, scale_col=V_GMIX)
                    load_w(sring, w_p, wB, w_in_d, l * 1024, 8, 2560, 512, scale_col=V_GMIX)
                    load_w(sring, pw, wB, pw_d, l * 512, 4, 0, 128)
                    for c in range(4):
                        for k in range(31):
                            if k % 2 == 0:
                                S.op("dve", lambda e, c=c, k=k: e.tensor_scalar(out=diag_t[:, c, k, :], in0=cst[:, 0:128], scalar1=vecs[:, V_DW + c * 31 + k: V_DW + c * 31 + k + 1], scalar2=None, op0=ALU.mult),
                                     reads=[vB, cstB], writes=[wB])
                            else:
                                S.op("act", lambda e, c=c, k=k: e.activation(out=diag_t[:, c, k, :], in_=cst[:, 0:128], func=AF.Copy, scale=vecs[:, V_DW + c * 31 + k: V_DW + c * 31 + k + 1]),
                                     reads=[vB, cstB], writes=[wB])
                    S.flush()
                xnT_t = sb(ph, "xnT", [128, 2, 8, 512], BF16)
                cb_t = sb(ph, "cb", [128, 2, 4, 542], BF16)
                sig_t = sb(ph, "sig", [128, 2, 512], F32)
                xc_t = sb(ph, "xc", [128, 4, 512], F32)
                xbf_t = sb(ph, "xbf", [128, 4, 512], BF16)
                xsq_t = sb(ph, "xsq", [128, 4, 512], BF16)
                ln_t = sb(ph, "ln", [128, 4, 512], F32)
                tt_t = sb(ph, "tt", [128, 2, 512], F32)
                hs_t = sb(ph, "hs", [128, 2, 4, 512], BF16)
                pb_t = sb(ph, "pb", [128, 2, 4, 527], F32)
                pa_t = sb(ph, "pa", [128, 2, 527], F32)
                yp_t = sb(ph, "yp", [128, 4, 512], BF16)
                zp_t = sb(ph, "zp", [128, 2, 4, 512], BF16)
                icnt = sb(ph, "icnt", [128, 4, 16], F32)
                t16 = sb(ph, "t16", [128, 16], F32)
                iB = S.buf()
                S.op("sp", lambda e: e.dma_start(out=icnt[:, :, :], in_=icnt_d.rearrange("p (g t) -> p g t", g=4)), writes=[iB], dma=True)
                br = bank_ring()
                xnT_r = Ring(S, [xnT_t[:, i, :, :] for i in range(2)])
                cb_r = Ring(S, [cb_t[:, i, :, :] for i in range(2)])
                sig_r = Ring(S, [sig_t[:, i, :] for i in range(2)])
                tt_r = Ring(S, [tt_t[:, i, :] for i in range(2)])
                hs_r = Ring(S, [hs_t[:, i, :, :] for i in range(2)])
                pb_r = Ring(S, [pb_t[:, i, :, :] for i in range(2)])
                pa_r = Ring(S, [pa_t[:, i, :] for i in range(2)])
                zp_r = Ring(S, [zp_t[:, i, :, :] for i in range(2)])
                lnB, t16B = S.buf(), S.buf()
                xcB = [S.buf() for _ in range(4)]
                xbfB = [S.buf() for _ in range(4)]
                xsqB = [S.buf() for _ in range(4)]
                ypB = [S.buf() for _ in range(4)]
                cb_prev = None
                pb_prev = None

                def ldxT(G):
                    xT, xTB = xnT_r.next()
                    S.op("sp", lambda e: e.dma_start(out=xT, in_=xnT_d[:, :, G * 512:(G + 1) * 512]), writes=[xTB], dma=True)
                    return xT, xTB

                nxt = ldxT(0)
                for G in range(NG):
                    xT, xTB = nxt
                    if G + 1 < NG:
                        nxt = ldxT(G + 1)
                    cb, _ = cb_r.next()
                    cbB = [S.buf() for _ in range(4)]
                    pb, _ = pb_r.next()
                    pbB = [S.buf() for _ in range(4)]
                    zp, zpB = zp_r.next()

                    def Pj(c, G=G, cb=cb, cbB=cbB, xT=xT, xTB=xTB, cb_prev=cb_prev):
                        if G == 0:
                            S.op("pool", lambda e: e.memset(cb[:, c, 0:30], 0.0), writes=[cbB[c]])
                        else:
                            pcb, pcbB = cb_prev
                            S.op("pool", lambda e: e.tensor_copy(out=cb[:, c, 0:30], in_=pcb[:, c, 512:542]), reads=[pcbB[c]], writes=[cbB[c]])
                        psa, psaB = br.next()
                        psg, psgB = br.next()
                        for kc in range(8):
                            S.op("pe", lambda e, kc=kc: e.matmul(psa, lhsT=w_u[:, kc, c * 128:(c + 1) * 128], rhs=xT[:, kc, :], start=(kc == 0), stop=(kc == 7)),
                                 reads=[wB, xTB], writes=[psaB])
                        for kc in range(8):
                            S.op("pe", lambda e, kc=kc: e.matmul(psg, lhsT=w_u[:, kc, 512 + c * 128:512 + (c + 1) * 128], rhs=xT[:, kc, :], start=(kc == 0), stop=(kc == 7)),
                                 reads=[wB, xTB], writes=[psgB])
                        sg, sgB = sig_r.next()
                        S.op("act", lambda e: e.activation(out=sg, in_=psg, func=AF.Sigmoid), reads=[psgB], writes=[sgB])
                        S.op("dve", lambda e: e.tensor_tensor(out=cb[:, c, 30:542], in0=psa, in1=sg, op=ALU.mult), reads=[psaB, sgB], writes=[cbB[c]])

                    def Cv(c, cb=cb, cbB=cbB):
                        psc, pscB = br.next()
                        for k in range(31):
                            S.op("pe", lambda e, k=k: e.matmul(psc, lhsT=diag_t[:, c, k, :], rhs=cb[:, c, k:k + 512], start=(k == 0), stop=(k == 30)),
                                 reads=[wB, cbB[c]], writes=[pscB])
                        bcol = vecs[:, V_DWB + c:V_DWB + c + 1]
                        S.op("act", lambda e: e.activation(out=xc_t[:, c, :], in_=psc, func=AF.Identity, bias=bcol), reads=[pscB, vB], writes=[xcB[c]])
                        S.op("act", lambda e: e.activation(out=xsq_t[:, c, :], in_=psc, func=AF.Square, bias=bcol), reads=[pscB, vB], writes=[xsqB[c]])
                        S.op("dve", lambda e: e.tensor_copy(out=xbf_t[:, c, :], in_=xc_t[:, c, :]), reads=[xcB[c]], writes=[xbfB[c]])

                    def Pp(g, G=G, pb=pb, pbB=pbB, xT=xT, xTB=xTB, pb_prev=pb_prev):
                        w = 2 << g
                        if G == 0:
                            S.op("pool", lambda e: e.memset(pb[:, g, 0:15], 0.0), writes=[pbB[g]])
                        else:
                            ppb, ppbB = pb_prev
                            S.op("pool", lambda e: e.tensor_copy(out=pb[:, g, 0:15], in_=ppb[:, g, 512:527]), reads=[ppbB[g]], writes=[pbB[g]])
                        psp, pspB = br.next()
                        for kc in range(8):
                            S.op("pe", lambda e, kc=kc: e.matmul(psp, lhsT=w_p[:, kc, g * 128:(g + 1) * 128], rhs=xT[:, kc, :], start=(kc == 0), stop=(kc == 7)),
                                 reads=[wB, xTB], writes=[pspB])
                        S.op("act", lambda e: e.copy(out=pb[:, g, 15:527], in_=psp), reads=[pspB], writes=[pbB[g]])
                        cur = pb[:, g, :]
                        curB = pbB[g]
                        m = 1
                        while m < w:
                            nx, nxB = pa_r.next()
                            S.op("dve", lambda e, nx=nx, cur=cur, m=m: e.tensor_tensor(out=nx[:, m:527], in0=cur[:, m:527], in1=cur[:, 0:527 - m], op=ALU.add),
                                 reads=[curB], writes=[nxB])
                            cur, curB = nx, nxB
                            m *= 2
                        S.op("dve", lambda e, cur=cur: e.scalar_tensor_tensor(out=yp_t[:, g, :], in0=cur[:, 15:527], scalar=1.0 / w, in1=pb[:, g, 15:527], op0=ALU.mult, op1=ALU.subtract),
                             reads=[curB, pbB[g]], writes=[ypB[g]])
                        if G == 0:
                            S.op("dve", lambda e, cur=cur: e.tensor_tensor(out=t16[:, :], in0=cur[:, 15:31], in1=icnt[:, g, :], op=ALU.mult), reads=[curB, iB], writes=[t16B])
                            S.op("dve", lambda e: e.tensor_tensor(out=yp_t[:, g, 0:16], in0=t16[:, :], in1=pb[:, g, 15:31], op=ALU.subtract), reads=[t16B, pbB[g]], writes=[ypB[g]])

                    def Pw(g, zp=zp, zpB=zpB):
                        psq, psqB = br.next()
                        S.op("pe", lambda e: e.matmul(psq, lhsT=pw[:, g, :], rhs=yp_t[:, g, :], start=True, stop=True), reads=[wB, ypB[g]], writes=[psqB])
                        S.op("act", lambda e: e.activation(out=zp[:, g, :], in_=psq, func=AF.Copy, scale=vecs[:, V_PS + g:V_PS + g + 1]), reads=[psqB, vB], writes=[zpB])

                    Pj(0)
                    Pj(1)
                    Cv(0)
                    Pp(0)
                    Pj(2)
                    Cv(1)
                    Pp(1)
                    Pj(3)
                    Cv(2)
                    Pp(2)
                    Cv(3)
                    Pp(3)
                    cb_prev = (cb, cbB)
                    s1, s1B = br.next()
                    s2, s2B = br.next()
                    for c in range(4):
                        S.op("pe", lambda e, s1=s1, c=c: e.matmul(s1, lhsT=ones, rhs=xbf_t[:, c, :], start=(c == 0), stop=(c == 3)), reads=[xbfB[c], cstB], writes=[s1B])
                    for c in range(4):
                        S.op("pe", lambda e, s2=s2, c=c: e.matmul(s2, lhsT=ones, rhs=xsq_t[:, c, :], start=(c == 0), stop=(c == 3)), reads=[xsqB[c], cstB], writes=[s2B])
                    for g in range(4):
                        Pw(g)
                    S.op("sp", lambda e, zp=zp, G=G: e.dma_start(out=zpT_d[:, :, G * 512:(G + 1) * 512], in_=zp), reads=[zpB], dma=True)
                    mean, msq, var = ln_t[:, 0, :], ln_t[:, 1, :], ln_t[:, 2, :]
                    S.op("act", lambda e, s1=s1: e.activation(out=mean, in_=s1, func=AF.Copy, scale=1.0 / 512), reads=[s1B], writes=[lnB])
                    S.op("dve", lambda e: e.tensor_tensor(out=msq, in0=mean, in1=mean, op=ALU.mult), reads=[lnB], writes=[lnB])
                    S.op("dve", lambda e, s2=s2: e.scalar_tensor_tensor(out=var, in0=s2, scalar=1.0 / 512, in1=msq, op0=ALU.mult, op1=ALU.subtract), reads=[s2B, lnB], writes=[lnB])
                    S.op("act", lambda e: e.activation(out=var, in_=var, func=AF.Ln, bias=epsT[:, 0:1]), reads=[lnB], writes=[lnB])
                    S.op("act", lambda e: e.activation(out=var, in_=var, func=AF.Exp, scale=-0.5), reads=[lnB], writes=[lnB])
                    hs, hsB = hs_r.next()
                    for c in range(4):
                        t1, t1B = tt_r.next()
                        S.op("dve", lambda e, t1=t1, c=c: e.tensor_tensor(out=t1, in0=xc_t[:, c, :], in1=mean, op=ALU.subtract), reads=[xcB[c], lnB], writes=[t1B])
                        S.op("pool", lambda e, t1=t1: e.tensor_tensor(out=t1, in0=t1, in1=var, op=ALU.mult), reads=[lnB, t1B], writes=[t1B])
                        S.op("act", lambda e, t1=t1, c=c, hs=hs: e.activation(out=hs[:, c, :], in_=t1, func=AF.Silu, scale=vecs[:, V_LNG + c:V_LNG + c + 1], bias=vecs[:, V_LNB + c:V_LNB + c + 1]),
                             reads=[t1B, vB], writes=[hsB])
                    S.op("sp", lambda e, hs=hs, G=G: e.dma_start(out=hsT_d[:, :, G * 512:(G + 1) * 512], in_=hs), reads=[hsB], dma=True)
                    pb_prev = (pb, pbB)
                S.flush()

            with contextlib.ExitStack() as ph:
                w_g = sb(ph, "w_g", [128, 8, 3072], BF16)
                w_br = sb(ph, "w_br", [128, 3, 4, 1024], BF16)
                w_o = sb(ph, "w_o", [128, 8, 1024], BF16)
                wB = S.buf()
                with contextlib.ExitStack() as st_es:
                    stg = sb(st_es, "stg", [128, 6, 1024], F32)
                    sring = Ring(S, [stg[:, i, :] for i in range(6)])
                    load_w(sring, w_g, wB, w_in_d, l * 1024, 8, 3072, 3072, scale_col=V_GMIX)
                    load_w(sring, w_br[:, 0, :, :], wB, w_co_d, l * 512, 4, 0, 1024)
                    load_w(sring, w_br[:, 1, :, :], wB, w_ao_d, l * 512, 4, 0, 1024)
                    load_w(sring, w_br[:, 2, :, :], wB, w_po_d, l * 512, 4, 0, 1024)
                    load_w(sring, w_o, wB, w_o_d, l * 1024, 8, 0, 1024)
                    S.flush()
                xnT_t = sb(ph, "xnT", [128, 2, 8, 512], BF16)
                src_t = sb(ph, "srcs", [128, 2, 3, 4, 512], BF16)
                hin_t = sb(ph, "hin", [128, 2, 4, 1024], F32)
                mg_t = sb(ph, "mg", [128, 8, 512], F32)
                mgb_t = sb(ph, "mgb", [128, 8, 512], BF16)
                gate_t = sb(ph, "gate", [128, 2, 512], F32)
                tm_t = sb(ph, "tm", [128, 2, 512], F32)
                ho_t = sb(ph, "ho", [128, 2, 1024], F32)
                br = bank_ring()
                xnT_r = Ring(S, [xnT_t[:, i, :, :] for i in range(2)])
                src_r = Ring(S, [src_t[:, i, :, :, :] for i in range(2)])
                gate_r = Ring(S, [gate_t[:, i, :] for i in range(2)])
                tm_r = Ring(S, [tm_t[:, i, :] for i in range(2)])
                ho_r = Ring(S, [ho_t[:, i, :] for i in range(2)])
                hin_r = Ring(S, [hin_t[:, i, :, :] for i in range(2)])
                mgB = [S.buf() for _ in range(8)]
                mgbB = S.buf()
                srcs_d = [hsT_d, oT_d, zpT_d]
                def ldall(G):
                    xT, xTB = xnT_r.next()
                    S.op("sp", lambda e: e.dma_start(out=xT, in_=xnT_d[:, :, G * 512:(G + 1) * 512]), writes=[xTB], dma=True)
                    sr, srB = src_r.next()
                    for b3 in range(3):
                        S.op("sp", lambda e, b3=b3: e.dma_start(out=sr[:, b3, :, :], in_=srcs_d[b3][:, :, G * 512:(G + 1) * 512]), writes=[srB], dma=True)
                    hin, hinB = hin_r.next()
                    for tt in range(4):
                        r0 = G * 512 + tt * 128
                        S.op("sp", lambda e, tt=tt, r0=r0: e.dma_start(out=hin[:, tt, :], in_=src_d[r0:r0 + 128, :]), writes=[hinB], dma=True)
                    return xT, xTB, sr, srB, hin, hinB

                nxt = ldall(0)
                for G in range(NG):
                    xT, xTB, sr, srB, hin, hinB = nxt
                    if G + 1 < NG:
                        nxt = ldall(G + 1)
                    for dc in range(8):
                        for b3 in range(3):
                            psg, psgB = br.next()
                            col = b3 * 1024 + dc * 128
                            for kc in range(8):
                                S.op("pe", lambda e, psg=psg, kc=kc, col=col, xT=xT: e.matmul(psg, lhsT=w_g[:, kc, col:col + 128], rhs=xT[:, kc, :], start=(kc == 0), stop=(kc == 7)),
                                     reads=[wB, xTB], writes=[psgB])
                            gt, gtB = gate_r.next()
                            gcol = vecs[:, V_GB + b3 * 8 + dc: V_GB + b3 * 8 + dc + 1]
                            S.op("act", lambda e, gt=gt, psg=psg, gcol=gcol: e.activation(out=gt, in_=psg, func=AF.Sigmoid, bias=gcol), reads=[psgB, vB], writes=[gtB])
                            psy, psyB = br.next()
                            for kc in range(4):
                                S.op("pe", lambda e, psy=psy, kc=kc, b3=b3, dc=dc, sr=sr: e.matmul(psy, lhsT=w_br[:, b3, kc, dc * 128:(dc + 1) * 128], rhs=sr[:, b3, kc, :], start=(kc == 0), stop=(kc == 3)),
                                     reads=[wB, srB], writes=[psyB])
                            if b3 == 0:
                                S.op("dve", lambda e, dc=dc, psy=psy, gt=gt: e.tensor_tensor(out=mg_t[:, dc, :], in0=psy, in1=gt, op=ALU.mult), reads=[psyB, gtB], writes=[mgB[dc]])
                            else:
                                tm, tmB = tm_r.next()
                                S.op("dve", lambda e, tm=tm, psy=psy, gt=gt: e.tensor_tensor(out=tm, in0=psy, in1=gt, op=ALU.mult), reads=[psyB, gtB], writes=[tmB])
                                if b3 == 1:
                                    S.op("pool", lambda e, dc=dc, tm=tm: e.tensor_tensor(out=mg_t[:, dc, :], in0=mg_t[:, dc, :], in1=tm, op=ALU.add), reads=[tmB], writes=[mgB[dc]])
                                else:
                                    S.op("pool", lambda e, dc=dc, tm=tm: e.tensor_tensor(out=mgb_t[:, dc, :], in0=mg_t[:, dc, :], in1=tm, op=ALU.add), reads=[tmB, mgB[dc]], writes=[mgbB])
                    for tt in range(4):
                        ho, hoB = ho_r.next()
                        for dh in range(2):
                            pso, psoB = br.next()
                            for fc in range(8):
                                S.op("pe", lambda e, pso=pso, fc=fc, tt=tt, dh=dh: e.matmul(pso, lhsT=mgb_t[:, fc, tt * 128:(tt + 1) * 128], rhs=w_o[:, fc, dh * 512:(dh + 1) * 512], start=(fc == 0), stop=(fc == 7)),
                                     reads=[wB, mgbB], writes=[psoB])
                            S.op("dve", lambda e, ho=ho, pso=pso, tt=tt, dh=dh, hin=hin: e.tensor_tensor(out=ho[:, dh * 512:(dh + 1) * 512], in0=pso, in1=hin[:, tt, dh * 512:(dh + 1) * 512], op=ALU.add),
                                 reads=[psoB, hinB], writes=[hoB])
                        r0 = G * 512 + tt * 128
                        S.op("sp", lambda e, ho=ho, r0=r0: e.dma_start(out=hA_d[r0:r0 + 128, :], in_=ho), reads=[hoB], dma=True)
                S.flush()

            with contextlib.ExitStack() as ph:
                w1 = sb(ph, "w1", [128, 8, 4096], BF16)
                w2 = sb(ph, "w2", [128, 32, 1024], BF16)
                wB = S.buf()
                with contextlib.ExitStack() as st_es:
                    stg = sb(st_es, "stg", [128, 6, 1024], F32)
                    sring = Ring(S, [stg[:, i, :] for i in range(6)])
                    load_w(sring, w1, wB, w_m1_d, l * 1024, 8, 0, 4096, scale_col=V_GMLP)
                    load_w(sring, w2, wB, w_m2_d, l * 4096, 32, 0, 1024)
                    S.flush()
                HT = 256
                xin_t = sb(ph, "xin", [128, 4, 1024], F32)
                xn_t = sb(ph, "xn", [128, 2, 1024], F32)
                stat_t = sb(ph, "stat", [128, 2, 4], F32)
                xnT_t = sb(ph, "xnT", [128, 2, 8, HT], BF16)
                ff_t = sb(ph, "ff", [128, 2, 32, HT], BF16)
                rl_t = sb(ph, "rl", [128, 2, HT], F32)
                ho_t = sb(ph, "ho", [128, 2, 1024], F32)
                xin_r = Ring(S, [xin_t[:, i, :] for i in range(4)])
                xn_r = Ring(S, [xn_t[:, i, :] for i in range(2)])
                stat_r = Ring(S, [stat_t[:, i, :] for i in range(2)])
                xnT_r = Ring(S, [xnT_t[:, i, :, :] for i in range(2)])
                rl_r = Ring(S, [rl_t[:, i, :] for i in range(2)])
                ho_r = Ring(S, [ho_t[:, i, :] for i in range(2)])
                junkB = S.buf()
                ff_r = Ring(S, [ff_t[:, i, :, :] for i in range(2)])
                for it in ff_r.items:
                    pass
                ff_r.items = [(ap, [S.buf() for _ in range(32)]) for (ap, _) in ff_r.items]
                aps = []
                for p in psum[0:3]:
                    aps.append(p[:, 0:512])
                    aps.append(p[:, 512:1024])
                br = Ring(S, aps)
                ppB = S.buf()
                def ldh(H):
                    res = []
                    for tt in range(HT // 128):
                        r0 = H * HT + tt * 128
                        xin, xinB = xin_r.next()
                        S.op("sp", lambda e, xin=xin, r0=r0: e.dma_start(out=xin, in_=hA_d[r0:r0 + 128, :]), writes=[xinB], dma=True)
                        res.append((xin, xinB))
                    return res

                NH = T // HT

                def prep_norm(H, curx):
                    xT, xTB = xnT_r.next()
                    tiles = []
                    for tt in range(HT // 128):
                        xin, xinB = curx[tt]
                        xn, xnB = xn_r.next()
                        stt, sttB = stat_r.next()
                        rms_tile(None, xin, xinB, xn, xnB, stt, sttB, xn, xnB)
                        tiles.append((xin, xinB, xn, xnB))
                    return xT, xTB, tiles

                def prep_T(pr):
                    xT, xTB, tiles = pr
                    for tt, (xin, xinB, xn, xnB) in enumerate(tiles):
                        transpose_tile(xn, xnB, psum[3], ppB, xT, xTB, tt * 128, ("act", "dve"))

                ld_cur = ldh(0)
                pr_cur = prep_norm(0, ld_cur)
                prep_T(pr_cur)
                ld_nxt = ldh(1) if NH > 1 else None
                for H in range(NH):
                    xT, xTB, tiles = pr_cur
                    ff, ffB = ff_r.next()
                    if H + 1 < NH:
                        pr_nxt = prep_norm(H + 1, ld_nxt)
                    for fc in range(32):
                        ps, psB = br.next()
                        for kc in range(8):
                            S.op("pe", lambda e, ps=ps, kc=kc, fc=fc, xT=xT: e.matmul(ps[:, 0:HT], lhsT=w1[:, kc, fc * 128:(fc + 1) * 128], rhs=xT[:, kc, :], start=(kc == 0), stop=(kc == 7)),
                                 reads=[wB, xTB], writes=[psB])
                        rl, rlB = rl_r.next()
                        S.op("act", lambda e, rl=rl, ps=ps: e.activation(out=rl, in_=ps[:, 0:HT], func=AF.Relu), reads=[psB], writes=[rlB])
                        eng = "pool" if fc % 2 == 0 else "dve"
                        S.op(eng, lambda e, rl=rl, fc=fc, ff=ff: e.tensor_tensor(out=ff[:, fc, :], in0=rl, in1=rl, op=ALU.mult), reads=[rlB], writes=[ffB[fc]])
                    if H + 1 < NH:
                        prep_T(pr_nxt)
                    for tt in range(HT // 128):
                        ho, hoB = ho_r.next()
                        xin, xinB = tiles[tt][0], tiles[tt][1]
                        for dh in range(2):
                            pso, psoB = br.next()
                            for fc in range(32):
                                S.op("pe", lambda e, pso=pso, fc=fc, tt=tt, dh=dh, ff=ff: e.matmul(pso, lhsT=ff[:, fc, tt * 128:(tt + 1) * 128], rhs=w2[:, fc, dh * 512:(dh + 1) * 512], start=(fc == 0), stop=(fc == 31)),
                                     reads=[wB, ffB[fc]], writes=[psoB])
                            S.op("dve", lambda e, ho=ho, pso=pso, xin=xin, dh=dh: e.tensor_tensor(out=ho[:, dh * 512:(dh + 1) * 512], in0=pso, in1=xin[:, dh * 512:(dh + 1) * 512], op=ALU.add),
                                 reads=[psoB, xinB], writes=[hoB])
                        r0 = H * HT + tt * 128
                        S.op("sp", lambda e, ho=ho, r0=r0: e.dma_start(out=out_d[r0:r0 + 128, :], in_=ho), reads=[hoB], dma=True)
                    if H + 1 < NH:
                        pr_cur = pr_nxt
                        ld_nxt = ldh(H + 2) if H + 2 < NH else None
                S.flush()
    return nc


def host_prep(inp, L):
    f = lambda a: np.ascontiguousarray(np.asarray(a, dtype=np.float32))
    vecs = np.zeros((L, 128, NVEC), np.float32)
    for l in range(L):
        vecs[l, :, V_GB:V_GB + 24] = f(inp["gate_b"])[l].reshape(24, 128).T
        vecs[l, :, V_DWB:V_DWB + 4] = f(inp["conv_dw_b"])[l].reshape(4, 128).T
        vecs[l, :, V_LNG:V_LNG + 4] = f(inp["conv_ln_g"])[l].reshape(4, 128).T
        vecs[l, :, V_LNB:V_LNB + 4] = f(inp["conv_ln_b"])[l].reshape(4, 128).T
        vecs[l, :, V_PS:V_PS + 4] = f(inp["pool_scale"])[l].reshape(4, 128).T
        vecs[l, :, V_GQ] = np.tile(f(inp["q_norm_g"])[l], 2)
        vecs[l, :, V_GK] = np.tile(f(inp["k_norm_g"])[l], 2)
        vecs[l, :, V_GMIX:V_GMIX + 8] = f(inp["mix_norm_g"])[l].reshape(8, 128).T
        vecs[l, :, V_GMLP:V_GMLP + 8] = f(inp["mlp_norm_g"])[l].reshape(8, 128).T
        dw = f(inp["conv_dw"])[l]
        vecs[l, :, V_DW:V_DW + 124] = dw.reshape(31, 4, 128).transpose(2, 1, 0).reshape(128, 124)
    ident = np.eye(128, dtype=np.float32)
    jj, kk = np.meshgrid(np.arange(128), np.arange(128), indexing="ij")
    negtri = -(jj >= kk).astype(np.float32)
    negones = -np.ones((128, 128), np.float32)
    ones = np.ones((128, 128), np.float32)
    blk = (jj // 64 == kk // 64).astype(np.float32)
    cst = np.concatenate([ident, negtri, negones, ones, blk], axis=1).astype(ml_dtypes.bfloat16)
    p = np.arange(128)[:, None]
    q = np.arange(512)[None, :]
    masks = np.concatenate([(q > p + j * 128).astype(np.float32) for j in range(4)], axis=1).astype(ml_dtypes.bfloat16)
    icnt = np.zeros((128, 4, 16), np.float32)
    for g in range(4):
        w = 2 << g
        icnt[:, g, :] = 1.0 / np.minimum(np.arange(16) + 1, w).astype(np.float32)
    com = {
        "w_in": f(inp["w_in"]).reshape(L * 1024, 6144),
        "w_conv_out": f(inp["w_conv_out"]).reshape(L * 512, 1024),
        "w_att_out": f(inp["w_att_out"]).reshape(L * 512, 1024),
        "w_pool_out": f(inp["w_pool_out"]).reshape(L * 512, 1024),
        "pool_w": f(inp["pool_w"]).reshape(L * 512, 128),
        "w_o": f(inp["w_o"]).reshape(L * 1024, 1024),
        "w_mlp_in": f(inp["w_mlp_in"]).reshape(L * 1024, 4096),
        "w_mlp_out": f(inp["w_mlp_out"]).reshape(L * 4096, 1024),
        "vecs": vecs.reshape(L * 128, NVEC),
        "cst": cst,
        "identf": ident,
        "masks": masks,
        "icnt": icnt.reshape(128, 64),
    }
    return com


_NC_CACHE = {}


def run(inp, seqs, T, L):
    key = (T, L)
    if key not in _NC_CACHE:
        _NC_CACHE[key] = build(T, L)
    nc = _NC_CACHE[key]
    com = host_prep(inp, L)
    in_maps = []
    for c in range(8):
        m = dict(com)
        m["x"] = np.ascontiguousarray(seqs[c], dtype=np.float32)
        in_maps.append(m)
    res = run_bass_kernel_spmd(nc, in_maps, core_ids=list(range(8)))
    return [res.results[c]["y"] for c in range(8)]


def kernel(**inputs):
    x = np.asarray(inputs["x"], dtype=np.float32)
    B, T, D = x.shape
    L = np.asarray(inputs["w_in"]).shape[0]
    seqs = [x[c // 2] for c in range(8)]
    outs = run(inputs, seqs, T, L)
    return np.stack([outs[2 * b] for b in range(B)], axis=0).astype(np.float32)
```
